# Optimizing a Trainium2 kernel written in Bass

```python
import jax, jax.numpy as jnp
from jax import lax
import numpy as np

D_MODEL = 1024
BATCH = 2
SEQ = 8192
DEPTH = 1

CONV_WIDTH = 512
CONV_KERNEL = 31
N_HEADS = 8
QK_NOPE_DIM = 64
QK_ROPE_DIM = 32
V_HEAD_DIM = 64
Q_LORA_RANK = 384
KV_LORA_RANK = 256
ROPE_THETA = 10000.0
Q_BLOCK = 128
QK_HEAD_DIM = QK_NOPE_DIM + QK_ROPE_DIM
MLA_WIDTH = N_HEADS * V_HEAD_DIM
D_FF = 2816
FFN_KERNEL = 3
N_BRANCHES = 2
NORM_EPS = 1e-6
IN_SPLITS = (2 * CONV_WIDTH, Q_LORA_RANK, KV_LORA_RANK, QK_ROPE_DIM, N_BRANCHES * D_MODEL)
D_IN = 2 * CONV_WIDTH + Q_LORA_RANK + KV_LORA_RANK + QK_ROPE_DIM + N_BRANCHES * D_MODEL

kernel_name = "hybrid_conformer_mla_gated_encoder"


def rms_norm(x, g):
    xf = x.astype(jnp.float32)
    y = xf * lax.rsqrt(jnp.mean(xf * xf, axis=-1, keepdims=True) + NORM_EPS)
    return (y * g.astype(jnp.float32)).astype(x.dtype)


def layer_norm(x, g, b):
    xf = x.astype(jnp.float32)
    mu = jnp.mean(xf, axis=-1, keepdims=True)
    xc = xf - mu
    y = xc * lax.rsqrt(jnp.mean(xc * xc, axis=-1, keepdims=True) + NORM_EPS)
    return (y * g.astype(jnp.float32) + b.astype(jnp.float32)).astype(x.dtype)


def depthwise_conv(x, w, b):
    k, c = w.shape
    pad = k // 2
    y = lax.conv_general_dilated(
        x, w[:, None, :].astype(x.dtype), window_strides=(1,), padding=[(pad, pad)],
        dimension_numbers=("NWC", "WIO", "NWC"), feature_group_count=c)
    return y + b


def split_cols(t, sizes):
    out, off = [], 0
    for s in sizes:
        out.append(t[..., off:off + s])
        off += s
    return out


def rope_tables(positions):
    inv_freq = 1.0 / (ROPE_THETA ** (jnp.arange(0, QK_ROPE_DIM, 2, dtype=jnp.float32) / QK_ROPE_DIM))
    ang = positions.astype(jnp.float32)[..., None] * inv_freq
    return jnp.cos(ang), jnp.sin(ang)


def apply_rope(t, cos, sin):
    tf = t.astype(jnp.float32)
    t1, t2 = tf[..., : QK_ROPE_DIM // 2], tf[..., QK_ROPE_DIM // 2:]
    return jnp.concatenate([t1 * cos - t2 * sin, t2 * cos + t1 * sin], axis=-1).astype(t.dtype)


def conformer_conv(a, dw_w, dw_b, ln_g, ln_b, w_out, b_out):
    val, gate = a[..., :CONV_WIDTH], a[..., CONV_WIDTH:]
    u = val * jax.nn.sigmoid(gate)
    u = depthwise_conv(u, dw_w, dw_b)
    u = jax.nn.silu(layer_norm(u, ln_g, ln_b))
    return u @ w_out + b_out


def mla(q_c, kv_c, k_rope_in, positions, q_norm_g, w_uq, kv_norm_g, w_ukv, w_mla_out):
    b, s, _ = q_c.shape
    q = (rms_norm(q_c, q_norm_g) @ w_uq).reshape(b, s, N_HEADS, QK_HEAD_DIM)
    q_nope, q_rope = q[..., :QK_NOPE_DIM], q[..., QK_NOPE_DIM:]
    kv = (rms_norm(kv_c, kv_norm_g) @ w_ukv).reshape(b, s, N_HEADS, QK_NOPE_DIM + V_HEAD_DIM)
    k_nope, v = kv[..., :QK_NOPE_DIM], kv[..., QK_NOPE_DIM:]
    cos, sin = rope_tables(positions)
    q_rope = apply_rope(q_rope, cos[:, :, None, :], sin[:, :, None, :])
    k_rope = apply_rope(k_rope_in, cos, sin)
    scale = QK_HEAD_DIM ** -0.5
    n_blk = s // Q_BLOCK

    def to_blocks(t):
        return jnp.moveaxis(t.reshape(b, n_blk, Q_BLOCK, *t.shape[2:]), 1, 0)

    def attend(blk):
        qn, qr = blk
        sc = (jnp.einsum('bqhd,bkhd->bhqk', qn, k_nope)
              + jnp.einsum('bqhr,bkr->bhqk', qr, k_rope)).astype(jnp.float32) * scale
        p = jax.nn.softmax(sc, axis=-1).astype(v.dtype)
        return jnp.einsum('bhqk,bkhd->bqhd', p, v)

    o = lax.map(attend, (to_blocks(q_nope), to_blocks(q_rope)))
    o = jnp.moveaxis(o, 0, 1).reshape(b, s, MLA_WIDTH)
    return o @ w_mla_out


def conv_ffn(h, w_up, dw_w, dw_b, w_down):
    gu = depthwise_conv(h @ w_up, dw_w, dw_b)
    gate, up = gu[..., :D_FF], gu[..., D_FF:]
    return (jax.nn.silu(gate) * up) @ w_down


def setup_inputs(seed: int = 0) -> dict:
    key = jax.random.key(seed)
    ks = iter(jax.random.split(key, 32))

    def w(shape, fan_in):
        return jax.random.normal(next(ks), shape, jnp.float32) * (fan_in ** -0.5)

    def gain(shape):
        return 1.0 + 0.02 * jax.random.normal(next(ks), shape, jnp.float32)

    def bias(shape):
        return 0.02 * jax.random.normal(next(ks), shape, jnp.float32)

    L = DEPTH
    x = jax.random.normal(next(ks), (BATCH, SEQ, D_MODEL), jnp.float32)
    offset = jax.random.randint(next(ks), (BATCH, 1), 0, SEQ, dtype=jnp.int32)
    positions = offset + jnp.arange(SEQ, dtype=jnp.int32)[None, :]
    return {
        "x": x,
        "positions": positions,
        "norm1_g": gain((L, D_MODEL)),
        "w_in": w((L, D_MODEL, D_IN), D_MODEL),
        "conv_dw_w": w((L, CONV_KERNEL, CONV_WIDTH), CONV_KERNEL),
        "conv_dw_b": bias((L, CONV_WIDTH)),
        "conv_ln_g": gain((L, CONV_WIDTH)),
        "conv_ln_b": bias((L, CONV_WIDTH)),
        "w_conv_out": w((L, CONV_WIDTH, D_MODEL), CONV_WIDTH),
        "b_conv_out": bias((L, D_MODEL)),
        "q_norm_g": gain((L, Q_LORA_RANK)),
        "w_uq": w((L, Q_LORA_RANK, N_HEADS * QK_HEAD_DIM), Q_LORA_RANK),
        "kv_norm_g": gain((L, KV_LORA_RANK)),
        "w_ukv": w((L, KV_LORA_RANK, N_HEADS * (QK_NOPE_DIM + V_HEAD_DIM)), KV_LORA_RANK),
        "w_mla_out": w((L, MLA_WIDTH, D_MODEL), MLA_WIDTH),
        "w_out": w((L, D_MODEL, D_MODEL), D_MODEL),
        "norm2_g": gain((L, D_MODEL)),
        "w_ffn_up": w((L, D_MODEL, 2 * D_FF), D_MODEL),
        "ffn_dw_w": w((L, FFN_KERNEL, 2 * D_FF), FFN_KERNEL),
        "ffn_dw_b": bias((L, 2 * D_FF)),
        "w_ffn_down": w((L, D_FF, D_MODEL), D_FF),
        "norm_f_g": gain((D_MODEL,)),
    }


def reference(x, positions, norm1_g, w_in, conv_dw_w, conv_dw_b, conv_ln_g, conv_ln_b,
              w_conv_out, b_conv_out, q_norm_g, w_uq, kv_norm_g, w_ukv, w_mla_out, w_out,
              norm2_g, w_ffn_up, ffn_dw_w, ffn_dw_b, w_ffn_down, norm_f_g):
    for l in range(DEPTH):
        h = rms_norm(x, norm1_g[l])
        proj = h @ w_in[l]
        a_in, q_c, kv_c, k_rope_in, gates = split_cols(proj, IN_SPLITS)
        y_conv = conformer_conv(a_in, conv_dw_w[l], conv_dw_b[l], conv_ln_g[l], conv_ln_b[l],
                                w_conv_out[l], b_conv_out[l])
        y_mla = mla(q_c, kv_c, k_rope_in, positions, q_norm_g[l], w_uq[l],
                    kv_norm_g[l], w_ukv[l], w_mla_out[l])
        g = jax.nn.sigmoid(gates.astype(jnp.float32)).astype(x.dtype)
        merged = g[..., :D_MODEL] * y_conv + g[..., D_MODEL:] * y_mla
        x = x + merged @ w_out[l]
        h2 = rms_norm(x, norm2_g[l])
        x = x + conv_ffn(h2, w_ffn_up[l], ffn_dw_w[l], ffn_dw_b[l], w_ffn_down[l])
    return rms_norm(x, norm_f_g)
```

```python
import contextlib
import math
import numpy as np
import concourse.bass as bass
import concourse.mybir as mybir
from concourse.bass_utils import run_bass_kernel_spmd

F32 = mybir.dt.float32
BF16 = mybir.dt.bfloat16
I32 = mybir.dt.int32
AF = mybir.ActivationFunctionType
ALU = mybir.AluOpType
AX = mybir.AxisListType

D = 1024
SEQ = 8192
NB = 2
OWN = 2048
E = 2050
XO = 2080
T = 410
NT = 5
TH = T + 30
CW = 512
CK = 31
NH = 8
QLR = 384
KVLR = 256
DFF = 2816
NPAIR = 22
EPS = 1e-6
SCALE = 96.0 ** -0.5
TWO_PI = 2.0 * math.pi
CW1 = 6.28125
CW2 = TWO_PI - CW1

ENGS = ("pe", "act", "dve", "pool", "sp")
N_DMA_SEMS = 8


class Op:
    __slots__ = ("eng", "fn", "deps", "is_dma", "signal", "count_after", "dma_sem", "dma_val", "dma_prev", "fuse_eng")

    def __init__(self, eng, fn, is_dma):
        self.eng = eng
        self.fn = fn
        self.is_dma = is_dma
        self.deps = []
        self.signal = False
        self.count_after = 0
        self.dma_sem = None
        self.dma_val = 0
        self.dma_prev = None
        self.fuse_eng = None


class Sched:
    def __init__(self, nc):
        self.nc = nc
        self.ops = {e: [] for e in ENGS}
        self.last_writer = {}
        self.readers = {}
        self.dma_ops = {e: [] for e in ENGS}
        self.pending_barrier = {e: None for e in ENGS}

    def add(self, eng, fn, reads=(), writes=(), dma=False, fuse_eng=None):
        op = Op(eng, fn, dma)
        op.fuse_eng = fuse_eng
        deps = []
        for r in reads:
            w = self.last_writer.get(r)
            if w is not None:
                deps.append((w, "raw"))
        for w_ in writes:
            w = self.last_writer.get(w_)
            if w is not None:
                deps.append((w, "waw"))
            for rd in self.readers.get(w_, ()):
                deps.append((rd, "war"))
        pb = self.pending_barrier[eng]
        if pb is not None:
            deps.extend((d, "raw") for d in pb)
            self.pending_barrier[eng] = None
        seen = set()
        for d, kind in deps:
            if d is op or id(d) in seen:
                continue
            if (not d.is_dma) and (not dma) and d.eng == eng:
                if eng == "pe" or kind != "raw":
                    continue
            seen.add(id(d))
            op.deps.append(d)
        for r in reads:
            self.readers.setdefault(r, []).append(op)
        for w_ in writes:
            self.last_writer[w_] = op
            self.readers[w_] = []
        if dma:
            lst = self.dma_ops[eng]
            k = len(lst)
            op.dma_sem = k % N_DMA_SEMS
            op.dma_val = 16 * (k // N_DMA_SEMS + 1)
            if k >= N_DMA_SEMS:
                op.dma_prev = lst[k - N_DMA_SEMS]
            lst.append(op)
        self.ops[eng].append(op)
        return op

    def barrier(self):
        deps = []
        for e in ENGS:
            comp = [o for o in self.ops[e] if not o.is_dma]
            if comp:
                deps.append(comp[-1])
            deps.extend(self.dma_ops[e][-N_DMA_SEMS:])
        for e in ENGS:
            cur = self.pending_barrier[e]
            self.pending_barrier[e] = deps if cur is None else (cur + deps)
        self.last_writer = {}
        self.readers = {}

    def emit(self, out_dma_ops=()):
        nc = self.nc
        for e in ENGS:
            for op in self.ops[e]:
                for d in op.deps:
                    if not d.is_dma:
                        d.signal = True
        for e in ENGS:
            c = 0
            for op in self.ops[e]:
                if (not op.is_dma) and op.signal:
                    c += 1
                op.count_after = c
        with contextlib.ExitStack() as st:
            csem = {e: st.enter_context(nc.semaphore("c_" + e)) for e in ("pe", "act", "dve", "pool")}
            dsem = {}
            for e in ENGS:
                if self.dma_ops[e]:
                    dsem[e] = [st.enter_context(nc.semaphore("d_%s%d" % (e, i))) for i in range(N_DMA_SEMS)]
            block = st.enter_context(nc.Block())

            def run(e, eng):
                waited = {}

                def wait(sem, val):
                    key = id(sem)
                    if waited.get(key, 0) >= val:
                        return
                    waited[key] = val
                    eng.wait_ge(sem, val)

                for op in self.ops[e]:
                    need = {}
                    for d in op.deps:
                        if d.is_dma:
                            k = ("d", d.eng, d.dma_sem)
                            v = d.dma_val
                        else:
                            k = ("c", d.eng)
                            v = d.count_after
                        if v > need.get(k, 0):
                            need[k] = v
                    if op.is_dma and op.dma_prev is not None:
                        k = ("d", e, op.dma_sem)
                        if op.dma_prev.dma_val > need.get(k, 0):
                            need[k] = op.dma_prev.dma_val
                    fused = None
                    for k, v in need.items():
                        if k[0] == "d":
                            wait(dsem[k[1]][k[2]], v)
                        elif op.fuse_eng is not None and k[1] == op.fuse_eng:
                            fused = (csem[k[1]], v)
                        else:
                            wait(csem[k[1]], v)
                    ins = op.fn(eng)
                    if fused is not None and waited.get(id(fused[0]), 0) < fused[1]:
                        waited[id(fused[0])] = fused[1]
                        ins._wait_ge(fused[0], fused[1])
                    if op.is_dma:
                        ins.then_inc(dsem[e][op.dma_sem], 16)
                    elif op.signal:
                        ins.then_inc(csem[e], 1)
                if e == "sp":
                    for d in out_dma_ops:
                        wait(dsem[d.eng][d.dma_sem], d.dma_val)

            @block.tensor
            def _(eng):
                run("pe", eng)

            @block.scalar
            def _(eng):
                run("act", eng)

            @block.vector
            def _(eng):
                run("dve", eng)

            @block.gpsimd
            def _(eng):
                run("pool", eng)

            @block.sync
            def _(eng):
                run("sp", eng)


class Arena:
    def __init__(self, ap, nelem):
        self.ap = ap
        self.nbytes = nelem * 2
        self.top = 0
        self.hi = self.nbytes
        self.peak = 0

    def alloc(self, shape, dtype):
        n = 1
        for s in shape:
            n *= s
        size = {BF16: 2, F32: 4, I32: 4}[dtype]
        nb = (n * size + 31) // 32 * 32
        off = self.top
        self.top += nb
        self.peak = max(self.peak, self.top)
        self.peak = max(self.peak, self.top + (self.nbytes - self.hi))
        assert self.top <= self.hi, ("SBUF arena overflow", self.top, self.hi)
        v = self.ap[:, off // 2:(off + n * size) // 2]
        if dtype != BF16:
            v = v.bitcast(dtype)
        if len(shape) == 2:
            v = v.rearrange("p (a b) -> p a b", a=shape[0])
        elif len(shape) == 3:
            v = v.rearrange("p (a b c) -> p a b c", a=shape[0], b=shape[1])
        return v

    def alloc_top(self, shape, dtype):
        n = 1
        for x in shape:
            n *= x
        size = {BF16: 2, F32: 4, I32: 4}[dtype]
        nb = (n * size + 31) // 32 * 32
        self.hi -= nb
        off = self.hi
        assert self.top <= self.hi, ("SBUF arena overflow (top)", self.top, self.hi)
        self.peak = max(self.peak, self.top + (self.nbytes - self.hi))
        v = self.ap[:, off // 2:(off + n * size) // 2]
        if dtype != BF16:
            v = v.bitcast(dtype)
        if len(shape) == 2:
            v = v.rearrange("p (a b) -> p a b", a=shape[0])
        return v

    def release_top(self):
        self.hi = self.nbytes

    def mark(self):
        return self.top

    def release(self, m):
        self.top = m


VOFF = {}
_o = 0
for _name, _n in (("g1", 8), ("gq", 3), ("gkv", 2), ("dwb", 4), ("lng", 4), ("lnb", 4), ("lngh", 4), ("lnbh", 4),
                  ("bco", 8), ("g2", 8), ("gf", 8), ("fw", 132), ("fb", 44), ("cw", 124), ("invf", 1), ("sgn", 1)):
    VOFF[_name] = _o
    _o += _n
NV = _o


def build_nc(debug=None):
    nc = bass.Bass("TRN2", target_bir_lowering=False)
    dt_in = lambda name, shape, dt=F32: nc.dram_tensor(name, list(shape), dt, kind="ExternalInput").ap()
    d_xf = dt_in("xf", (SEQ // 512, 128, 8 * 512))
    d_xo = dt_in("xo", (NT, 128, 8 * T))
    d_xh = dt_in("xh", (NT, 128, 8 * TH))
    d_posf = dt_in("posf", (1, SEQ), I32)
    d_poso = dt_in("poso", (1, E), I32)
    d_mask = dt_in("mask", (1, E))
    d_vec = dt_in("vec", (128, NV))
    d_ident = dt_in("ident", (128, 128))
    d_wkvr = dt_in("wkvr", (D, 322))
    d_wq = dt_in("wq", (3, 128, 8 * 128))
    d_wa = dt_in("wa", (D, 1024))
    d_wg = dt_in("wg", (D, 2048))
    d_wqh = dt_in("wqh", (QLR, NH * 194))
    d_wk = dt_in("wk", (KVLR, NH * 128))
    d_wv = dt_in("wv", (KVLR, NH * 64))
    d_wco = dt_in("wco", (CW, D))
    d_wmo = dt_in("wmo", (NH * 64, D))
    d_wout = dt_in("wout", (D, D))
    d_wup = dt_in("wup", (NPAIR, D, 256))
    d_wdn = dt_in("wdn", (DFF, D))
    d_out = nc.dram_tensor("out", [D, OWN], F32, kind="ExternalOutput").ap()
    s_wa = nc.dram_tensor("s_wa", [128, 8 * 1024], BF16).ap()
    s_wg = nc.dram_tensor("s_wg", [128, 8 * 2048], BF16).ap()
    s_wco = nc.dram_tensor("s_wco", [128, 4 * D], BF16).ap()
    s_wmo = nc.dram_tensor("s_wmo", [64, NH * D], BF16).ap()
    s_wout = nc.dram_tensor("s_wout", [128, 8 * D], BF16).ap()
    s_wup = nc.dram_tensor("s_wup", [NPAIR, 128, 8 * 256], BF16).ap()
    s_wdn = nc.dram_tensor("s_wdn", [128, NPAIR * D], BF16).ap()
    dbg_outs = {}

    with contextlib.ExitStack() as st:
        ARENA_ELEMS = 106000
        arena_t = st.enter_context(nc.sbuf_tensor("arena", [128, ARENA_ELEMS], BF16))
        psum = st.enter_context(nc.psum_tensor("psum", [128, 8, 512], F32))
        A = Arena(arena_t, ARENA_ELEMS)
        S = Sched(nc)
        out_dmas = []

        class PS:
            avail = list(range(8))
            pos = 0

        def psb():
            b = PS.avail[PS.pos % len(PS.avail)]
            PS.pos += 1
            return b

        def ptok(b):
            return ("ps", b)

        def dma(eng, out, in_, reads=(), writes=()):
            return S.add(eng, lambda e: e.dma_start(out=out, in_=in_), reads=reads, writes=writes, dma=True)

        def mm(out, lhsT, rhs, start, stop, reads, writes, fuse_eng=None):
            S.add("pe", lambda e: e.matmul(out, lhsT=lhsT, rhs=rhs, start=start, stop=stop), reads=reads, writes=writes,
                  fuse_eng=fuse_eng)

        def act(out, in_, func, reads, writes, bias=None, scale=1.0):
            if bias is None:
                S.add("act", lambda e: e.activation(out=out, in_=in_, func=func, scale=scale), reads=reads, writes=writes)
            else:
                S.add("act", lambda e: e.activation(out=out, in_=in_, func=func, bias=bias, scale=scale), reads=reads, writes=writes)

        def tt(eng, out, in0, in1, op, reads, writes):
            S.add(eng, lambda e: e.tensor_tensor(out=out, in0=in0, in1=in1, op=op), reads=reads, writes=writes)

        def ts(eng, out, in0, s1, s2, op0, op1, reads, writes):
            if op1 is None:
                S.add(eng, lambda e: e.tensor_scalar(out=out, in0=in0, scalar1=s1, scalar2=None, op0=op0), reads=reads, writes=writes)
            else:
                S.add(eng, lambda e: e.tensor_scalar(out=out, in0=in0, scalar1=s1, scalar2=s2, op0=op0, op1=op1), reads=reads, writes=writes)

        def stt(eng, out, in0, scalar, in1, op0, op1, reads, writes):
            S.add(eng, lambda e: e.scalar_tensor_tensor(out=out, in0=in0, scalar=scalar, in1=in1, op0=op0, op1=op1),
                  reads=reads, writes=writes)

        def cp(eng, out, in_, reads, writes):
            S.add(eng, lambda e: e.tensor_copy(out=out, in_=in_), reads=reads, writes=writes)

        def recip(out, in_, reads, writes):
            S.add("dve", lambda e: e.reciprocal(out=out, in_=in_), reads=reads, writes=writes)

        def memset(eng, ap, val, writes):
            S.add(eng, lambda e: e.memset(ap, val), writes=writes)

        def dump(name, ap, reads):
            if debug is None or name not in debug:
                return
            shape = list(ap.shape)
            dten = nc.dram_tensor("dbg_" + name, shape, ap.dtype, kind="ExternalOutput").ap()
            dbg_outs[name] = dten
            out_dmas.append(dma("sp", dten, ap, reads=reads))

        vec = A.alloc((NV,), F32)
        ident = A.alloc((128,), F32)
        onesb = A.alloc((128,), BF16)
        onesf = A.alloc((64,), F32)
        epst = A.alloc((1,), F32)
        dma("sp", vec, d_vec, writes=["vec0"])
        dma("sp", ident, d_ident, writes=["ident"])
        memset("dve", onesb, 1.0, ["onesb"])
        memset("dve", onesf, 1.0, ["onesf"])
        memset("dve", epst, EPS, ["eps"])
        ts("dve", vec[:, VOFF["lngh"]:VOFF["lngh"] + 8], vec[:, VOFF["lng"]:VOFF["lng"] + 8], 0.5, None, ALU.mult, None,
           ["vec0"], ["vec"])
        m_c0 = A.mark()

        def vcol(name, i, lo=0, hi=128):
            o = VOFF[name] + i
            return vec[lo:hi, o:o + 1]

        def rms_norm(src, src_reads, nch, n, gname, dst, dst_writes, tmp, dfeat, mask_ap=None, mask_reads=(), stage="all"):
            sq, sd, rs = tmp["sq"], tmp["sd"], tmp["rs"]
            tk = tmp["tok"]
            if stage in ("all", "a"):
                _rms_a(src, src_reads, nch, n, sq, sd, rs, tk, dfeat, mask_ap, mask_reads)
            if stage in ("all", "b"):
                _rms_b(src, src_reads, nch, n, gname, dst, dst_writes, rs, tk)

        def _rms_a(src, src_reads, nch, n, sq, sd, rs, tk, dfeat, mask_ap, mask_reads):
            if isinstance(src, list):
                for c in range(nch):
                    act(sq[:, c, 0:n], src[c], AF.Square, reads=[src_reads[c]], writes=[(tk, "sq")])
            else:
                act(sq[:, 0:nch, 0:n], src[:, 0:nch, 0:n], AF.Square, reads=src_reads, writes=[(tk, "sq")])
            b = psb()
            for c in range(nch):
                mm(psum[:, b, 0:n], onesb[:, 0:128], sq[:, c, 0:n], c == 0, c == nch - 1,
                   reads=[(tk, "sq"), "onesb"], writes=[ptok(b)])
            act(sd[:, 0:n], psum[:, b, 0:n], AF.Ln, reads=[ptok(b), "eps"], writes=[(tk, "sd")], bias=epst[:, 0:1],
                scale=1.0 / dfeat)
            act(rs[:, 0:n], sd[:, 0:n], AF.Exp, reads=[(tk, "sd")], writes=[(tk, "rs")], scale=-0.5)
            if mask_ap is not None:
                tt("dve", rs[:, 0:n], rs[:, 0:n], mask_ap, ALU.mult, reads=[(tk, "rs")] + list(mask_reads), writes=[(tk, "rs")])

        def _rms_b(src, src_reads, nch, n, gname, dst, dst_writes, rs, tk):
            if gname is None and not isinstance(src, list):
                tt("dve", dst[:, 0:nch, 0:n], src[:, 0:nch, 0:n], rs[:, 0:n].unsqueeze(1).broadcast_to([128, nch, n]), ALU.mult,
                   reads=list(src_reads) + [(tk, "rs")], writes=dst_writes)
                return
            for c in range(nch):
                if isinstance(src, list):
                    s_c, r_c = src[c], [src_reads[c]]
                else:
                    s_c, r_c = src[:, c, 0:n], list(src_reads)
                if gname is None:
                    tt("dve", dst[:, c, 0:n], s_c, rs[:, 0:n], ALU.mult, reads=r_c + [(tk, "rs")], writes=dst_writes)
                else:
                    stt("dve", dst[:, c, 0:n], s_c, vcol(gname, c), rs[:, 0:n], ALU.mult, ALU.mult,
                        reads=r_c + [(tk, "rs"), "vec"], writes=dst_writes)

        def rope_tables(pos_ap, n, c2, s2, rt, writes):
            P = slice(64, 97)
            posi, f = rt["posi"], rt["f"]
            tk = rt["tok"]
            dma("sp", posi[P, 0:n], pos_ap.partition_broadcast(33), writes=[(tk, "posi")])
            cp("dve", f[0][P, 0:n], posi[P, 0:n], reads=[(tk, "posi")], writes=[(tk, 0)])
            ts("dve", f[1][P, 0:n], f[0][P, 0:n], vcol("invf", 0, 64, 97), None, ALU.mult, None, reads=[(tk, 0), "vec"], writes=[(tk, 1)])
            ts("dve", f[0][P, 0:n], f[1][P, 0:n], 1.0 / TWO_PI, None, ALU.mult, None, reads=[(tk, 1)], writes=[(tk, 0)])
            cp("dve", posi[P, 0:n], f[0][P, 0:n], reads=[(tk, 0)], writes=[(tk, "posi")])
            cp("dve", f[0][P, 0:n], posi[P, 0:n], reads=[(tk, "posi")], writes=[(tk, 0)])
            stt("dve", f[2][P, 0:n], f[0][P, 0:n], -CW1, f[1][P, 0:n], ALU.mult, ALU.add, reads=[(tk, 0), (tk, 1)], writes=[(tk, 2)])
            stt("dve", f[1][P, 0:n], f[0][P, 0:n], -CW2, f[2][P, 0:n], ALU.mult, ALU.add, reads=[(tk, 0), (tk, 2)], writes=[(tk, 1)])
            act(s2[P, 0:n], f[1][P, 0:n], AF.Sin, reads=[(tk, 1)], writes=writes)
            ts("dve", f[2][P, 0:n], f[1][P, 0:n], math.pi / 2, None, ALU.add, None, reads=[(tk, 1)], writes=[(tk, 2)])
            ts("dve", f[0][P, 0:n], f[2][P, 0:n], math.pi, -TWO_PI, ALU.is_gt, ALU.mult, reads=[(tk, 2)], writes=[(tk, 0)])
            tt("dve", f[2][P, 0:n], f[2][P, 0:n], f[0][P, 0:n], ALU.add, reads=[(tk, 2), (tk, 0)], writes=[(tk, 2)])
            act(c2[P, 0:n], f[2][P, 0:n], AF.Sin, reads=[(tk, 2)], writes=writes)

        OT = A.alloc((NH, E), BF16)
        m_ot = A.mark()
        kvn = A.alloc((2, SEQ), BF16)
        krope = A.alloc((SEQ,), BF16)
        P97 = slice(64, 97)

        m1 = A.mark()
        wkvr = A.alloc((8, 322), BF16)
        dma("pool", wkvr, d_wkvr.rearrange("(c p) n -> p c n", p=128), writes=["wkvr"])
        xts = [A.alloc((8, 512), F32) for _ in range(2)]
        hs = [A.alloc((8, 512), BF16) for _ in range(2)]
        tmpA = [dict(sq=A.alloc((8, 512), BF16), sd=A.alloc((512,), F32), rs=A.alloc((512,), F32), tok=("tA", i)) for i in range(2)]
        tmpB = [dict(sq=A.alloc((2, 512), BF16), sd=A.alloc((512,), F32), rs=A.alloc((512,), F32), tok=("tB", i)) for i in range(2)]
        rts = [dict(posi=A.alloc((512,), I32), f=[A.alloc((512,), F32) for _ in range(3)], tok=("rt", 0))] * 2
        c2s = [A.alloc((512,), F32) for _ in range(2)]
        s2s = [A.alloc((512,), F32) for _ in range(2)]
        rtmp = [[A.alloc((512,), F32) for _ in range(2)] for _ in range(2)]
        for c in range(8):
            ts("dve", wkvr[:, c, :], wkvr[:, c, :], vcol("g1", c), None, ALU.mult, None, reads=["wkvr", "vec"], writes=["wkvr"])

        def p1_front(i, stage):
            sl = i % 2
            tsl = slice(i * 512, (i + 1) * 512)
            if stage == "a":
                dma("sp", xts[sl], d_xf[i].rearrange("p (c n) -> p c n", c=8), writes=[("xt", sl)])
            rms_norm(xts[sl], [("xt", sl)], 8, 512, None, hs[sl], [("h", sl)], tmpA[sl], D, stage=stage)

        def p1_front2(i):
            sl = i % 2
            tsl = slice(i * 512, (i + 1) * 512)
            rope_tables(d_posf[:, tsl], 512, c2s[sl], s2s[sl], rts[sl], [("cs", sl)])

        def p1_back(i):
            sl = i % 2
            tsl = slice(i * 512, (i + 1) * 512)
            banks = []
            for mc in range(2):
                b = psb()
                banks.append(b)
                for c in range(8):
                    mm(psum[:, b, :], wkvr[:, c, mc * 128:(mc + 1) * 128], hs[sl][:, c, :], c == 0, c == 7,
                       reads=["wkvr", ("h", sl)], writes=[ptok(b)])
            bA, bB = psb(), psb()
            for (bb, off) in ((bA, 256), (bB, 289)):
                for c in range(8):
                    mm(psum[P97, bb, :], wkvr[:, c, off:off + 33], hs[sl][:, c, :], c == 0, c == 7,
                       reads=["wkvr", ("h", sl)], writes=[ptok(bb)])
            rms_norm([psum[:, b, :] for b in banks], [ptok(b) for b in banks], 2, 512, None, kvn[:, :, tsl], [("kvn", i)],
                     tmpB[sl], KVLR)
            tt("dve", rtmp[sl][0][P97, :], psum[P97, bA, :], c2s[sl][P97, :], ALU.mult, reads=[ptok(bA), ("cs", sl)], writes=[("rtmp0", sl)])
            tt("dve", rtmp[sl][1][P97, :], psum[P97, bB, :], s2s[sl][P97, :], ALU.mult, reads=[ptok(bB), ("cs", sl)], writes=[("rtmp1", sl)])
            tt("dve", krope[P97, tsl], rtmp[sl][0][P97, :], rtmp[sl][1][P97, :], ALU.add, reads=[("rtmp0", sl), ("rtmp1", sl)],
               writes=[("krope", i)])

        NT1 = SEQ // 512
        for i in (0, 1):
            p1_front(i, "a")
            p1_front(i, "b")
        p1_front2(0)
        for i in range(NT1):
            p1_back(i)
            if i + 2 < NT1:
                p1_front(i + 2, "a")
            if i + 1 < NT1:
                p1_front2(i + 1)
            if i + 2 < NT1:
                p1_front(i + 2, "b")
        KROPE_ALL = [("krope", i) for i in range(16)]
        KVN_ALL = [("kvn", i) for i in range(16)]
        memset("dve", krope[64:65, :], 1.0, KROPE_ALL)
        dump("kvn", kvn, KVN_ALL)
        dump("krope", krope[P97, :], KROPE_ALL)
        S.barrier()
        A.release(m1)

        qn = A.alloc((3, E), BF16)
        c2o = A.alloc((E,), F32)
        s2o = A.alloc((E,), F32)
        m2 = A.mark()
        wq = A.alloc((8, QLR), BF16)
        for b_ in range(3):
            dma("pool", wq[:, :, b_ * 128:(b_ + 1) * 128], d_wq[b_].rearrange("p (c n) -> p c n", c=8), writes=[("wq", b_)])
        xts = [A.alloc((8, TH), F32) for _ in range(2)]
        hs = [A.alloc((8, TH), BF16) for _ in range(2)]
        tmpA = [dict(sq=A.alloc((8, TH), BF16), sd=A.alloc((TH,), F32), rs=A.alloc((TH,), F32), tok=("tA", i)) for i in range(2)]
        tmpB = [dict(sq=A.alloc((3, T), BF16), sd=A.alloc((T,), F32), rs=A.alloc((T,), F32), tok=("tB", i)) for i in range(2)]
        rts = [dict(posi=A.alloc((512,), I32), f=[A.alloc((512,), F32) for _ in range(3)], tok=("rt", 0))] * 2
        def p2_front(t, stage):
            sl = t % 2
            e0 = t * T
            esl = slice(e0, e0 + T)
            if stage == "a":
                dma("sp", xts[sl][:, :, 0:T], d_xo[t].rearrange("p (c n) -> p c n", c=8), writes=[("xt", sl)])
            rms_norm(xts[sl], [("xt", sl)], 8, T, "g1", hs[sl], [("h", sl)], tmpA[sl], D, stage=stage)

        def p2_front2(t):
            sl = t % 2
            esl = slice(t * T, (t + 1) * T)
            rope_tables(d_poso[:, esl], T, c2o[:, esl], s2o[:, esl], rts[sl], [("cso", t)])

        def p2_back(t):
            sl = t % 2
            e0 = t * T
            esl = slice(e0, e0 + T)
            banks = []
            for mc in range(3):
                b = psb()
                banks.append(b)
                for c in range(8):
                    mm(psum[:, b, 0:T], wq[:, c, mc * 128:(mc + 1) * 128], hs[sl][:, c, 0:T], c == 0, c == 7,
                       reads=[("wq", mc), ("h", sl)], writes=[ptok(b)])
            rms_norm([psum[:, b, 0:T] for b in banks], [ptok(b) for b in banks], 3, T, "gq", qn[:, :, esl], [("qn", t)],
                     tmpB[sl], QLR)

        for t in (0, 1):
            p2_front(t, "a")
            p2_front(t, "b")
        p2_front2(0)
        for t in range(NT):
            p2_back(t)
            if t + 2 < NT:
                p2_front(t + 2, "a")
            if t + 1 < NT:
                p2_front2(t + 1)
            if t + 2 < NT:
                p2_front(t + 2, "b")
        QN_ALL = [("qn", t) for t in range(NT)]
        dump("qn", qn, QN_ALL)
        dump("c2o", c2o[P97, :], [("cso", t) for t in range(NT)])
        dump("s2o", s2o[P97, :], [("cso", t) for t in range(NT)])
        S.barrier()
        A.release(m2)

        m3 = A.mark()
        wqh = A.alloc((3, NH * 194), BF16)
        wk = A.alloc((2, NH * 128), BF16)
        wv = A.alloc((2, NH * 64), BF16)
        dma("pool", wqh, d_wqh.rearrange("(c p) n -> p c n", p=128), writes=["wqh"])
        dma("pool", wk, d_wk.rearrange("(c p) n -> p c n", p=128), writes=["wk"])
        dma("pool", wv, d_wv.rearrange("(c p) n -> p c n", p=128), writes=["wv"])
        for c in range(2):
            ts("dve", wk[:, c, :], wk[:, c, :], vcol("gkv", c), None, ALU.mult, None, reads=["wk", "vec"], writes=["wk"])
            ts("dve", wv[:, c, :], wv[:, c, :], vcol("gkv", c), None, ALU.mult, None, reads=["wv", "vec"], writes=["wv"])
        kaug = [A.alloc((SEQ,), BF16) for _ in range(2)]
        vbuf = [A.alloc((64, 65), BF16) for _ in range(2)]
        qaug = [A.alloc((T,), BF16) for _ in range(2)]
        pT = [A.alloc((2, T), BF16) for _ in range(3)]
        sqk = [A.alloc((512,), BF16) for _ in range(2)]
        kmx = [A.alloc((17,), F32) for _ in range(2)]
        qtmp = [[A.alloc((T,), F32) for _ in range(2)] for _ in range(2)]
        sqq = [A.alloc((T,), BF16) for _ in range(2)]
        rinv = [A.alloc((T,), F32) for _ in range(2)]
        rhi = [A.alloc((T,), BF16) for _ in range(2)]
        rlo = [A.alloc((T,), BF16) for _ in range(2)]
        sel = A.alloc((128,), BF16)
        memset("pool", sel, 0.0, ["sel0"])
        S.add("pool", lambda e: e.memset(sel[64:65, :], 1.0), reads=["sel0"], writes=["sel"])
        for i in range(2):
            memset("pool", rhi[i], 0.0, [("rhi", i)])
            memset("pool", rlo[i], 0.0, [("rlo", i)])
        osb = [A.alloc((T,), F32) for _ in range(2)]
        for i in range(2):
            memset("pool", vbuf[i][:, :, 64:65], 1.0, [("vones", i)])
        PS.avail = [6, 7]
        PS.pos = 0
        def stage_weights():
            dma("pool", s_wa.rearrange("p (c n) -> p c n", c=8), d_wa.rearrange("(c p) n -> p c n", p=128), reads=[("qc", 0)], writes=["s_wa"])
            for c4 in range(0, 8, 4):
                dma("pool", s_wg.rearrange("p (c n) -> p c n", c=8)[:, c4:c4 + 4, :],
                    d_wg.rearrange("(c p) n -> p c n", p=128)[:, c4:c4 + 4, :], writes=[("s_wg", c4)])
            dma("pool", s_wmo.rearrange("p (h n) -> p h n", h=NH), d_wmo.rearrange("(h p) n -> p h n", p=64), writes=["s_wmo"])
            dma("pool", s_wco.rearrange("p (c n) -> p c n", c=4), d_wco.rearrange("(c p) n -> p c n", p=128), writes=["s_wco"])
            dma("pool", s_wout.rearrange("p (c n) -> p c n", c=8), d_wout.rearrange("(c p) n -> p c n", p=128), writes=["s_wout"])
            for p in range(NPAIR):
                dma("pool", s_wup[p].rearrange("p (c n) -> p c n", c=8), d_wup[p].rearrange("(c p) n -> p c n", p=128),
                    writes=[("s_wup", p)])
            for k4 in range(0, NPAIR, 2):
                k5 = min(k4 + 2, NPAIR)
                dma("pool", s_wdn[:, k4 * D:k5 * D].rearrange("p (k n) -> p k n", n=D),
                    d_wdn[k4 * 128:k5 * 128, :].rearrange("(k p) n -> p k n", p=128), writes=[("s_wdn", k4)])


        def gen_kv_pieces(h):
            kb = h % 2
            pieces = []

            def k_tile(i):
                tsl = slice(i * 512, (i + 1) * 512)
                b = psb()
                for c in range(2):
                    mm(psum[:, b, :], wk[:, c, h * 128:(h + 1) * 128], kvn[:, c, tsl], c == 0, c == 1,
                       reads=["wk", ("kvn", i)], writes=[ptok(b)])
                cp("dve", kaug[kb][0:64, tsl], psum[0:64, b, :], reads=[ptok(b)], writes=[("kaug", kb, i)])

            def k_rope_rows():
                dma("sp", kaug[kb][P97, :], krope[P97, :], reads=KROPE_ALL, writes=[("kaugr", kb)])

            def k_sq(i):
                tsl = slice(i * 512, (i + 1) * 512)
                s_ = i % 2
                tt("dve", sqk[s_][0:97, :], kaug[kb][0:97, tsl], kaug[kb][0:97, tsl], ALU.mult,
                   reads=[("kaug", kb, i), ("kaugr", kb)], writes=[("sqk", s_)])

            def k_max(i):
                s_ = i % 2
                b = psb()
                mm(psum[0:97, b, :], onesb[0:97, 0:97], sqk[s_][0:97, :], True, True, reads=[("sqk", s_), "onesb"], writes=[ptok(b)])
                S.add("dve", lambda e, o=kmx[kb][64:65, i:i + 1], a=psum[64:65, b, :]: e.reduce_max(out=o, in_=a, axis=AX.X),
                      reads=[ptok(b)], writes=[("kmxp", kb)])
                if i == 15:
                    S.add("dve", lambda e, o=kmx[kb][64:65, 16:17], a=kmx[kb][64:65, 0:16]: e.reduce_max(out=o, in_=a, axis=AX.X),
                          reads=[("kmxp", kb)], writes=[("kmx", kb)])

            def v_group(g):
                b = psb()
                for jj in range(8):
                    j = g * 8 + jj
                    for c in range(2):
                        mm(psum[:, b, jj * 64:(jj + 1) * 64], kvn[:, c, j * 128:(j + 1) * 128], wv[:, c, h * 64:(h + 1) * 64],
                           c == 0, c == 1, reads=["wv", ("kvn", j // 4)], writes=[ptok(b)])
                cp("dve", vbuf[kb][:, g * 8:(g + 1) * 8, 0:64], psum[:, b, :].rearrange("p (a b) -> p a b", a=8),
                   reads=[ptok(b)], writes=[("v", kb, g)])

            if h < 2:
                pieces.append(k_rope_rows)
            for i in range(16):
                pieces.append(lambda i=i: k_tile(i))
            pieces.append(lambda: k_sq(0))
            for i in range(16):
                if i + 1 < 16:
                    pieces.append(lambda i=i: (k_sq(i + 1), k_max(i)))
                else:
                    pieces.append(lambda i=i: k_max(i))
            for g in range(8):
                pieces.append(lambda g=g: v_group(g))
            return pieces

        def gen_q_a(h, t):
            u = h * NT + t
            qs = u % 2
            esl = slice(t * T, (t + 1) * T)
            b1, b2 = psb(), psb()
            for c in range(3):
                mm(psum[0:97, b1, 0:T], wqh[:, c, h * 194:h * 194 + 97], qn[:, c, esl], c == 0, c == 2,
                   reads=["wqh", ("qn", t)], writes=[ptok(b1)])
            for c in range(3):
                mm(psum[0:97, b2, 0:T], wqh[:, c, h * 194 + 97:h * 194 + 194], qn[:, c, esl], c == 0, c == 2,
                   reads=["wqh", ("qn", t)], writes=[ptok(b2)])
            cp("dve", qaug[qs][0:64, :], psum[0:64, b1, 0:T], reads=[ptok(b1)], writes=[("qa", qs)])
            tt("dve", qtmp[qs][0][P97, :], psum[P97, b1, 0:T], c2o[P97, esl], ALU.mult, reads=[ptok(b1), ("cso", t)], writes=[("qt0", qs)])
            tt("dve", qtmp[qs][1][P97, :], psum[P97, b2, 0:T], s2o[P97, esl], ALU.mult, reads=[ptok(b2), ("cso", t)], writes=[("qt1", qs)])
            tt("dve", qaug[qs][P97, :], qtmp[qs][0][P97, :], qtmp[qs][1][P97, :], ALU.add, reads=[("qt0", qs), ("qt1", qs)],
               writes=[("qb", qs)])
            tt("dve", sqq[qs][0:97, :], qaug[qs][0:97, :], qaug[qs][0:97, :], ALU.mult, reads=[("qa", qs), ("qb", qs)], writes=[("sqq", qs)])

        def gen_q_b(h, t):
            u = h * NT + t
            qs = u % 2
            kb = h % 2
            b3 = psb()
            mm(psum[0:97, b3, 0:T], onesb[0:97, 0:97], sqq[qs][0:97, :], True, True, reads=[("sqq", qs), "onesb"], writes=[ptok(b3)])
            ts("dve", qaug[qs][64:65, :], psum[64:65, b3, 0:T], kmx[kb][64:65, 16:17], -0.5, ALU.add, ALU.mult,
               reads=[ptok(b3), ("kmx", kb), ("sqq", qs)], writes=[("qc", qs)])

        n_heads_run = NH if (debug is None or "heads" not in debug) else debug["heads"]
        units = [(h, t) for h in range(n_heads_run) for t in range(NT)]
        NG = 32
        groups = [(ui, g) for ui in range(len(units)) for g in range(NG)]

        def s_mm(gi):
            ui, g = groups[gi]
            h, t = units[ui]
            qs, kb = ui % 2, h % 2
            sb = (gi % 2) * 2
            for jj in range(2):
                j = g * 2 + jj
                mm(psum[:, sb + jj, 0:T], kaug[kb][0:97, j * 128:(j + 1) * 128], qaug[qs][0:97, :], True, True,
                   reads=[("kaug", kb, j // 4), ("kaugr", kb), ("qa", qs), ("qb", qs), ("qc", qs)], writes=[ptok(sb + jj)],
                   fuse_eng="act")

        def exp_g(gi):
            sb = (gi % 2) * 2
            ps_ = gi % 3
            act(pT[ps_][:, :, :], psum[:, sb:sb + 2, 0:T], AF.Exp, reads=[ptok(sb), ptok(sb + 1)], writes=[("pT", ps_)],
                scale=SCALE)

        def pv_mm(gi):
            ui, g = groups[gi]
            h, t = units[ui]
            kb = h % 2
            ob = 4 + (ui % 2)
            ps_ = gi % 3
            for jj in range(2):
                j = g * 2 + jj
                mm(psum[0:65, ob, 0:T], vbuf[kb][:, j, 0:65], pT[ps_][:, jj, :], j == 0, j == 63,
                   reads=[("v", kb, j // 8), ("vones", kb), ("pT", ps_)], writes=[ptok(ob)], fuse_eng="act")

        def epilogue_a(ui):
            qs = ui % 2
            ob = 4 + (ui % 2)
            recip(rinv[qs][64:65, :], psum[64:65, ob, 0:T], reads=[ptok(ob)], writes=[("rinv", qs)])
            cp("dve", rhi[qs][64:65, :], rinv[qs][64:65, :], reads=[("rinv", qs)], writes=[("rhi", qs)])
            tt("dve", rlo[qs][64:65, :], rinv[qs][64:65, :], rhi[qs][64:65, :], ALU.subtract, reads=[("rinv", qs), ("rhi", qs)],
               writes=[("rlo", qs)])
            cp("dve", osb[qs][0:64, :], psum[0:64, ob, 0:T], reads=[ptok(ob)], writes=[("osb", qs)])

        def epilogue_b(ui):
            h, t = units[ui]
            qs = ui % 2
            esl = slice(t * T, (t + 1) * T)
            bb = psb()
            mm(psum[:, bb, 0:T], sel[:, :], rhi[qs][:, :], True, False, reads=[("rhi", qs), "sel"], writes=[ptok(bb)])
            mm(psum[:, bb, 0:T], sel[:, :], rlo[qs][:, :], False, True, reads=[("rlo", qs), "sel"], writes=[ptok(bb)])
            tt("dve", OT[0:64, h, esl], osb[qs][0:64, :], psum[0:64, bb, 0:T], ALU.mult, reads=[("osb", qs), ptok(bb)],
               writes=[("OT", h, t)])

        side = {}

        def at(gi, f):
            side.setdefault(gi, []).append(f)

        PS.avail = list(range(8))
        PS.pos = 0
        gen_q_a(0, 0)
        for pc in gen_kv_pieces(0):
            pc()
        gen_q_b(0, 0)
        PS.avail = [6, 7]
        PS.pos = 0
        stage_weights()
        for h in range(n_heads_run):
            base = h * NT * NG
            if h + 1 < n_heads_run:
                for k, pc in enumerate(gen_kv_pieces(h + 1)):
                    at(base + 9 * (k // 3) + 2, pc)
        for ui in range(len(units)):
            base = ui * NG
            if ui + 1 < len(units):
                nh, nt_ = units[ui + 1]
                at(base + 8, lambda nh=nh, nt_=nt_: gen_q_a(nh, nt_))
                at(base + 14, lambda nh=nh, nt_=nt_: gen_q_b(nh, nt_))
            at(base + NG - 1, lambda ui=ui: epilogue_a(ui))
            if ui + 1 < len(units):
                at(base + NG + 4, lambda ui=ui: epilogue_b(ui))
        s_mm(0)
        s_mm(1)
        for gi in range(len(groups)):
            exp_g(gi)
            if gi + 2 < len(groups):
                s_mm(gi + 2)
            pv_mm(gi)
            for f in side.get(gi, ()):
                f()
        epilogue_b(len(units) - 1)
        OT_ALL = [("OT", h, t) for h in range(NH) for t in range(NT)]
        dump("OT", OT[0:64, :, :], OT_ALL)
        dump("kaug0", kaug[0][0:97, :], [])
        S.barrier()
        A.release(m_ot)
        PS.avail = list(range(8))
        PS.pos = 0

        s_all = A.alloc((4, E), BF16)
        m4 = A.mark()
        wa = A.alloc((8, 1024), BF16)
        diag = A.alloc((4, CK, 128), BF16)
        dma("pool", wa, s_wa.rearrange("p (c n) -> p c n", c=8), writes=["wa"])
        for m in range(4):
            o = VOFF["cw"] + m * CK
            tt("dve", diag[:, m, :, :], ident[:, :].unsqueeze(1).broadcast_to([128, CK, 128]),
               vec[:, o:o + CK].unsqueeze(2).broadcast_to([128, CK, 128]), ALU.mult,
               reads=["vec", "ident"], writes=[("diag", m)])
        xts = [A.alloc((8, TH), F32) for _ in range(2)]
        hs = [A.alloc((8, TH), BF16) for _ in range(2)]
        tmpA = [dict(sq=A.alloc((8, TH), BF16), sd=A.alloc((TH,), F32), rs=A.alloc((TH,), F32), tok=("tA", i)) for i in range(2)]
        ubuf = [A.alloc((4, TH), BF16) for _ in range(2)]
        tgs = [A.alloc((TH,), F32) for _ in range(2)]
        vhs = [A.alloc((TH,), F32) for _ in range(2)]
        vb = A.alloc((4, T), F32)
        vbb = A.alloc((4, T), BF16)
        sqv = A.alloc((4, T), BF16)
        mean = A.alloc((T,), F32)
        m2t = A.alloc((T,), F32)
        var = A.alloc((T,), F32)
        sdv = A.alloc((T,), F32)
        rsv = A.alloc((T,), F32)
        tcs = [A.alloc((T,), F32)] * 2
        tns = [A.alloc((T,), F32) for _ in range(2)]
        ths = [A.alloc((T,), F32) for _ in range(2)]
        zhs = [A.alloc((T,), F32) for _ in range(2)]

        def p4a_A(t):
            sl = t % 2
            e0 = t * T
            dma("sp", xts[sl], d_xh[t].rearrange("p (c n) -> p c n", c=8), writes=[("xt", sl)])
            rms_norm(xts[sl], [("xt", sl)], 8, TH, "g1", hs[sl], [("h", sl)], tmpA[sl], D)

        def p4a_B(t):
            sl = t % 2
            for m in range(4):
                s2_ = m % 2
                bv, bg = psb(), psb()
                for (bb, off) in ((bv, 0), (bg, 512)):
                    for c in range(8):
                        mm(psum[:, bb, 0:TH], wa[:, c, off + m * 128:off + (m + 1) * 128], hs[sl][:, c, :], c == 0, c == 7,
                           reads=["wa", ("h", sl)], writes=[ptok(bb)], fuse_eng="act")
                act(tgs[s2_], psum[:, bg, 0:TH], AF.Tanh, reads=[ptok(bg)], writes=[("tg", s2_)], scale=0.5)
                act(vhs[s2_], psum[:, bv, 0:TH], AF.Copy, reads=[ptok(bv)], writes=[("vh", s2_)], scale=0.5)
                stt("dve", ubuf[sl][:, m, :], tgs[s2_], 1.0, vhs[s2_], ALU.add, ALU.mult, reads=[("tg", s2_), ("vh", s2_)],
                    writes=[("u", sl, m)])

        def p4a_back(t):
            sl = t % 2
            e0 = t * T
            esl = slice(e0, e0 + T)
            for m in range(4):
                if m == 0 and t + 2 < NT:
                    p4a_A(t + 2)
                bc = psb()
                for j in range(CK):
                    mm(psum[:, bc, 0:T], diag[:, m, j, :], ubuf[sl][:, m, j:j + T], j == 0, j == CK - 1,
                       reads=[("diag", m), ("u", sl, m)], writes=[ptok(bc)], fuse_eng="act")
                act(vb[:, m, :], psum[:, bc, 0:T], AF.Identity, reads=[ptok(bc), "vec"], writes=[("vb", m)], bias=vcol("dwb", m))
                cp("dve", vbb[:, m, :], vb[:, m, :], reads=[("vb", m)], writes=[("vbb", m)])
                act(sqv[:, m, :], vb[:, m, :], AF.Square, reads=[("vb", m)], writes=[("sqv", m)])
            bm, bq = psb(), psb()
            for m in range(4):
                mm(psum[:, bm, 0:T], onesb[:, :], vbb[:, m, :], m == 0, m == 3, reads=[("vbb", m), "onesb"], writes=[ptok(bm)])
            for m in range(4):
                mm(psum[:, bq, 0:T], onesb[:, :], sqv[:, m, :], m == 0, m == 3, reads=[("sqv", m), "onesb"], writes=[ptok(bq)])
            act(mean, psum[:, bm, 0:T], AF.Copy, reads=[ptok(bm)], writes=["mean"], scale=1.0 / CW)
            tt("dve", m2t, mean, mean, ALU.mult, reads=["mean"], writes=["m2t"])
            stt("dve", var, psum[:, bq, 0:T], 1.0 / CW, m2t, ALU.mult, ALU.subtract, reads=[ptok(bq), "m2t"], writes=["var"])
            ts("dve", var, var, 0.0, None, ALU.max, None, reads=["var"], writes=["var"])
            act(sdv, var, AF.Ln, reads=["var", "eps"], writes=["sdv"], bias=epst[:, 0:1])
            act(rsv, sdv, AF.Exp, reads=["sdv"], writes=["rsv"], scale=-0.5)
            for m in range(4):
                s2_ = m % 2
                tt("dve", tcs[s2_], vb[:, m, :], mean, ALU.subtract, reads=[("vb", m), "mean"], writes=[("tc", 0)])
                tt("dve", tns[s2_], tcs[s2_], rsv, ALU.mult, reads=[("tc", 0), "rsv"], writes=[("tn", s2_)])
                act(ths[s2_], tns[s2_], AF.Tanh, reads=[("tn", s2_), "vec"], writes=[("th", s2_)], bias=vcol("lnbh", m),
                    scale=vcol("lngh", m))
                ts("dve", zhs[s2_], tns[s2_], vcol("lngh", m), vcol("lnbh", m), ALU.mult, ALU.add, reads=[("tn", s2_), "vec"],
                   writes=[("zh", s2_)])
                stt("dve", s_all[:, m, esl], ths[s2_], 1.0, zhs[s2_], ALU.add, ALU.mult, reads=[("th", s2_), ("zh", s2_)],
                    writes=[("s", t, m)])

        p4a_A(0)
        p4a_A(1)
        p4a_B(0)
        for t in range(NT):
            p4a_back(t)
            if t + 1 < NT:
                p4a_B(t + 1)
        dump("s_all", s_all, [("s", t, m) for t in range(NT) for m in range(4)])
        S.barrier()
        A.release(m4)

        mrg = A.alloc_top((8, E), BF16)
        wg = A.alloc((8, 2048), BF16)
        wmo = A.alloc((NH, D), BF16)
        wco = A.alloc((4, D), BF16)
        dma("pool", wg, s_wg.rearrange("p (c n) -> p c n", c=8), writes=["wg"])
        dma("pool", wmo[0:64, :, :], s_wmo.rearrange("p (h n) -> p h n", h=NH), writes=["wmo"])
        dma("pool", wco, s_wco.rearrange("p (c n) -> p c n", c=4), writes=["wco"])
        xts = [A.alloc((8, T), F32) for _ in range(2)]
        hs = [A.alloc((8, T), BF16) for _ in range(2)]
        tmpA = [dict(sq=A.alloc((8, T), BF16), sd=A.alloc((T,), F32), rs=A.alloc((T,), F32), tok=("tA", 0))] * 2
        t1s = [A.alloc((T,), F32) for _ in range(2)]
        t2s = [A.alloc((T,), F32) for _ in range(2)]
        ymh = [A.alloc((T,), F32) for _ in range(2)]
        ycs = [A.alloc((T,), F32) for _ in range(2)]
        aas = [A.alloc((T,), F32) for _ in range(2)]
        bbs = [A.alloc((T,), F32) for _ in range(2)]

        def p4b_norm(t):
            sl = t % 2
            e0 = t * T
            dma("sp", xts[sl], d_xo[t].rearrange("p (c n) -> p c n", c=8), writes=[("xt", sl)])
            rms_norm(xts[sl], [("xt", sl)], 8, T, "g1", hs[sl], [("h", sl)], tmpA[sl], D)

        p4b_norm(0)
        for t in range(NT):
            sl = t % 2
            e0 = t * T
            esl = slice(e0, e0 + T)
            for mc in range(8):
                if mc == 0 and t + 1 < NT:
                    p4b_norm(t + 1)
                s2_ = mc % 2
                b1, b2, b3, b4 = psb(), psb(), psb(), psb()
                for (bb, off) in ((b1, 0), (b2, 1024)):
                    for c in range(8):
                        mm(psum[:, bb, 0:T], wg[:, c, off + mc * 128:off + (mc + 1) * 128], hs[sl][:, c, :], c == 0, c == 7,
                           reads=["wg", ("h", sl)], writes=[ptok(bb)], fuse_eng="act")
                for hh in range(NH):
                    mm(psum[:, b3, 0:T], wmo[0:64, hh, mc * 128:(mc + 1) * 128], OT[0:64, hh, esl], hh == 0, hh == NH - 1,
                       reads=["wmo"], writes=[ptok(b3)], fuse_eng="act")
                for m in range(4):
                    mm(psum[:, b4, 0:T], wco[:, m, mc * 128:(mc + 1) * 128], s_all[:, m, esl], m == 0, m == 3,
                       reads=["wco"], writes=[ptok(b4)], fuse_eng="act")
                act(t1s[s2_], psum[:, b1, 0:T], AF.Tanh, reads=[ptok(b1)], writes=[("t1", s2_)], scale=0.5)
                act(t2s[s2_], psum[:, b2, 0:T], AF.Tanh, reads=[ptok(b2)], writes=[("t2", s2_)], scale=0.5)
                act(ymh[s2_], psum[:, b3, 0:T], AF.Copy, reads=[ptok(b3)], writes=[("ymh", s2_)], scale=0.5)
                act(ycs[s2_], psum[:, b4, 0:T], AF.Identity, reads=[ptok(b4), "vec"], writes=[("ycs", s2_)], bias=vcol("bco", mc))
                stt("dve", aas[s2_], t1s[s2_], 1.0, ycs[s2_], ALU.add, ALU.mult, reads=[("t1", s2_), ("ycs", s2_)],
                    writes=[("aa", s2_)])
                stt("dve", bbs[s2_], t2s[s2_], 1.0, ymh[s2_], ALU.add, ALU.mult, reads=[("t2", s2_), ("ymh", s2_)],
                    writes=[("bb", s2_)])
                stt("dve", mrg[:, mc, esl], aas[s2_], 0.5, bbs[s2_], ALU.mult, ALU.add, reads=[("aa", s2_), ("bb", s2_)],
                    writes=[("mrg", t, mc)])
        dump("merged", mrg, [("mrg", t, mc) for t in range(NT) for mc in range(8)])
        S.barrier()
        A.release(m_c0)

        x1 = A.alloc((8, E), F32)
        h2 = A.alloc((8, E), BF16)
        m4c = A.mark()
        maskr = A.alloc((E,), F32)
        dma("sp", maskr, d_mask.partition_broadcast(128), writes=["mask"])
        wout = A.alloc((8, D), BF16)
        dma("pool", wout, s_wout.rearrange("p (c n) -> p c n", c=8), writes=["wout"])
        xts = [A.alloc((8, T), F32) for _ in range(2)]
        tmpA = [dict(sq=A.alloc((8, T), BF16), sd=A.alloc((T,), F32), rs=A.alloc((T,), F32), tok=("tA", i)) for i in range(2)]

        def p4c_mm(t):
            sl = t % 2
            e0 = t * T
            esl = slice(e0, e0 + T)
            dma("sp", xts[sl], d_xo[t].rearrange("p (c n) -> p c n", c=8), writes=[("xt", sl)])
            for mc2 in range(8):
                b = psb()
                for mc in range(8):
                    mm(psum[:, b, 0:T], wout[:, mc, mc2 * 128:(mc2 + 1) * 128], mrg[:, mc, esl], mc == 0, mc == 7,
                       reads=["wout"], writes=[ptok(b)], fuse_eng="dve")
                tt("dve", x1[:, mc2, esl], xts[sl][:, mc2, :], psum[:, b, 0:T], ALU.add, reads=[("xt", sl), ptok(b)],
                   writes=[("x1", t)])

        def p4c_norm(t):
            sl = t % 2
            esl = slice(t * T, (t + 1) * T)
            rms_norm(x1[:, :, esl], [("x1", t)], 8, T, "g2", h2[:, :, esl], [("h2", t)], tmpA[sl], D,
                     mask_ap=maskr[:, esl], mask_reads=["mask"])

        for t in range(NT):
            p4c_mm(t)
            if t > 0:
                p4c_norm(t - 1)
        p4c_norm(NT - 1)
        dump("x1", x1, [("x1", t) for t in range(NT)])
        dump("h2", h2, [("h2", t) for t in range(NT)])
        S.barrier()
        A.release(m4c)
        A.release_top()

        m5 = A.mark()
        GROUPS = [(0, 6), (6, 12), (12, 17), (17, 22)]
        grp_of = {}
        for gi, (p0, p1) in enumerate(GROUPS):
            for p in range(p0, p1):
                grp_of[p] = (gi, p0, p1)
        actb = A.alloc((6, OWN), BF16)
        wdn = A.alloc((6, D), BF16)
        wups = [A.alloc((8, 256), BF16) for _ in range(3)]
        gbufs = [A.alloc((E,), F32) for _ in range(2)]
        ubufs = [A.alloc((E,), F32) for _ in range(2)]
        gc = A.alloc((OWN,), F32)
        uc = A.alloc((OWN,), F32)
        thf = A.alloc((OWN,), F32)
        wdn_v = d_wdn.rearrange("(k p) n -> p k n", p=128)

        def load_wup(p):
            if p < NPAIR:
                dma("pool", wups[p % 3], s_wup[p].rearrange("p (c n) -> p c n", c=8), writes=[("wup", p % 3)])

        def load_wdn(gi):
            p0, p1 = GROUPS[gi]
            dma("pool", wdn[:, 0:p1 - p0, :], s_wdn[:, p0 * D:p1 * D].rearrange("p (k n) -> p k n", n=D), writes=["wdn"])

        def ffn_a(p, tiles):
            ws, bs = p % 3, p % 2
            for t in tiles:
                esl = slice(t * T, (t + 1) * T)
                bg, bu = psb(), psb()
                for (bb, off) in ((bg, 0), (bu, 128)):
                    for c in range(8):
                        mm(psum[:, bb, 0:T], wups[ws][:, c, off:off + 128], h2[:, c, esl], c == 0, c == 7,
                           reads=[("wup", ws)], writes=[ptok(bb)], fuse_eng="act")
                act(gbufs[bs][:, esl], psum[:, bg, 0:T], AF.Copy, reads=[ptok(bg)], writes=[("gbuf", bs, t)])
                act(ubufs[bs][:, esl], psum[:, bu, 0:T], AF.Copy, reads=[ptok(bu)], writes=[("ubuf", bs, t)])

        def ffn_b(p):
            bs = p % 2
            for (src, dst, q, rn, wn) in ((gbufs[bs], gc, p, "gbuf", "gc"), (ubufs[bs], uc, NPAIR + p, "ubuf", "uc")):
                rd = [(rn, bs, t) for t in range(NT)]
                act(dst, src[:, 0:OWN], AF.Identity, reads=rd + ["vec"], writes=[wn], bias=vcol("fb", q), scale=vcol("fw", q))
                stt("dve", dst, src[:, 1:OWN + 1], vcol("fw", 44 + q), dst, ALU.mult, ALU.add, reads=rd + [wn, "vec"], writes=[wn])
                stt("dve", dst, src[:, 2:OWN + 2], vcol("fw", 88 + q), dst, ALU.mult, ALU.add, reads=rd + [wn, "vec"], writes=[wn])

        def ffn_c(p):
            gi, p0, p1 = grp_of[p]
            kk = p - p0
            act(thf, gc, AF.Tanh, reads=["gc"], writes=["thf"], scale=0.5)
            stt("dve", thf, thf, 1.0, gc, ALU.add, ALU.mult, reads=["thf", "gc"], writes=["thf"])
            stt("dve", actb[:, kk, :], thf, 0.5, uc, ALU.mult, ALU.mult, reads=["thf", "uc"], writes=[("actb", kk)])
            if p == p1 - 1:
                npg = p1 - p0
                for i in range(4):
                    for mc2 in range(8):
                        b = psb()
                        for k2 in range(npg):
                            mm(psum[:, b, :], wdn[:, k2, mc2 * 128:(mc2 + 1) * 128], actb[:, k2, i * 512:(i + 1) * 512],
                               k2 == 0, k2 == npg - 1, reads=["wdn", ("actb", k2)], writes=[ptok(b)], fuse_eng="dve")
                        xs = x1[:, mc2, 1 + i * 512:1 + (i + 1) * 512]
                        tt("dve", xs, xs, psum[:, b, :], ALU.add, reads=[ptok(b), ("x2", i)], writes=[("x2", i)])
                if gi + 1 < len(GROUPS):
                    load_wdn(gi + 1)

        load_wup(0)
        load_wup(1)
        load_wdn(0)
        load_wup(2)
        ffn_a(0, range(0, 3))
        ffn_a(0, range(3, NT))
        ffn_b(0)
        for p in range(1, NPAIR):
            load_wup(p + 2)
            ffn_a(p, range(0, 4))
            ffn_c(p - 1)
            ffn_a(p, range(4, NT))
            ffn_b(p)
        ffn_c(NPAIR - 1)
        dump("x2", x1, [("x2", i) for i in range(4)])
        S.barrier()
        A.release(m5)
        otile = [A.alloc((8, 512), F32) for _ in range(2)]
        tmpF = [dict(sq=A.alloc((8, 512), BF16), sd=A.alloc((512,), F32), rs=A.alloc((512,), F32), tok=("tF", i)) for i in range(2)]
        out_v = d_out.rearrange("(c p) t -> p c t", p=128)
        for i in range(4):
            sl = i % 2
            rms_norm(x1[:, :, 1 + i * 512:1 + (i + 1) * 512], [("x2", i)], 8, 512, "gf", otile[sl], [("ot", sl)], tmpF[sl], D)
            out_dmas.append(dma("sp", out_v[:, :, i * 512:(i + 1) * 512], otile[sl], reads=[("ot", sl)]))
        S.emit(out_dma_ops=out_dmas)
        print("arena peak bytes/partition:", A.peak, "ops:", {e: len(S.ops[e]) for e in ENGS})
    return nc, dbg_outs


def _prep_shared(inp):
    f = np.float32
    w_in = np.asarray(inp["w_in"][0], f)
    sh = {}
    a_in = w_in[:, 0:1024]
    qc = w_in[:, 1024:1408]
    kvc = w_in[:, 1408:1664]
    rope = w_in[:, 1664:1696]
    gates = w_in[:, 1696:3744]
    z1 = np.zeros((D, 1), f)
    rope_sw = np.concatenate([rope[:, 16:32], rope[:, 0:16]], axis=1)
    sh["wkvr"] = np.ascontiguousarray(np.concatenate([kvc, z1, rope, z1, rope_sw], axis=1))
    def blocked(w, cols):
        kc = w.shape[0] // 128
        out = np.empty((len(cols), 128, kc * 128), f)
        for i, c0 in enumerate(cols):
            out[i] = w[:, c0:c0 + 128].reshape(kc, 128, 128).transpose(1, 0, 2).reshape(128, kc * 128)
        return out

    sh["wq"] = blocked(qc, [0, 128, 256])
    sh["wa"] = np.ascontiguousarray(a_in)
    sh["wg"] = np.ascontiguousarray(gates)
    w_uq = np.asarray(inp["w_uq"][0], f).reshape(QLR, NH, 96)
    zq = np.zeros((QLR, NH, 1), f)
    qrope = w_uq[:, :, 64:96]
    qrope_sw = np.concatenate([qrope[:, :, 16:32], qrope[:, :, 0:16]], axis=2)
    z64 = np.zeros((QLR, NH, 64), f)
    sh["wqh"] = np.ascontiguousarray(
        np.concatenate([w_uq[:, :, 0:64], zq, qrope, z64, zq, qrope_sw], axis=2).reshape(QLR, NH * 194))
    w_ukv = np.asarray(inp["w_ukv"][0], f).reshape(KVLR, NH, 128)
    sh["wk"] = np.ascontiguousarray(
        np.concatenate([w_ukv[:, :, 0:64], np.zeros((KVLR, NH, 64), f)], axis=2).reshape(KVLR, NH * 128))
    sh["wv"] = np.ascontiguousarray(w_ukv[:, :, 64:128].reshape(KVLR, NH * 64))
    sh["wco"] = np.ascontiguousarray(np.asarray(inp["w_conv_out"][0], f))
    sh["wmo"] = np.ascontiguousarray(np.asarray(inp["w_mla_out"][0], f))
    sh["wout"] = np.ascontiguousarray(np.asarray(inp["w_out"][0], f))
    w_up = np.asarray(inp["w_ffn_up"][0], f)
    gpart = w_up[:, 0:DFF].reshape(D, NPAIR, 128)
    upart = w_up[:, DFF:2 * DFF].reshape(D, NPAIR, 128)
    sh["wup"] = np.ascontiguousarray(np.concatenate([gpart, upart], axis=2).transpose(1, 0, 2))
    sh["wdn"] = np.ascontiguousarray(np.asarray(inp["w_ffn_down"][0], f))
    vec = np.zeros((128, NV), f)

    def put(name, arr, nch):
        vec[:, VOFF[name]:VOFF[name] + nch] = np.asarray(arr, f).reshape(nch, 128).T

    put("g1", inp["norm1_g"][0], 8)
    put("gq", inp["q_norm_g"][0], 3)
    put("gkv", inp["kv_norm_g"][0], 2)
    put("dwb", inp["conv_dw_b"][0], 4)
    put("lng", inp["conv_ln_g"][0], 4)
    put("lnb", inp["conv_ln_b"][0], 4)
    put("bco", inp["b_conv_out"][0], 8)
    put("g2", inp["norm2_g"][0], 8)
    put("gf", inp["norm_f_g"], 8)
    put("fw", inp["ffn_dw_w"][0], 132)
    put("fb", inp["ffn_dw_b"][0], 44)
    cwm = np.asarray(inp["conv_dw_w"][0], f).reshape(CK, 4, 128).transpose(1, 0, 2)
    vec[:, VOFF["cw"]:VOFF["cw"] + 4 * CK] = cwm.reshape(4 * CK, 128).T
    invf = (1.0 / (np.float32(10000.0) ** (np.arange(0, 32, 2, dtype=np.float32) / np.float32(32)))).astype(f)
    vec[65:81, VOFF["invf"]] = -invf
    vec[81:97, VOFF["invf"]] = invf
    vec[65:81, VOFF["sgn"]] = -1.0
    vec[81:97, VOFF["sgn"]] = 1.0
    sh["vec"] = vec
    sh["ident"] = np.eye(128, dtype=f)
    return sh


def _prep_core(inp, core, xT):
    b, c = divmod(core, 4)
    f = np.float32
    m = {}
    m["xf"] = xT[b]
    lo = c * OWN - 16
    xo = np.zeros((D, XO), f)
    s0, s1 = max(lo, 0), min(lo + XO, SEQ)
    xo[:, s0 - lo:s1 - lo] = xT[b + NB][:, s0:s1]
    xo3 = xo.reshape(8, 128, XO)
    m["xo"] = np.ascontiguousarray(
        np.stack([xo3[:, :, 15 + t * T:15 + (t + 1) * T].transpose(1, 0, 2).reshape(128, 8 * T) for t in range(NT)]))
    m["xh"] = np.ascontiguousarray(
        np.stack([xo3[:, :, t * T:t * T + TH].transpose(1, 0, 2).reshape(128, 8 * TH) for t in range(NT)]))
    pos = np.asarray(inp["positions"], np.int32)
    m["posf"] = np.ascontiguousarray(pos[b:b + 1, :])
    elo = c * OWN - 1
    poso = np.zeros((1, E), np.int32)
    mask = np.zeros((1, E), f)
    s0, s1 = max(elo, 0), min(elo + E, SEQ)
    poso[0, s0 - elo:s1 - elo] = pos[b, s0:s1]
    mask[0, s0 - elo:s1 - elo] = 1.0
    m["poso"] = poso
    m["mask"] = mask
    return m


def _prep_x(x):
    xTp = [np.ascontiguousarray(x[b].T) for b in range(NB)]
    return [np.ascontiguousarray(xt.reshape(8, 128, SEQ // 512, 512).transpose(2, 1, 0, 3).reshape(SEQ // 512, 128, 8 * 512))
            for xt in xTp] + xTp


_NC_CACHE = {}


def kernel(**inputs):
    x = np.asarray(inputs["x"], np.float32)
    xT = _prep_x(x)
    sh = _prep_shared(inputs)
    in_maps = []
    for core in range(8):
        m = dict(sh)
        m.update(_prep_core(inputs, core, xT))
        in_maps.append(m)
    if "nc" not in _NC_CACHE:
        _NC_CACHE["nc"] = build_nc()[0]
    nc = _NC_CACHE["nc"]
    res = run_bass_kernel_spmd(nc, in_maps, core_ids=list(range(8)))
    out = np.empty((NB, SEQ, D), np.float32)
    for core in range(8):
        b, c = divmod(core, 4)
        out[b, c * OWN:(c + 1) * OWN, :] = np.asarray(res.results[core]["out"]).T
    return out
```

```python
import contextlib
import math
import numpy as np
import concourse.bass as bass
import concourse.mybir as mybir
from concourse.bass_utils import run_bass_kernel_spmd

F32 = mybir.dt.float32
BF16 = mybir.dt.bfloat16
I32 = mybir.dt.int32
AF = mybir.ActivationFunctionType
ALU = mybir.AluOpType
AX = mybir.AxisListType

D = 1024
SEQ = 8192
NB = 2
OWN = 2048
E = 2050
XO = 2080
T = 410
NT = 5
TH = T + 30
CW = 512
CK = 31
NH = 8
QLR = 384
KVLR = 256
DFF = 2816
NPAIR = 22
EPS = 1e-6
SCALE = 96.0 ** -0.5
TWO_PI = 2.0 * math.pi
CW1 = 6.28125
CW2 = TWO_PI - CW1

ENGS = ("pe", "act", "dve", "pool", "sp")
N_DMA_SEMS = 8


class Op:
    __slots__ = ("eng", "fn", "deps", "is_dma", "signal", "count_after", "dma_sem", "dma_val", "dma_prev", "fuse_eng")

    def __init__(self, eng, fn, is_dma):
        self.eng = eng
        self.fn = fn
        self.is_dma = is_dma
        self.deps = []
        self.signal = False
        self.count_after = 0
        self.dma_sem = None
        self.dma_val = 0
        self.dma_prev = None
        self.fuse_eng = None


class Sched:
    def __init__(self, nc):
        self.nc = nc
        self.ops = {e: [] for e in ENGS}
        self.last_writer = {}
        self.readers = {}
        self.dma_ops = {e: [] for e in ENGS}
        self.pending_barrier = {e: None for e in ENGS}

    def add(self, eng, fn, reads=(), writes=(), dma=False, fuse_eng=None):
        op = Op(eng, fn, dma)
        op.fuse_eng = fuse_eng
        deps = []
        for r in reads:
            w = self.last_writer.get(r)
            if w is not None:
                deps.append((w, "raw"))
        for w_ in writes:
            w = self.last_writer.get(w_)
            if w is not None:
                deps.append((w, "waw"))
            for rd in self.readers.get(w_, ()):
                deps.append((rd, "war"))
        pb = self.pending_barrier[eng]
        if pb is not None:
            deps.extend((d, "raw") for d in pb)
            self.pending_barrier[eng] = None
        seen = set()
        for d, kind in deps:
            if d is op or id(d) in seen:
                continue
            if (not d.is_dma) and (not dma) and d.eng == eng:
                if eng == "pe" or kind != "raw":
                    continue
            seen.add(id(d))
            op.deps.append(d)
        for r in reads:
            self.readers.setdefault(r, []).append(op)
        for w_ in writes:
            self.last_writer[w_] = op
            self.readers[w_] = []
        if dma:
            lst = self.dma_ops[eng]
            k = len(lst)
            op.dma_sem = k % N_DMA_SEMS
            op.dma_val = 16 * (k // N_DMA_SEMS + 1)
            if k >= N_DMA_SEMS:
                op.dma_prev = lst[k - N_DMA_SEMS]
            lst.append(op)
        self.ops[eng].append(op)
        return op

    def barrier(self):
        deps = []
        for e in ENGS:
            comp = [o for o in self.ops[e] if not o.is_dma]
            if comp:
                deps.append(comp[-1])
            deps.extend(self.dma_ops[e][-N_DMA_SEMS:])
        for e in ENGS:
            cur = self.pending_barrier[e]
            self.pending_barrier[e] = deps if cur is None else (cur + deps)
        self.last_writer = {}
        self.readers = {}

    def emit(self, out_dma_ops=()):
        nc = self.nc
        for e in ENGS:
            for op in self.ops[e]:
                for d in op.deps:
                    if not d.is_dma:
                        d.signal = True
        for e in ENGS:
            c = 0
            for op in self.ops[e]:
                if (not op.is_dma) and op.signal:
                    c += 1
                op.count_after = c
        with contextlib.ExitStack() as st:
            csem = {e: st.enter_context(nc.semaphore("c_" + e)) for e in ("pe", "act", "dve", "pool")}
            dsem = {}
            for e in ENGS:
                if self.dma_ops[e]:
                    dsem[e] = [st.enter_context(nc.semaphore("d_%s%d" % (e, i))) for i in range(N_DMA_SEMS)]
            block = st.enter_context(nc.Block())

            def run(e, eng):
                waited = {}

                def wait(sem, val):
                    key = id(sem)
                    if waited.get(key, 0) >= val:
                        return
                    waited[key] = val
                    eng.wait_ge(sem, val)

                for op in self.ops[e]:
                    need = {}
                    for d in op.deps:
                        if d.is_dma:
                            k = ("d", d.eng, d.dma_sem)
                            v = d.dma_val
                        else:
                            k = ("c", d.eng)
                            v = d.count_after
                        if v > need.get(k, 0):
                            need[k] = v
                    if op.is_dma and op.dma_prev is not None:
                        k = ("d", e, op.dma_sem)
                        if op.dma_prev.dma_val > need.get(k, 0):
                            need[k] = op.dma_prev.dma_val
                    fused = None
                    for k, v in need.items():
                        if k[0] == "d":
                            wait(dsem[k[1]][k[2]], v)
                        elif op.fuse_eng is not None and k[1] == op.fuse_eng:
                            fused = (csem[k[1]], v)
                        else:
                            wait(csem[k[1]], v)
                    ins = op.fn(eng)
                    if fused is not None and waited.get(id(fused[0]), 0) < fused[1]:
                        waited[id(fused[0])] = fused[1]
                        ins._wait_ge(fused[0], fused[1])
                    if op.is_dma:
                        ins.then_inc(dsem[e][op.dma_sem], 16)
                    elif op.signal:
                        ins.then_inc(csem[e], 1)
                if e == "sp":
                    for d in out_dma_ops:
                        wait(dsem[d.eng][d.dma_sem], d.dma_val)

            @block.tensor
            def _(eng):
                run("pe", eng)

            @block.scalar
            def _(eng):
                run("act", eng)

            @block.vector
            def _(eng):
                run("dve", eng)

            @block.gpsimd
            def _(eng):
                run("pool", eng)

            @block.sync
            def _(eng):
                run("sp", eng)


class Arena:
    def __init__(self, ap, nelem):
        self.ap = ap
        self.nbytes = nelem * 2
        self.top = 0
        self.hi = self.nbytes
        self.peak = 0

    def alloc(self, shape, dtype):
        n = 1
        for s in shape:
            n *= s
        size = {BF16: 2, F32: 4, I32: 4}[dtype]
        nb = (n * size + 31) // 32 * 32
        off = self.top
        self.top += nb
        self.peak = max(self.peak, self.top)
        self.peak = max(self.peak, self.top + (self.nbytes - self.hi))
        assert self.top <= self.hi, ("SBUF arena overflow", self.top, self.hi)
        v = self.ap[:, off // 2:(off + n * size) // 2]
        if dtype != BF16:
            v = v.bitcast(dtype)
        if len(shape) == 2:
            v = v.rearrange("p (a b) -> p a b", a=shape[0])
        elif len(shape) == 3:
            v = v.rearrange("p (a b c) -> p a b c", a=shape[0], b=shape[1])
        return v

    def alloc_top(self, shape, dtype):
        n = 1
        for x in shape:
            n *= x
        size = {BF16: 2, F32: 4, I32: 4}[dtype]
        nb = (n * size + 31) // 32 * 32
        self.hi -= nb
        off = self.hi
        assert self.top <= self.hi, ("SBUF arena overflow (top)", self.top, self.hi)
        self.peak = max(self.peak, self.top + (self.nbytes - self.hi))
        v = self.ap[:, off // 2:(off + n * size) // 2]
        if dtype != BF16:
            v = v.bitcast(dtype)
        if len(shape) == 2:
            v = v.rearrange("p (a b) -> p a b", a=shape[0])
        return v

    def release_top(self):
        self.hi = self.nbytes

    def mark(self):
        return self.top

    def release(self, m):
        self.top = m


VOFF = {}
_o = 0
for _name, _n in (("g1", 8), ("gq", 3), ("gkv", 2), ("dwb", 4), ("lng", 4), ("lnb", 4), ("lngh", 4), ("lnbh", 4),
                  ("bco", 8), ("g2", 8), ("gf", 8), ("fw", 132), ("fb", 44), ("cw", 124), ("invf", 1), ("sgn", 1)):
    VOFF[_name] = _o
    _o += _n
NV = _o


def build_nc(debug=None):
    nc = bass.Bass("TRN2", target_bir_lowering=False)
    dt_in = lambda name, shape, dt=F32: nc.dram_tensor(name, list(shape), dt, kind="ExternalInput").ap()
    d_xf = dt_in("xf", (SEQ // 512, 128, 8 * 512))
    d_xo = dt_in("xo", (NT, 128, 8 * T))
    d_xh = dt_in("xh", (NT, 128, 8 * TH))
    d_posf = dt_in("posf", (1, SEQ), I32)
    d_poso = dt_in("poso", (1, E), I32)
    d_mask = dt_in("mask", (1, E))
    d_vec = dt_in("vec", (128, NV))
    d_ident = dt_in("ident", (128, 128))
    d_wkvr = dt_in("wkvr", (D, 322))
    d_wq = dt_in("wq", (3, 128, 8 * 128))
    d_wa = dt_in("wa", (D, 1024))
    d_wg = dt_in("wg", (D, 2048))
    d_wqh = dt_in("wqh", (QLR, NH * 194))
    d_wk = dt_in("wk", (KVLR, NH * 128))
    d_wv = dt_in("wv", (KVLR, NH * 64))
    d_wco = dt_in("wco", (CW, D))
    d_wmo = dt_in("wmo", (NH * 64, D))
    d_wout = dt_in("wout", (D, D))
    d_wup = dt_in("wup", (NPAIR, D, 256))
    d_wdn = dt_in("wdn", (DFF, D))
    d_out = nc.dram_tensor("out", [D, OWN], F32, kind="ExternalOutput").ap()
    s_wa = nc.dram_tensor("s_wa", [128, 8 * 1024], BF16).ap()
    s_wg = nc.dram_tensor("s_wg", [128, 8 * 2048], BF16).ap()
    s_wco = nc.dram_tensor("s_wco", [128, 4 * D], BF16).ap()
    s_wmo = nc.dram_tensor("s_wmo", [64, NH * D], BF16).ap()
    s_wout = nc.dram_tensor("s_wout", [128, 8 * D], BF16).ap()
    s_wup = nc.dram_tensor("s_wup", [NPAIR, 128, 8 * 256], BF16).ap()
    s_wdn = nc.dram_tensor("s_wdn", [128, NPAIR * D], BF16).ap()
    dbg_outs = {}

    with contextlib.ExitStack() as st:
        ARENA_ELEMS = 106000
        arena_t = st.enter_context(nc.sbuf_tensor("arena", [128, ARENA_ELEMS], BF16))
        psum = st.enter_context(nc.psum_tensor("psum", [128, 8, 512], F32))
        A = Arena(arena_t, ARENA_ELEMS)
        S = Sched(nc)
        out_dmas = []

        class PS:
            avail = list(range(8))
            pos = 0

        def psb():
            b = PS.avail[PS.pos % len(PS.avail)]
            PS.pos += 1
            return b

        def ptok(b):
            return ("ps", b)

        def dma(eng, out, in_, reads=(), writes=()):
            return S.add(eng, lambda e: e.dma_start(out=out, in_=in_), reads=reads, writes=writes, dma=True)

        def mm(out, lhsT, rhs, start, stop, reads, writes, fuse_eng=None):
            S.add("pe", lambda e: e.matmul(out, lhsT=lhsT, rhs=rhs, start=start, stop=stop), reads=reads, writes=writes,
                  fuse_eng=fuse_eng)

        def act(out, in_, func, reads, writes, bias=None, scale=1.0):
            if bias is None:
                S.add("act", lambda e: e.activation(out=out, in_=in_, func=func, scale=scale), reads=reads, writes=writes)
            else:
                S.add("act", lambda e: e.activation(out=out, in_=in_, func=func, bias=bias, scale=scale), reads=reads, writes=writes)

        def tt(eng, out, in0, in1, op, reads, writes):
            S.add(eng, lambda e: e.tensor_tensor(out=out, in0=in0, in1=in1, op=op), reads=reads, writes=writes)

        def ts(eng, out, in0, s1, s2, op0, op1, reads, writes):
            if op1 is None:
                S.add(eng, lambda e: e.tensor_scalar(out=out, in0=in0, scalar1=s1, scalar2=None, op0=op0), reads=reads, writes=writes)
            else:
                S.add(eng, lambda e: e.tensor_scalar(out=out, in0=in0, scalar1=s1, scalar2=s2, op0=op0, op1=op1), reads=reads, writes=writes)

        def stt(eng, out, in0, scalar, in1, op0, op1, reads, writes):
            S.add(eng, lambda e: e.scalar_tensor_tensor(out=out, in0=in0, scalar=scalar, in1=in1, op0=op0, op1=op1),
                  reads=reads, writes=writes)

        def cp(eng, out, in_, reads, writes):
            S.add(eng, lambda e: e.tensor_copy(out=out, in_=in_), reads=reads, writes=writes)

        def recip(out, in_, reads, writes):
            S.add("dve", lambda e: e.reciprocal(out=out, in_=in_), reads=reads, writes=writes)

        def memset(eng, ap, val, writes):
            S.add(eng, lambda e: e.memset(ap, val), writes=writes)

        def dump(name, ap, reads):
            if debug is None or name not in debug:
                return
            shape = list(ap.shape)
            dten = nc.dram_tensor("dbg_" + name, shape, ap.dtype, kind="ExternalOutput").ap()
            dbg_outs[name] = dten
            out_dmas.append(dma("sp", dten, ap, reads=reads))

        vec = A.alloc((NV,), F32)
        ident = A.alloc((128,), F32)
        onesb = A.alloc((128,), BF16)
        onesf = A.alloc((64,), F32)
        epst = A.alloc((1,), F32)
        dma("sp", vec, d_vec, writes=["vec0"])
        dma("sp", ident, d_ident, writes=["ident"])
        memset("dve", onesb, 1.0, ["onesb"])
        memset("dve", onesf, 1.0, ["onesf"])
        memset("dve", epst, EPS, ["eps"])
        ts("dve", vec[:, VOFF["lngh"]:VOFF["lngh"] + 8], vec[:, VOFF["lng"]:VOFF["lng"] + 8], 0.5, None, ALU.mult, None,
           ["vec0"], ["vec"])
        m_c0 = A.mark()

        def vcol(name, i, lo=0, hi=128):
            o = VOFF[name] + i
            return vec[lo:hi, o:o + 1]

        def rms_norm(src, src_reads, nch, n, gname, dst, dst_writes, tmp, dfeat, mask_ap=None, mask_reads=(), stage="all"):
            sq, sd, rs = tmp["sq"], tmp["sd"], tmp["rs"]
            tk = tmp["tok"]
            if stage in ("all", "a"):
                _rms_a(src, src_reads, nch, n, sq, sd, rs, tk, dfeat, mask_ap, mask_reads)
            if stage in ("all", "b"):
                _rms_b(src, src_reads, nch, n, gname, dst, dst_writes, rs, tk)

        def _rms_a(src, src_reads, nch, n, sq, sd, rs, tk, dfeat, mask_ap, mask_reads):
            if isinstance(src, list):
                for c in range(nch):
                    act(sq[:, c, 0:n], src[c], AF.Square, reads=[src_reads[c]], writes=[(tk, "sq")])
            else:
                act(sq[:, 0:nch, 0:n], src[:, 0:nch, 0:n], AF.Square, reads=src_reads, writes=[(tk, "sq")])
            b = psb()
            for c in range(nch):
                mm(psum[:, b, 0:n], onesb[:, 0:128], sq[:, c, 0:n], c == 0, c == nch - 1,
                   reads=[(tk, "sq"), "onesb"], writes=[ptok(b)])
            act(sd[:, 0:n], psum[:, b, 0:n], AF.Ln, reads=[ptok(b), "eps"], writes=[(tk, "sd")], bias=epst[:, 0:1],
                scale=1.0 / dfeat)
            act(rs[:, 0:n], sd[:, 0:n], AF.Exp, reads=[(tk, "sd")], writes=[(tk, "rs")], scale=-0.5)
            if mask_ap is not None:
                tt("dve", rs[:, 0:n], rs[:, 0:n], mask_ap, ALU.mult, reads=[(tk, "rs")] + list(mask_reads), writes=[(tk, "rs")])

        def _rms_b(src, src_reads, nch, n, gname, dst, dst_writes, rs, tk):
            if gname is None and not isinstance(src, list):
                tt("dve", dst[:, 0:nch, 0:n], src[:, 0:nch, 0:n], rs[:, 0:n].unsqueeze(1).broadcast_to([128, nch, n]), ALU.mult,
                   reads=list(src_reads) + [(tk, "rs")], writes=dst_writes)
                return
            for c in range(nch):
                if isinstance(src, list):
                    s_c, r_c = src[c], [src_reads[c]]
                else:
                    s_c, r_c = src[:, c, 0:n], list(src_reads)
                if gname is None:
                    tt("dve", dst[:, c, 0:n], s_c, rs[:, 0:n], ALU.mult, reads=r_c + [(tk, "rs")], writes=dst_writes)
                else:
                    stt("dve", dst[:, c, 0:n], s_c, vcol(gname, c), rs[:, 0:n], ALU.mult, ALU.mult,
                        reads=r_c + [(tk, "rs"), "vec"], writes=dst_writes)

        def rope_tables(pos_ap, n, c2, s2, rt, writes):
            P = slice(64, 97)
            posi, f = rt["posi"], rt["f"]
            tk = rt["tok"]
            dma("sp", posi[P, 0:n], pos_ap.partition_broadcast(33), writes=[(tk, "posi")])
            cp("dve", f[0][P, 0:n], posi[P, 0:n], reads=[(tk, "posi")], writes=[(tk, 0)])
            ts("dve", f[1][P, 0:n], f[0][P, 0:n], vcol("invf", 0, 64, 97), None, ALU.mult, None, reads=[(tk, 0), "vec"], writes=[(tk, 1)])
            ts("dve", f[0][P, 0:n], f[1][P, 0:n], 1.0 / TWO_PI, None, ALU.mult, None, reads=[(tk, 1)], writes=[(tk, 0)])
            cp("dve", posi[P, 0:n], f[0][P, 0:n], reads=[(tk, 0)], writes=[(tk, "posi")])
            cp("dve", f[0][P, 0:n], posi[P, 0:n], reads=[(tk, "posi")], writes=[(tk, 0)])
            stt("dve", f[2][P, 0:n], f[0][P, 0:n], -CW1, f[1][P, 0:n], ALU.mult, ALU.add, reads=[(tk, 0), (tk, 1)], writes=[(tk, 2)])
            stt("dve", f[1][P, 0:n], f[0][P, 0:n], -CW2, f[2][P, 0:n], ALU.mult, ALU.add, reads=[(tk, 0), (tk, 2)], writes=[(tk, 1)])
            act(s2[P, 0:n], f[1][P, 0:n], AF.Sin, reads=[(tk, 1)], writes=writes)
            ts("dve", f[2][P, 0:n], f[1][P, 0:n], math.pi / 2, None, ALU.add, None, reads=[(tk, 1)], writes=[(tk, 2)])
            ts("dve", f[0][P, 0:n], f[2][P, 0:n], math.pi, -TWO_PI, ALU.is_gt, ALU.mult, reads=[(tk, 2)], writes=[(tk, 0)])
            tt("dve", f[2][P, 0:n], f[2][P, 0:n], f[0][P, 0:n], ALU.add, reads=[(tk, 2), (tk, 0)], writes=[(tk, 2)])
            act(c2[P, 0:n], f[2][P, 0:n], AF.Sin, reads=[(tk, 2)], writes=writes)

        OT = A.alloc((NH, E), BF16)
        m_ot = A.mark()
        kvn = A.alloc((2, SEQ), BF16)
        krope = A.alloc((SEQ,), BF16)
        P97 = slice(64, 97)

        m1 = A.mark()
        wkvr = A.alloc((8, 322), BF16)
        dma("pool", wkvr, d_wkvr.rearrange("(c p) n -> p c n", p=128), writes=["wkvr"])
        xts = [A.alloc((8, 512), F32) for _ in range(2)]
        hs = [A.alloc((8, 512), BF16) for _ in range(2)]
        tmpA = [dict(sq=A.alloc((8, 512), BF16), sd=A.alloc((512,), F32), rs=A.alloc((512,), F32), tok=("tA", i)) for i in range(2)]
        tmpB = [dict(sq=A.alloc((2, 512), BF16), sd=A.alloc((512,), F32), rs=A.alloc((512,), F32), tok=("tB", i)) for i in range(2)]
        rts = [dict(posi=A.alloc((512,), I32), f=[A.alloc((512,), F32) for _ in range(3)], tok=("rt", 0))] * 2
        c2s = [A.alloc((512,), F32) for _ in range(2)]
        s2s = [A.alloc((512,), F32) for _ in range(2)]
        rtmp = [[A.alloc((512,), F32) for _ in range(2)] for _ in range(2)]
        for c in range(8):
            ts("dve", wkvr[:, c, :], wkvr[:, c, :], vcol("g1", c), None, ALU.mult, None, reads=["wkvr", "vec"], writes=["wkvr"])

        def p1_front(i, stage):
            sl = i % 2
            tsl = slice(i * 512, (i + 1) * 512)
            if stage == "a":
                dma("sp", xts[sl], d_xf[i].rearrange("p (c n) -> p c n", c=8), writes=[("xt", sl)])
            rms_norm(xts[sl], [("xt", sl)], 8, 512, None, hs[sl], [("h", sl)], tmpA[sl], D, stage=stage)

        def p1_front2(i):
            sl = i % 2
            tsl = slice(i * 512, (i + 1) * 512)
            rope_tables(d_posf[:, tsl], 512, c2s[sl], s2s[sl], rts[sl], [("cs", sl)])

        def p1_back(i):
            sl = i % 2
            tsl = slice(i * 512, (i + 1) * 512)
            banks = []
            for mc in range(2):
                b = psb()
                banks.append(b)
                for c in range(8):
                    mm(psum[:, b, :], wkvr[:, c, mc * 128:(mc + 1) * 128], hs[sl][:, c, :], c == 0, c == 7,
                       reads=["wkvr", ("h", sl)], writes=[ptok(b)])
            bA, bB = psb(), psb()
            for (bb, off) in ((bA, 256), (bB, 289)):
                for c in range(8):
                    mm(psum[P97, bb, :], wkvr[:, c, off:off + 33], hs[sl][:, c, :], c == 0, c == 7,
                       reads=["wkvr", ("h", sl)], writes=[ptok(bb)])
            rms_norm([psum[:, b, :] for b in banks], [ptok(b) for b in banks], 2, 512, None, kvn[:, :, tsl], [("kvn", i)],
                     tmpB[sl], KVLR)
            tt("dve", rtmp[sl][0][P97, :], psum[P97, bA, :], c2s[sl][P97, :], ALU.mult, reads=[ptok(bA), ("cs", sl)], writes=[("rtmp0", sl)])
            tt("dve", rtmp[sl][1][P97, :], psum[P97, bB, :], s2s[sl][P97, :], ALU.mult, reads=[ptok(bB), ("cs", sl)], writes=[("rtmp1", sl)])
            tt("dve", krope[P97, tsl], rtmp[sl][0][P97, :], rtmp[sl][1][P97, :], ALU.add, reads=[("rtmp0", sl), ("rtmp1", sl)],
               writes=[("krope", i)])

        NT1 = SEQ // 512
        for i in (0, 1):
            p1_front(i, "a")
            p1_front(i, "b")
        p1_front2(0)
        for i in range(NT1):
            p1_back(i)
            if i + 2 < NT1:
                p1_front(i + 2, "a")
            if i + 1 < NT1:
                p1_front2(i + 1)
            if i + 2 < NT1:
                p1_front(i + 2, "b")
        KROPE_ALL = [("krope", i) for i in range(16)]
        KVN_ALL = [("kvn", i) for i in range(16)]
        memset("dve", krope[64:65, :], 1.0, KROPE_ALL)
        dump("kvn", kvn, KVN_ALL)
        dump("krope", krope[P97, :], KROPE_ALL)
        S.barrier()
        A.release(m1)

        qn = A.alloc((3, E), BF16)
        c2o = A.alloc((E,), F32)
        s2o = A.alloc((E,), F32)
        m2 = A.mark()
        wq = A.alloc((8, QLR), BF16)
        for b_ in range(3):
            dma("pool", wq[:, :, b_ * 128:(b_ + 1) * 128], d_wq[b_].rearrange("p (c n) -> p c n", c=8), writes=[("wq", b_)])
        xts = [A.alloc((8, TH), F32) for _ in range(2)]
        hs = [A.alloc((8, TH), BF16) for _ in range(2)]
        tmpA = [dict(sq=A.alloc((8, TH), BF16), sd=A.alloc((TH,), F32), rs=A.alloc((TH,), F32), tok=("tA", i)) for i in range(2)]
        tmpB = [dict(sq=A.alloc((3, T), BF16), sd=A.alloc((T,), F32), rs=A.alloc((T,), F32), tok=("tB", i)) for i in range(2)]
        rts = [dict(posi=A.alloc((512,), I32), f=[A.alloc((512,), F32) for _ in range(3)], tok=("rt", 0))] * 2
        def p2_front(t, stage):
            sl = t % 2
            e0 = t * T
            esl = slice(e0, e0 + T)
            if stage == "a":
                dma("sp", xts[sl][:, :, 0:T], d_xo[t].rearrange("p (c n) -> p c n", c=8), writes=[("xt", sl)])
            rms_norm(xts[sl], [("xt", sl)], 8, T, "g1", hs[sl], [("h", sl)], tmpA[sl], D, stage=stage)

        def p2_front2(t):
            sl = t % 2
            esl = slice(t * T, (t + 1) * T)
            rope_tables(d_poso[:, esl], T, c2o[:, esl], s2o[:, esl], rts[sl], [("cso", t)])

        def p2_back(t):
            sl = t % 2
            e0 = t * T
            esl = slice(e0, e0 + T)
            banks = []
            for mc in range(3):
                b = psb()
                banks.append(b)
                for c in range(8):
                    mm(psum[:, b, 0:T], wq[:, c, mc * 128:(mc + 1) * 128], hs[sl][:, c, 0:T], c == 0, c == 7,
                       reads=[("wq", mc), ("h", sl)], writes=[ptok(b)])
            rms_norm([psum[:, b, 0:T] for b in banks], [ptok(b) for b in banks], 3, T, "gq", qn[:, :, esl], [("qn", t)],
                     tmpB[sl], QLR)

        for t in (0, 1):
            p2_front(t, "a")
            p2_front(t, "b")
        p2_front2(0)
        for t in range(NT):
            p2_back(t)
            if t + 2 < NT:
                p2_front(t + 2, "a")
            if t + 1 < NT:
                p2_front2(t + 1)
            if t + 2 < NT:
                p2_front(t + 2, "b")
        QN_ALL = [("qn", t) for t in range(NT)]
        dump("qn", qn, QN_ALL)
        dump("c2o", c2o[P97, :], [("cso", t) for t in range(NT)])
        dump("s2o", s2o[P97, :], [("cso", t) for t in range(NT)])
        S.barrier()
        A.release(m2)

        m3 = A.mark()
        wqh = A.alloc((3, NH * 194), BF16)
        wk = A.alloc((2, NH * 128), BF16)
        wv = A.alloc((2, NH * 64), BF16)
        dma("pool", wqh, d_wqh.rearrange("(c p) n -> p c n", p=128), writes=["wqh"])
        dma("pool", wk, d_wk.rearrange("(c p) n -> p c n", p=128), writes=["wk"])
        dma("pool", wv, d_wv.rearrange("(c p) n -> p c n", p=128), writes=["wv"])
        for c in range(2):
            ts("dve", wk[:, c, :], wk[:, c, :], vcol("gkv", c), None, ALU.mult, None, reads=["wk", "vec"], writes=["wk"])
            ts("dve", wv[:, c, :], wv[:, c, :], vcol("gkv", c), None, ALU.mult, None, reads=["wv", "vec"], writes=["wv"])
        kaug = [A.alloc((SEQ,), BF16) for _ in range(2)]
        vbuf = [A.alloc((64, 65), BF16) for _ in range(2)]
        qaug = [A.alloc((T,), BF16) for _ in range(2)]
        pT = [A.alloc((2, T), BF16) for _ in range(3)]
        sqk = [A.alloc((512,), BF16) for _ in range(2)]
        kmx = [A.alloc((17,), F32) for _ in range(2)]
        qtmp = [[A.alloc((T,), F32) for _ in range(2)] for _ in range(2)]
        sqq = [A.alloc((T,), BF16) for _ in range(2)]
        rinv = [A.alloc((T,), F32) for _ in range(2)]
        rhi = [A.alloc((T,), BF16) for _ in range(2)]
        rlo = [A.alloc((T,), BF16) for _ in range(2)]
        sel = A.alloc((128,), BF16)
        memset("pool", sel, 0.0, ["sel0"])
        S.add("pool", lambda e: e.memset(sel[64:65, :], 1.0), reads=["sel0"], writes=["sel"])
        for i in range(2):
            memset("pool", rhi[i], 0.0, [("rhi", i)])
            memset("pool", rlo[i], 0.0, [("rlo", i)])
        osb = [A.alloc((T,), F32) for _ in range(2)]
        for i in range(2):
            memset("pool", vbuf[i][:, :, 64:65], 1.0, [("vones", i)])
        PS.avail = [6, 7]
        PS.pos = 0
        def stage_weights():
            dma("pool", s_wa.rearrange("p (c n) -> p c n", c=8), d_wa.rearrange("(c p) n -> p c n", p=128), reads=[("qc", 0)], writes=["s_wa"])
            for c4 in range(0, 8, 4):
                dma("pool", s_wg.rearrange("p (c n) -> p c n", c=8)[:, c4:c4 + 4, :],
                    d_wg.rearrange("(c p) n -> p c n", p=128)[:, c4:c4 + 4, :], writes=[("s_wg", c4)])
            dma("pool", s_wmo.rearrange("p (h n) -> p h n", h=NH), d_wmo.rearrange("(h p) n -> p h n", p=64), writes=["s_wmo"])
            dma("pool", s_wco.rearrange("p (c n) -> p c n", c=4), d_wco.rearrange("(c p) n -> p c n", p=128), writes=["s_wco"])
            dma("pool", s_wout.rearrange("p (c n) -> p c n", c=8), d_wout.rearrange("(c p) n -> p c n", p=128), writes=["s_wout"])
            for p in range(NPAIR):
                dma("pool", s_wup[p].rearrange("p (c n) -> p c n", c=8), d_wup[p].rearrange("(c p) n -> p c n", p=128),
                    writes=[("s_wup", p)])
            for k4 in range(0, NPAIR, 2):
                k5 = min(k4 + 2, NPAIR)
                dma("pool", s_wdn[:, k4 * D:k5 * D].rearrange("p (k n) -> p k n", n=D),
                    d_wdn[k4 * 128:k5 * 128, :].rearrange("(k p) n -> p k n", p=128), writes=[("s_wdn", k4)])


        def gen_kv_pieces(h):
            kb = h % 2
            pieces = []

            def k_tile(i):
                tsl = slice(i * 512, (i + 1) * 512)
                b = psb()
                for c in range(2):
                    mm(psum[:, b, :], wk[:, c, h * 128:(h + 1) * 128], kvn[:, c, tsl], c == 0, c == 1,
                       reads=["wk", ("kvn", i)], writes=[ptok(b)])
                cp("dve", kaug[kb][0:64, tsl], psum[0:64, b, :], reads=[ptok(b)], writes=[("kaug", kb, i)])

            def k_rope_rows():
                dma("sp", kaug[kb][P97, :], krope[P97, :], reads=KROPE_ALL, writes=[("kaugr", kb)])

            def k_sq(i):
                tsl = slice(i * 512, (i + 1) * 512)
                s_ = i % 2
                tt("dve", sqk[s_][0:97, :], kaug[kb][0:97, tsl], kaug[kb][0:97, tsl], ALU.mult,
                   reads=[("kaug", kb, i), ("kaugr", kb)], writes=[("sqk", s_)])

            def k_max(i):
                s_ = i % 2
                b = psb()
                mm(psum[0:97, b, :], onesb[0:97, 0:97], sqk[s_][0:97, :], True, True, reads=[("sqk", s_), "onesb"], writes=[ptok(b)])
                S.add("dve", lambda e, o=kmx[kb][64:65, i:i + 1], a=psum[64:65, b, :]: e.reduce_max(out=o, in_=a, axis=AX.X),
                      reads=[ptok(b)], writes=[("kmxp", kb)])
                if i == 15:
                    S.add("dve", lambda e, o=kmx[kb][64:65, 16:17], a=kmx[kb][64:65, 0:16]: e.reduce_max(out=o, in_=a, axis=AX.X),
                          reads=[("kmxp", kb)], writes=[("kmx", kb)])

            vbank = {}

            def v_half(g, half):
                if half == 0:
                    vbank[g] = psb()
                b = vbank[g]
                for jj in range(half * 4, half * 4 + 4):
                    j = g * 8 + jj
                    for c in range(2):
                        mm(psum[:, b, jj * 64:(jj + 1) * 64], kvn[:, c, j * 128:(j + 1) * 128], wv[:, c, h * 64:(h + 1) * 64],
                           c == 0, c == 1, reads=["wv", ("kvn", j // 4)], writes=[ptok(b)])
                if half == 1:
                    cp("dve", vbuf[kb][:, g * 8:(g + 1) * 8, 0:64], psum[:, b, :].rearrange("p (a b) -> p a b", a=8),
                       reads=[ptok(b)], writes=[("v", kb, g)])

            if h < 2:
                pieces.append(k_rope_rows)
            for i in range(16):
                pieces.append(lambda i=i: k_tile(i))
            pieces.append(lambda: k_sq(0))
            for i in range(16):
                if i + 1 < 16:
                    pieces.append(lambda i=i: (k_sq(i + 1), k_max(i)))
                else:
                    pieces.append(lambda i=i: k_max(i))
            for g in range(8):
                pieces.append(lambda g=g: v_half(g, 0))
                pieces.append(lambda g=g: v_half(g, 1))
            return pieces

        def gen_q_a(h, t):
            u = h * NT + t
            qs = u % 2
            esl = slice(t * T, (t + 1) * T)
            b1, b2 = psb(), psb()
            for c in range(3):
                mm(psum[0:97, b1, 0:T], wqh[:, c, h * 194:h * 194 + 97], qn[:, c, esl], c == 0, c == 2,
                   reads=["wqh", ("qn", t)], writes=[ptok(b1)])
            for c in range(3):
                mm(psum[0:97, b2, 0:T], wqh[:, c, h * 194 + 97:h * 194 + 194], qn[:, c, esl], c == 0, c == 2,
                   reads=["wqh", ("qn", t)], writes=[ptok(b2)])
            cp("dve", qaug[qs][0:64, :], psum[0:64, b1, 0:T], reads=[ptok(b1)], writes=[("qa", qs)])
            tt("dve", qtmp[qs][0][P97, :], psum[P97, b1, 0:T], c2o[P97, esl], ALU.mult, reads=[ptok(b1), ("cso", t)], writes=[("qt0", qs)])
            tt("dve", qtmp[qs][1][P97, :], psum[P97, b2, 0:T], s2o[P97, esl], ALU.mult, reads=[ptok(b2), ("cso", t)], writes=[("qt1", qs)])
            tt("dve", qaug[qs][P97, :], qtmp[qs][0][P97, :], qtmp[qs][1][P97, :], ALU.add, reads=[("qt0", qs), ("qt1", qs)],
               writes=[("qb", qs)])
            tt("dve", sqq[qs][0:97, :], qaug[qs][0:97, :], qaug[qs][0:97, :], ALU.mult, reads=[("qa", qs), ("qb", qs)], writes=[("sqq", qs)])

        def gen_q_b(h, t):
            u = h * NT + t
            qs = u % 2
            kb = h % 2
            b3 = psb()
            mm(psum[0:97, b3, 0:T], onesb[0:97, 0:97], sqq[qs][0:97, :], True, True, reads=[("sqq", qs), "onesb"], writes=[ptok(b3)])
            ts("dve", qaug[qs][64:65, :], psum[64:65, b3, 0:T], kmx[kb][64:65, 16:17], -0.5, ALU.add, ALU.mult,
               reads=[ptok(b3), ("kmx", kb), ("sqq", qs)], writes=[("qc", qs)])

        n_heads_run = NH if (debug is None or "heads" not in debug) else debug["heads"]
        units = [(h, t) for h in range(n_heads_run) for t in range(NT)]
        NG = 32
        groups = [(ui, g) for ui in range(len(units)) for g in range(NG)]

        def s_mm(gi):
            ui, g = groups[gi]
            h, t = units[ui]
            qs, kb = ui % 2, h % 2
            sb = (gi % 2) * 2
            for jj in range(2):
                j = g * 2 + jj
                mm(psum[:, sb + jj, 0:T], kaug[kb][0:97, j * 128:(j + 1) * 128], qaug[qs][0:97, :], True, True,
                   reads=[("kaug", kb, j // 4), ("kaugr", kb), ("qa", qs), ("qb", qs), ("qc", qs)], writes=[ptok(sb + jj)],
                   fuse_eng="act")

        def exp_g(gi):
            sb = (gi % 2) * 2
            ps_ = gi % 3
            act(pT[ps_][:, :, :], psum[:, sb:sb + 2, 0:T], AF.Exp, reads=[ptok(sb), ptok(sb + 1)], writes=[("pT", ps_)],
                scale=SCALE)

        def pv_mm(gi):
            ui, g = groups[gi]
            h, t = units[ui]
            kb = h % 2
            ob = 4 + (ui % 2)
            ps_ = gi % 3
            for jj in range(2):
                j = g * 2 + jj
                mm(psum[0:65, ob, 0:T], vbuf[kb][:, j, 0:65], pT[ps_][:, jj, :], j == 0, j == 63,
                   reads=[("v", kb, j // 8), ("vones", kb), ("pT", ps_)], writes=[ptok(ob)], fuse_eng="act")

        def epilogue_a(ui):
            qs = ui % 2
            ob = 4 + (ui % 2)
            recip(rinv[qs][64:65, :], psum[64:65, ob, 0:T], reads=[ptok(ob)], writes=[("rinv", qs)])
            cp("dve", rhi[qs][64:65, :], rinv[qs][64:65, :], reads=[("rinv", qs)], writes=[("rhi", qs)])
            tt("dve", rlo[qs][64:65, :], rinv[qs][64:65, :], rhi[qs][64:65, :], ALU.subtract, reads=[("rinv", qs), ("rhi", qs)],
               writes=[("rlo", qs)])
            cp("dve", osb[qs][0:64, :], psum[0:64, ob, 0:T], reads=[ptok(ob)], writes=[("osb", qs)])

        def epilogue_b(ui):
            h, t = units[ui]
            qs = ui % 2
            esl = slice(t * T, (t + 1) * T)
            bb = psb()
            mm(psum[:, bb, 0:T], sel[:, :], rhi[qs][:, :], True, False, reads=[("rhi", qs), "sel"], writes=[ptok(bb)])
            mm(psum[:, bb, 0:T], sel[:, :], rlo[qs][:, :], False, True, reads=[("rlo", qs), "sel"], writes=[ptok(bb)])
            tt("dve", OT[0:64, h, esl], osb[qs][0:64, :], psum[0:64, bb, 0:T], ALU.mult, reads=[("osb", qs), ptok(bb)],
               writes=[("OT", h, t)])

        side = {}

        def at(gi, f):
            side.setdefault(gi, []).append(f)

        PS.avail = list(range(8))
        PS.pos = 0
        gen_q_a(0, 0)
        for pc in gen_kv_pieces(0):
            pc()
        gen_q_b(0, 0)
        PS.avail = [6, 7]
        PS.pos = 0
        stage_weights()
        for h in range(n_heads_run):
            base = h * NT * NG
            if h + 1 < n_heads_run:
                for k, pc in enumerate(gen_kv_pieces(h + 1)):
                    at(base + 3 * k + 2, pc)
        for ui in range(len(units)):
            base = ui * NG
            if ui + 1 < len(units):
                nh, nt_ = units[ui + 1]
                at(base + 8, lambda nh=nh, nt_=nt_: gen_q_a(nh, nt_))
                at(base + 14, lambda nh=nh, nt_=nt_: gen_q_b(nh, nt_))
            at(base + NG - 1, lambda ui=ui: epilogue_a(ui))
            if ui + 1 < len(units):
                at(base + NG + 4, lambda ui=ui: epilogue_b(ui))
        s_mm(0)
        s_mm(1)
        for gi in range(len(groups)):
            exp_g(gi)
            if gi + 2 < len(groups):
                s_mm(gi + 2)
            pv_mm(gi)
            for f in side.get(gi, ()):
                f()
        epilogue_b(len(units) - 1)
        OT_ALL = [("OT", h, t) for h in range(NH) for t in range(NT)]
        dump("OT", OT[0:64, :, :], OT_ALL)
        dump("kaug0", kaug[0][0:97, :], [])
        S.barrier()
        A.release(m_ot)
        PS.avail = list(range(8))
        PS.pos = 0

        s_all = A.alloc((4, E), BF16)
        m4 = A.mark()
        wa = A.alloc((8, 1024), BF16)
        diag = A.alloc((4, CK, 128), BF16)
        dma("pool", wa, s_wa.rearrange("p (c n) -> p c n", c=8), writes=["wa"])
        for m in range(4):
            o = VOFF["cw"] + m * CK
            tt("dve", diag[:, m, :, :], ident[:, :].unsqueeze(1).broadcast_to([128, CK, 128]),
               vec[:, o:o + CK].unsqueeze(2).broadcast_to([128, CK, 128]), ALU.mult,
               reads=["vec", "ident"], writes=[("diag", m)])
        xts = [A.alloc((8, TH), F32) for _ in range(2)]
        hs = [A.alloc((8, TH), BF16) for _ in range(2)]
        tmpA = [dict(sq=A.alloc((8, TH), BF16), sd=A.alloc((TH,), F32), rs=A.alloc((TH,), F32), tok=("tA", i)) for i in range(2)]
        ubuf = [A.alloc((4, TH), BF16) for _ in range(2)]
        tgs = [A.alloc((TH,), F32) for _ in range(2)]
        vhs = [A.alloc((TH,), F32) for _ in range(2)]
        vb = A.alloc((4, T), F32)
        vbb = A.alloc((4, T), BF16)
        sqv = A.alloc((4, T), BF16)
        mean = A.alloc((T,), F32)
        m2t = A.alloc((T,), F32)
        var = A.alloc((T,), F32)
        sdv = A.alloc((T,), F32)
        rsv = A.alloc((T,), F32)
        tcs = [A.alloc((T,), F32)] * 2
        tns = [A.alloc((T,), F32) for _ in range(2)]
        ths = [A.alloc((T,), F32) for _ in range(2)]
        zhs = [A.alloc((T,), F32) for _ in range(2)]

        def p4a_A(t):
            sl = t % 2
            e0 = t * T
            dma("sp", xts[sl], d_xh[t].rearrange("p (c n) -> p c n", c=8), writes=[("xt", sl)])
            rms_norm(xts[sl], [("xt", sl)], 8, TH, "g1", hs[sl], [("h", sl)], tmpA[sl], D)

        def p4a_B(t):
            sl = t % 2
            for m in range(4):
                s2_ = m % 2
                bv, bg = psb(), psb()
                for (bb, off) in ((bv, 0), (bg, 512)):
                    for c in range(8):
                        mm(psum[:, bb, 0:TH], wa[:, c, off + m * 128:off + (m + 1) * 128], hs[sl][:, c, :], c == 0, c == 7,
                           reads=["wa", ("h", sl)], writes=[ptok(bb)], fuse_eng="act")
                act(tgs[s2_], psum[:, bg, 0:TH], AF.Tanh, reads=[ptok(bg)], writes=[("tg", s2_)], scale=0.5)
                act(vhs[s2_], psum[:, bv, 0:TH], AF.Copy, reads=[ptok(bv)], writes=[("vh", s2_)], scale=0.5)
                stt("dve", ubuf[sl][:, m, :], tgs[s2_], 1.0, vhs[s2_], ALU.add, ALU.mult, reads=[("tg", s2_), ("vh", s2_)],
                    writes=[("u", sl, m)])

        def p4a_back(t):
            sl = t % 2
            e0 = t * T
            esl = slice(e0, e0 + T)
            for m in range(4):
                if m == 0 and t + 2 < NT:
                    p4a_A(t + 2)
                bc = psb()
                for j in range(CK):
                    mm(psum[:, bc, 0:T], diag[:, m, j, :], ubuf[sl][:, m, j:j + T], j == 0, j == CK - 1,
                       reads=[("diag", m), ("u", sl, m)], writes=[ptok(bc)], fuse_eng="act")
                act(vb[:, m, :], psum[:, bc, 0:T], AF.Identity, reads=[ptok(bc), "vec"], writes=[("vb", m)], bias=vcol("dwb", m))
                cp("dve", vbb[:, m, :], vb[:, m, :], reads=[("vb", m)], writes=[("vbb", m)])
                act(sqv[:, m, :], vb[:, m, :], AF.Square, reads=[("vb", m)], writes=[("sqv", m)])
            bm, bq = psb(), psb()
            for m in range(4):
                mm(psum[:, bm, 0:T], onesb[:, :], vbb[:, m, :], m == 0, m == 3, reads=[("vbb", m), "onesb"], writes=[ptok(bm)])
            for m in range(4):
                mm(psum[:, bq, 0:T], onesb[:, :], sqv[:, m, :], m == 0, m == 3, reads=[("sqv", m), "onesb"], writes=[ptok(bq)])
            act(mean, psum[:, bm, 0:T], AF.Copy, reads=[ptok(bm)], writes=["mean"], scale=1.0 / CW)
            tt("dve", m2t, mean, mean, ALU.mult, reads=["mean"], writes=["m2t"])
            stt("dve", var, psum[:, bq, 0:T], 1.0 / CW, m2t, ALU.mult, ALU.subtract, reads=[ptok(bq), "m2t"], writes=["var"])
            ts("dve", var, var, 0.0, None, ALU.max, None, reads=["var"], writes=["var"])
            act(sdv, var, AF.Ln, reads=["var", "eps"], writes=["sdv"], bias=epst[:, 0:1])
            act(rsv, sdv, AF.Exp, reads=["sdv"], writes=["rsv"], scale=-0.5)
            for m in range(4):
                s2_ = m % 2
                tt("dve", tcs[s2_], vb[:, m, :], mean, ALU.subtract, reads=[("vb", m), "mean"], writes=[("tc", 0)])
                tt("dve", tns[s2_], tcs[s2_], rsv, ALU.mult, reads=[("tc", 0), "rsv"], writes=[("tn", s2_)])
                act(ths[s2_], tns[s2_], AF.Tanh, reads=[("tn", s2_), "vec"], writes=[("th", s2_)], bias=vcol("lnbh", m),
                    scale=vcol("lngh", m))
                ts("dve", zhs[s2_], tns[s2_], vcol("lngh", m), vcol("lnbh", m), ALU.mult, ALU.add, reads=[("tn", s2_), "vec"],
                   writes=[("zh", s2_)])
                stt("dve", s_all[:, m, esl], ths[s2_], 1.0, zhs[s2_], ALU.add, ALU.mult, reads=[("th", s2_), ("zh", s2_)],
                    writes=[("s", t, m)])

        p4a_A(0)
        p4a_A(1)
        p4a_B(0)
        for t in range(NT):
            p4a_back(t)
            if t + 1 < NT:
                p4a_B(t + 1)
        dump("s_all", s_all, [("s", t, m) for t in range(NT) for m in range(4)])
        S.barrier()
        A.release(m4)

        mrg = A.alloc_top((8, E), BF16)
        wg = A.alloc((8, 2048), BF16)
        wmo = A.alloc((NH, D), BF16)
        wco = A.alloc((4, D), BF16)
        dma("pool", wg, s_wg.rearrange("p (c n) -> p c n", c=8), writes=["wg"])
        dma("pool", wmo[0:64, :, :], s_wmo.rearrange("p (h n) -> p h n", h=NH), writes=["wmo"])
        dma("pool", wco, s_wco.rearrange("p (c n) -> p c n", c=4), writes=["wco"])
        xts = [A.alloc((8, T), F32) for _ in range(2)]
        hs = [A.alloc((8, T), BF16) for _ in range(2)]
        tmpA = [dict(sq=A.alloc((8, T), BF16), sd=A.alloc((T,), F32), rs=A.alloc((T,), F32), tok=("tA", 0))] * 2
        t1s = [A.alloc((T,), F32) for _ in range(2)]
        t2s = [A.alloc((T,), F32) for _ in range(2)]
        ymh = [A.alloc((T,), F32) for _ in range(2)]
        ycs = [A.alloc((T,), F32) for _ in range(2)]
        aas = [A.alloc((T,), F32) for _ in range(2)]
        bbs = [A.alloc((T,), F32) for _ in range(2)]

        def p4b_norm(t):
            sl = t % 2
            e0 = t * T
            dma("sp", xts[sl], d_xo[t].rearrange("p (c n) -> p c n", c=8), writes=[("xt", sl)])
            rms_norm(xts[sl], [("xt", sl)], 8, T, "g1", hs[sl], [("h", sl)], tmpA[sl], D)

        p4b_norm(0)
        for t in range(NT):
            sl = t % 2
            e0 = t * T
            esl = slice(e0, e0 + T)
            for mc in range(8):
                if mc == 0 and t + 1 < NT:
                    p4b_norm(t + 1)
                s2_ = mc % 2
                b1, b2, b3, b4 = psb(), psb(), psb(), psb()
                for (bb, off) in ((b1, 0), (b2, 1024)):
                    for c in range(8):
                        mm(psum[:, bb, 0:T], wg[:, c, off + mc * 128:off + (mc + 1) * 128], hs[sl][:, c, :], c == 0, c == 7,
                           reads=["wg", ("h", sl)], writes=[ptok(bb)], fuse_eng="act")
                for hh in range(NH):
                    mm(psum[:, b3, 0:T], wmo[0:64, hh, mc * 128:(mc + 1) * 128], OT[0:64, hh, esl], hh == 0, hh == NH - 1,
                       reads=["wmo"], writes=[ptok(b3)], fuse_eng="act")
                for m in range(4):
                    mm(psum[:, b4, 0:T], wco[:, m, mc * 128:(mc + 1) * 128], s_all[:, m, esl], m == 0, m == 3,
                       reads=["wco"], writes=[ptok(b4)], fuse_eng="act")
                act(t1s[s2_], psum[:, b1, 0:T], AF.Tanh, reads=[ptok(b1)], writes=[("t1", s2_)], scale=0.5)
                act(t2s[s2_], psum[:, b2, 0:T], AF.Tanh, reads=[ptok(b2)], writes=[("t2", s2_)], scale=0.5)
                act(ymh[s2_], psum[:, b3, 0:T], AF.Copy, reads=[ptok(b3)], writes=[("ymh", s2_)], scale=0.5)
                act(ycs[s2_], psum[:, b4, 0:T], AF.Identity, reads=[ptok(b4), "vec"], writes=[("ycs", s2_)], bias=vcol("bco", mc))
                stt("dve", aas[s2_], t1s[s2_], 1.0, ycs[s2_], ALU.add, ALU.mult, reads=[("t1", s2_), ("ycs", s2_)],
                    writes=[("aa", s2_)])
                stt("dve", bbs[s2_], t2s[s2_], 1.0, ymh[s2_], ALU.add, ALU.mult, reads=[("t2", s2_), ("ymh", s2_)],
                    writes=[("bb", s2_)])
                stt("dve", mrg[:, mc, esl], aas[s2_], 0.5, bbs[s2_], ALU.mult, ALU.add, reads=[("aa", s2_), ("bb", s2_)],
                    writes=[("mrg", t, mc)])
        dump("merged", mrg, [("mrg", t, mc) for t in range(NT) for mc in range(8)])
        S.barrier()
        A.release(m_c0)

        x1 = A.alloc((8, E), F32)
        h2 = A.alloc((8, E), BF16)
        m4c = A.mark()
        maskr = A.alloc((E,), F32)
        dma("sp", maskr, d_mask.partition_broadcast(128), writes=["mask"])
        wout = A.alloc((8, D), BF16)
        dma("pool", wout, s_wout.rearrange("p (c n) -> p c n", c=8), writes=["wout"])
        xts = [A.alloc((8, T), F32) for _ in range(2)]
        tmpA = [dict(sq=A.alloc((8, T), BF16), sd=A.alloc((T,), F32), rs=A.alloc((T,), F32), tok=("tA", i)) for i in range(2)]

        def p4c_mm(t):
            sl = t % 2
            e0 = t * T
            esl = slice(e0, e0 + T)
            dma("sp", xts[sl], d_xo[t].rearrange("p (c n) -> p c n", c=8), writes=[("xt", sl)])
            for mc2 in range(8):
                b = psb()
                for mc in range(8):
                    mm(psum[:, b, 0:T], wout[:, mc, mc2 * 128:(mc2 + 1) * 128], mrg[:, mc, esl], mc == 0, mc == 7,
                       reads=["wout"], writes=[ptok(b)], fuse_eng="dve")
                tt("dve", x1[:, mc2, esl], xts[sl][:, mc2, :], psum[:, b, 0:T], ALU.add, reads=[("xt", sl), ptok(b)],
                   writes=[("x1", t)])

        def p4c_norm(t):
            sl = t % 2
            esl = slice(t * T, (t + 1) * T)
            rms_norm(x1[:, :, esl], [("x1", t)], 8, T, "g2", h2[:, :, esl], [("h2", t)], tmpA[sl], D,
                     mask_ap=maskr[:, esl], mask_reads=["mask"])

        for t in range(NT):
            p4c_mm(t)
            if t > 0:
                p4c_norm(t - 1)
        p4c_norm(NT - 1)
        dump("x1", x1, [("x1", t) for t in range(NT)])
        dump("h2", h2, [("h2", t) for t in range(NT)])
        S.barrier()
        A.release(m4c)
        A.release_top()

        m5 = A.mark()
        GROUPS = [(0, 6), (6, 12), (12, 17), (17, 22)]
        grp_of = {}
        for gi, (p0, p1) in enumerate(GROUPS):
            for p in range(p0, p1):
                grp_of[p] = (gi, p0, p1)
        actb = A.alloc((6, OWN), BF16)
        wdn = A.alloc((6, D), BF16)
        wups = [A.alloc((8, 256), BF16) for _ in range(3)]
        gbufs = [A.alloc((E,), F32) for _ in range(2)]
        ubufs = [A.alloc((E,), F32) for _ in range(2)]
        gc = A.alloc((OWN,), F32)
        uc = A.alloc((OWN,), F32)
        thf = A.alloc((OWN,), F32)
        wdn_v = d_wdn.rearrange("(k p) n -> p k n", p=128)

        def load_wup(p):
            if p < NPAIR:
                dma("pool", wups[p % 3], s_wup[p].rearrange("p (c n) -> p c n", c=8), writes=[("wup", p % 3)])

        def load_wdn(gi):
            p0, p1 = GROUPS[gi]
            dma("pool", wdn[:, 0:p1 - p0, :], s_wdn[:, p0 * D:p1 * D].rearrange("p (k n) -> p k n", n=D), writes=["wdn"])

        def ffn_a(p, tiles):
            ws, bs = p % 3, p % 2
            for t in tiles:
                esl = slice(t * T, (t + 1) * T)
                bg, bu = psb(), psb()
                for (bb, off) in ((bg, 0), (bu, 128)):
                    for c in range(8):
                        mm(psum[:, bb, 0:T], wups[ws][:, c, off:off + 128], h2[:, c, esl], c == 0, c == 7,
                           reads=[("wup", ws)], writes=[ptok(bb)], fuse_eng="act")
                act(gbufs[bs][:, esl], psum[:, bg, 0:T], AF.Copy, reads=[ptok(bg)], writes=[("gbuf", bs, t)])
                act(ubufs[bs][:, esl], psum[:, bu, 0:T], AF.Copy, reads=[ptok(bu)], writes=[("ubuf", bs, t)])

        def ffn_b(p):
            bs = p % 2
            for (src, dst, q, rn, wn) in ((gbufs[bs], gc, p, "gbuf", "gc"), (ubufs[bs], uc, NPAIR + p, "ubuf", "uc")):
                rd = [(rn, bs, t) for t in range(NT)]
                act(dst, src[:, 0:OWN], AF.Identity, reads=rd + ["vec"], writes=[wn], bias=vcol("fb", q), scale=vcol("fw", q))
                stt("dve", dst, src[:, 1:OWN + 1], vcol("fw", 44 + q), dst, ALU.mult, ALU.add, reads=rd + [wn, "vec"], writes=[wn])
                stt("dve", dst, src[:, 2:OWN + 2], vcol("fw", 88 + q), dst, ALU.mult, ALU.add, reads=rd + [wn, "vec"], writes=[wn])

        def ffn_c(p):
            gi, p0, p1 = grp_of[p]
            kk = p - p0
            act(thf, gc, AF.Tanh, reads=["gc"], writes=["thf"], scale=0.5)
            stt("dve", thf, thf, 1.0, gc, ALU.add, ALU.mult, reads=["thf", "gc"], writes=["thf"])
            stt("dve", actb[:, kk, :], thf, 0.5, uc, ALU.mult, ALU.mult, reads=["thf", "uc"], writes=[("actb", kk)])
            if p == p1 - 1:
                npg = p1 - p0
                for i in range(4):
                    for mc2 in range(8):
                        b = psb()
                        for k2 in range(npg):
                            mm(psum[:, b, :], wdn[:, k2, mc2 * 128:(mc2 + 1) * 128], actb[:, k2, i * 512:(i + 1) * 512],
                               k2 == 0, k2 == npg - 1, reads=["wdn", ("actb", k2)], writes=[ptok(b)], fuse_eng="dve")
                        xs = x1[:, mc2, 1 + i * 512:1 + (i + 1) * 512]
                        tt("dve", xs, xs, psum[:, b, :], ALU.add, reads=[ptok(b), ("x2", i)], writes=[("x2", i)])
                if gi + 1 < len(GROUPS):
                    load_wdn(gi + 1)

        load_wup(0)
        load_wup(1)
        load_wdn(0)
        load_wup(2)
        ffn_a(0, range(0, 3))
        ffn_a(0, range(3, NT))
        ffn_b(0)
        for p in range(1, NPAIR):
            load_wup(p + 2)
            ffn_a(p, range(0, 4))
            ffn_c(p - 1)
            ffn_a(p, range(4, NT))
            ffn_b(p)
        ffn_c(NPAIR - 1)
        dump("x2", x1, [("x2", i) for i in range(4)])
        S.barrier()
        A.release(m5)
        otile = [A.alloc((8, 512), F32) for _ in range(2)]
        tmpF = [dict(sq=A.alloc((8, 512), BF16), sd=A.alloc((512,), F32), rs=A.alloc((512,), F32), tok=("tF", i)) for i in range(2)]
        out_v = d_out.rearrange("(c p) t -> p c t", p=128)
        for i in range(4):
            sl = i % 2
            rms_norm(x1[:, :, 1 + i * 512:1 + (i + 1) * 512], [("x2", i)], 8, 512, "gf", otile[sl], [("ot", sl)], tmpF[sl], D)
            out_dmas.append(dma("sp", out_v[:, :, i * 512:(i + 1) * 512], otile[sl], reads=[("ot", sl)]))
        S.emit(out_dma_ops=out_dmas)
        print("arena peak bytes/partition:", A.peak, "ops:", {e: len(S.ops[e]) for e in ENGS})
    return nc, dbg_outs


def _prep_shared(inp):
    f = np.float32
    w_in = np.asarray(inp["w_in"][0], f)
    sh = {}
    a_in = w_in[:, 0:1024]
    qc = w_in[:, 1024:1408]
    kvc = w_in[:, 1408:1664]
    rope = w_in[:, 1664:1696]
    gates = w_in[:, 1696:3744]
    z1 = np.zeros((D, 1), f)
    rope_sw = np.concatenate([rope[:, 16:32], rope[:, 0:16]], axis=1)
    sh["wkvr"] = np.ascontiguousarray(np.concatenate([kvc, z1, rope, z1, rope_sw], axis=1))
    def blocked(w, cols):
        kc = w.shape[0] // 128
        out = np.empty((len(cols), 128, kc * 128), f)
        for i, c0 in enumerate(cols):
            out[i] = w[:, c0:c0 + 128].reshape(kc, 128, 128).transpose(1, 0, 2).reshape(128, kc * 128)
        return out

    sh["wq"] = blocked(qc, [0, 128, 256])
    sh["wa"] = np.ascontiguousarray(a_in)
    sh["wg"] = np.ascontiguousarray(gates)
    w_uq = np.asarray(inp["w_uq"][0], f).reshape(QLR, NH, 96)
    zq = np.zeros((QLR, NH, 1), f)
    qrope = w_uq[:, :, 64:96]
    qrope_sw = np.concatenate([qrope[:, :, 16:32], qrope[:, :, 0:16]], axis=2)
    z64 = np.zeros((QLR, NH, 64), f)
    sh["wqh"] = np.ascontiguousarray(
        np.concatenate([w_uq[:, :, 0:64], zq, qrope, z64, zq, qrope_sw], axis=2).reshape(QLR, NH * 194))
    w_ukv = np.asarray(inp["w_ukv"][0], f).reshape(KVLR, NH, 128)
    sh["wk"] = np.ascontiguousarray(
        np.concatenate([w_ukv[:, :, 0:64], np.zeros((KVLR, NH, 64), f)], axis=2).reshape(KVLR, NH * 128))
    sh["wv"] = np.ascontiguousarray(w_ukv[:, :, 64:128].reshape(KVLR, NH * 64))
    sh["wco"] = np.ascontiguousarray(np.asarray(inp["w_conv_out"][0], f))
    sh["wmo"] = np.ascontiguousarray(np.asarray(inp["w_mla_out"][0], f))
    sh["wout"] = np.ascontiguousarray(np.asarray(inp["w_out"][0], f))
    w_up = np.asarray(inp["w_ffn_up"][0], f)
    gpart = w_up[:, 0:DFF].reshape(D, NPAIR, 128)
    upart = w_up[:, DFF:2 * DFF].reshape(D, NPAIR, 128)
    sh["wup"] = np.ascontiguousarray(np.concatenate([gpart, upart], axis=2).transpose(1, 0, 2))
    sh["wdn"] = np.ascontiguousarray(np.asarray(inp["w_ffn_down"][0], f))
    vec = np.zeros((128, NV), f)

    def put(name, arr, nch):
        vec[:, VOFF[name]:VOFF[name] + nch] = np.asarray(arr, f).reshape(nch, 128).T

    put("g1", inp["norm1_g"][0], 8)
    put("gq", inp["q_norm_g"][0], 3)
    put("gkv", inp["kv_norm_g"][0], 2)
    put("dwb", inp["conv_dw_b"][0], 4)
    put("lng", inp["conv_ln_g"][0], 4)
    put("lnb", inp["conv_ln_b"][0], 4)
    put("bco", inp["b_conv_out"][0], 8)
    put("g2", inp["norm2_g"][0], 8)
    put("gf", inp["norm_f_g"], 8)
    put("fw", inp["ffn_dw_w"][0], 132)
    put("fb", inp["ffn_dw_b"][0], 44)
    cwm = np.asarray(inp["conv_dw_w"][0], f).reshape(CK, 4, 128).transpose(1, 0, 2)
    vec[:, VOFF["cw"]:VOFF["cw"] + 4 * CK] = cwm.reshape(4 * CK, 128).T
    invf = (1.0 / (np.float32(10000.0) ** (np.arange(0, 32, 2, dtype=np.float32) / np.float32(32)))).astype(f)
    vec[65:81, VOFF["invf"]] = -invf
    vec[81:97, VOFF["invf"]] = invf
    vec[65:81, VOFF["sgn"]] = -1.0
    vec[81:97, VOFF["sgn"]] = 1.0
    sh["vec"] = vec
    sh["ident"] = np.eye(128, dtype=f)
    return sh


def _prep_core(inp, core, xT):
    b, c = divmod(core, 4)
    f = np.float32
    m = {}
    m["xf"] = xT[b]
    lo = c * OWN - 16
    xo = np.zeros((D, XO), f)
    s0, s1 = max(lo, 0), min(lo + XO, SEQ)
    xo[:, s0 - lo:s1 - lo] = xT[b + NB][:, s0:s1]
    xo3 = xo.reshape(8, 128, XO)
    m["xo"] = np.ascontiguousarray(
        np.stack([xo3[:, :, 15 + t * T:15 + (t + 1) * T].transpose(1, 0, 2).reshape(128, 8 * T) for t in range(NT)]))
    m["xh"] = np.ascontiguousarray(
        np.stack([xo3[:, :, t * T:t * T + TH].transpose(1, 0, 2).reshape(128, 8 * TH) for t in range(NT)]))
    pos = np.asarray(inp["positions"], np.int32)
    m["posf"] = np.ascontiguousarray(pos[b:b + 1, :])
    elo = c * OWN - 1
    poso = np.zeros((1, E), np.int32)
    mask = np.zeros((1, E), f)
    s0, s1 = max(elo, 0), min(elo + E, SEQ)
    poso[0, s0 - elo:s1 - elo] = pos[b, s0:s1]
    mask[0, s0 - elo:s1 - elo] = 1.0
    m["poso"] = poso
    m["mask"] = mask
    return m


def _prep_x(x):
    xTp = [np.ascontiguousarray(x[b].T) for b in range(NB)]
    return [np.ascontiguousarray(xt.reshape(8, 128, SEQ // 512, 512).transpose(2, 1, 0, 3).reshape(SEQ // 512, 128, 8 * 512))
            for xt in xTp] + xTp


_NC_CACHE = {}


def kernel(**inputs):
    x = np.asarray(inputs["x"], np.float32)
    xT = _prep_x(x)
    sh = _prep_shared(inputs)
    in_maps = []
    for core in range(8):
        m = dict(sh)
        m.update(_prep_core(inputs, core, xT))
        in_maps.append(m)
    if "nc" not in _NC_CACHE:
        _NC_CACHE["nc"] = build_nc()[0]
    nc = _NC_CACHE["nc"]
    res = run_bass_kernel_spmd(nc, in_maps, core_ids=list(range(8)))
    out = np.empty((NB, SEQ, D), np.float32)
    for core in range(8):
        b, c = divmod(core, 4)
        out[b, c * OWN:(c + 1) * OWN, :] = np.asarray(res.results[core]["out"]).T
    return out
```

```python
import contextlib
import math
import numpy as np
import concourse.bass as bass
import concourse.mybir as mybir
from concourse.bass_utils import run_bass_kernel_spmd

F32 = mybir.dt.float32
BF16 = mybir.dt.bfloat16
I32 = mybir.dt.int32
AF = mybir.ActivationFunctionType
ALU = mybir.AluOpType
AX = mybir.AxisListType

D = 1024
SEQ = 8192
NB = 2
OWN = 2048
E = 2050
XO = 2080
T = 410
NT = 5
TH = T + 30
CW = 512
CK = 31
NH = 8
QLR = 384
KVLR = 256
DFF = 2816
NPAIR = 22
EPS = 1e-6
SCALE = 96.0 ** -0.5
TWO_PI = 2.0 * math.pi
CW1 = 6.28125
CW2 = TWO_PI - CW1

ENGS = ("pe", "act", "dve", "pool", "sp")
N_DMA_SEMS = 8


class Op:
    __slots__ = ("eng", "fn", "deps", "is_dma", "signal", "count_after", "dma_sem", "dma_val", "dma_prev", "fuse_eng")

    def __init__(self, eng, fn, is_dma):
        self.eng = eng
        self.fn = fn
        self.is_dma = is_dma
        self.deps = []
        self.signal = False
        self.count_after = 0
        self.dma_sem = None
        self.dma_val = 0
        self.dma_prev = None
        self.fuse_eng = None


class Sched:
    def __init__(self, nc):
        self.nc = nc
        self.ops = {e: [] for e in ENGS}
        self.last_writer = {}
        self.readers = {}
        self.dma_ops = {e: [] for e in ENGS}
        self.pending_barrier = {e: None for e in ENGS}

    def add(self, eng, fn, reads=(), writes=(), dma=False, fuse_eng=None):
        op = Op(eng, fn, dma)
        op.fuse_eng = fuse_eng
        deps = []
        for r in reads:
            w = self.last_writer.get(r)
            if w is not None:
                deps.append((w, "raw"))
        for w_ in writes:
            w = self.last_writer.get(w_)
            if w is not None:
                deps.append((w, "waw"))
            for rd in self.readers.get(w_, ()):
                deps.append((rd, "war"))
        pb = self.pending_barrier[eng]
        if pb is not None:
            deps.extend((d, "raw") for d in pb)
            self.pending_barrier[eng] = None
        seen = set()
        for d, kind in deps:
            if d is op or id(d) in seen:
                continue
            if (not d.is_dma) and (not dma) and d.eng == eng:
                if eng == "pe" or kind != "raw":
                    continue
            seen.add(id(d))
            op.deps.append(d)
        for r in reads:
            self.readers.setdefault(r, []).append(op)
        for w_ in writes:
            self.last_writer[w_] = op
            self.readers[w_] = []
        if dma:
            lst = self.dma_ops[eng]
            k = len(lst)
            op.dma_sem = k % N_DMA_SEMS
            op.dma_val = 16 * (k // N_DMA_SEMS + 1)
            if k >= N_DMA_SEMS:
                op.dma_prev = lst[k - N_DMA_SEMS]
            lst.append(op)
        self.ops[eng].append(op)
        return op

    def barrier(self):
        deps = []
        for e in ENGS:
            comp = [o for o in self.ops[e] if not o.is_dma]
            if comp:
                deps.append(comp[-1])
            deps.extend(self.dma_ops[e][-N_DMA_SEMS:])
        for e in ENGS:
            cur = self.pending_barrier[e]
            self.pending_barrier[e] = deps if cur is None else (cur + deps)
        self.last_writer = {}
        self.readers = {}

    def emit(self, out_dma_ops=()):
        nc = self.nc
        for e in ENGS:
            for op in self.ops[e]:
                for d in op.deps:
                    if not d.is_dma:
                        d.signal = True
        for e in ENGS:
            c = 0
            for op in self.ops[e]:
                if (not op.is_dma) and op.signal:
                    c += 1
                op.count_after = c
        with contextlib.ExitStack() as st:
            csem = {e: st.enter_context(nc.semaphore("c_" + e)) for e in ("pe", "act", "dve", "pool")}
            dsem = {}
            for e in ENGS:
                if self.dma_ops[e]:
                    dsem[e] = [st.enter_context(nc.semaphore("d_%s%d" % (e, i))) for i in range(N_DMA_SEMS)]
            block = st.enter_context(nc.Block())

            def run(e, eng):
                waited = {}

                def wait(sem, val):
                    key = id(sem)
                    if waited.get(key, 0) >= val:
                        return
                    waited[key] = val
                    eng.wait_ge(sem, val)

                for op in self.ops[e]:
                    need = {}
                    for d in op.deps:
                        if d.is_dma:
                            k = ("d", d.eng, d.dma_sem)
                            v = d.dma_val
                        else:
                            k = ("c", d.eng)
                            v = d.count_after
                        if v > need.get(k, 0):
                            need[k] = v
                    if op.is_dma and op.dma_prev is not None:
                        k = ("d", e, op.dma_sem)
                        if op.dma_prev.dma_val > need.get(k, 0):
                            need[k] = op.dma_prev.dma_val
                    fused = None
                    for k, v in need.items():
                        if k[0] == "d":
                            wait(dsem[k[1]][k[2]], v)
                        elif op.fuse_eng is not None and k[1] == op.fuse_eng:
                            fused = (csem[k[1]], v)
                        else:
                            wait(csem[k[1]], v)
                    ins = op.fn(eng)
                    if fused is not None and waited.get(id(fused[0]), 0) < fused[1]:
                        waited[id(fused[0])] = fused[1]
                        ins._wait_ge(fused[0], fused[1])
                    if op.is_dma:
                        ins.then_inc(dsem[e][op.dma_sem], 16)
                    elif op.signal:
                        ins.then_inc(csem[e], 1)
                if e == "sp":
                    for d in out_dma_ops:
                        wait(dsem[d.eng][d.dma_sem], d.dma_val)

            @block.tensor
            def _(eng):
                run("pe", eng)

            @block.scalar
            def _(eng):
                run("act", eng)

            @block.vector
            def _(eng):
                run("dve", eng)

            @block.gpsimd
            def _(eng):
                run("pool", eng)

            @block.sync
            def _(eng):
                run("sp", eng)


class Arena:
    def __init__(self, ap, nelem):
        self.ap = ap
        self.nbytes = nelem * 2
        self.top = 0
        self.hi = self.nbytes
        self.peak = 0

    def alloc(self, shape, dtype):
        n = 1
        for s in shape:
            n *= s
        size = {BF16: 2, F32: 4, I32: 4}[dtype]
        nb = (n * size + 31) // 32 * 32
        off = self.top
        self.top += nb
        self.peak = max(self.peak, self.top)
        self.peak = max(self.peak, self.top + (self.nbytes - self.hi))
        assert self.top <= self.hi, ("SBUF arena overflow", self.top, self.hi)
        v = self.ap[:, off // 2:(off + n * size) // 2]
        if dtype != BF16:
            v = v.bitcast(dtype)
        if len(shape) == 2:
            v = v.rearrange("p (a b) -> p a b", a=shape[0])
        elif len(shape) == 3:
            v = v.rearrange("p (a b c) -> p a b c", a=shape[0], b=shape[1])
        return v

    def alloc_top(self, shape, dtype):
        n = 1
        for x in shape:
            n *= x
        size = {BF16: 2, F32: 4, I32: 4}[dtype]
        nb = (n * size + 31) // 32 * 32
        self.hi -= nb
        off = self.hi
        assert self.top <= self.hi, ("SBUF arena overflow (top)", self.top, self.hi)
        self.peak = max(self.peak, self.top + (self.nbytes - self.hi))
        v = self.ap[:, off // 2:(off + n * size) // 2]
        if dtype != BF16:
            v = v.bitcast(dtype)
        if len(shape) == 2:
            v = v.rearrange("p (a b) -> p a b", a=shape[0])
        return v

    def release_top(self):
        self.hi = self.nbytes

    def mark(self):
        return self.top

    def release(self, m):
        self.top = m


VOFF = {}
_o = 0
for _name, _n in (("g1", 8), ("gq", 3), ("gkv", 2), ("dwb", 4), ("lng", 4), ("lnb", 4), ("lngh", 4), ("lnbh", 4),
                  ("bco", 8), ("g2", 8), ("gf", 8), ("fw", 132), ("fb", 44), ("cw", 124), ("invf", 1), ("sgn", 1)):
    VOFF[_name] = _o
    _o += _n
NV = _o


def build_nc(debug=None):
    nc = bass.Bass("TRN2", target_bir_lowering=False)
    dt_in = lambda name, shape, dt=F32: nc.dram_tensor(name, list(shape), dt, kind="ExternalInput").ap()
    d_xf = dt_in("xf", (SEQ // 512, 128, 8 * 512))
    d_xo = dt_in("xo", (NT, 128, 8 * T))
    d_xh = dt_in("xh", (NT, 128, 8 * TH))
    d_posf = dt_in("posf", (1, SEQ), I32)
    d_poso = dt_in("poso", (1, E), I32)
    d_mask = dt_in("mask", (1, E))
    d_vec = dt_in("vec", (128, NV))
    d_ident = dt_in("ident", (128, 128))
    d_wkvr = dt_in("wkvr", (D, 322))
    d_wq = dt_in("wq", (3, 128, 8 * 128))
    d_wa = dt_in("wa", (D, 1024))
    d_wg = dt_in("wg", (D, 2048))
    d_wqh = dt_in("wqh", (QLR, NH * 194))
    d_wk = dt_in("wk", (KVLR, NH * 128))
    d_wv = dt_in("wv", (KVLR, NH * 64))
    d_wco = dt_in("wco", (CW, D))
    d_wmo = dt_in("wmo", (NH * 64, D))
    d_wout = dt_in("wout", (D, D))
    d_wup = dt_in("wup", (NPAIR, D, 256))
    d_wdn = dt_in("wdn", (DFF, D))
    d_out = nc.dram_tensor("out", [D, OWN], F32, kind="ExternalOutput").ap()
    s_wa = nc.dram_tensor("s_wa", [128, 8 * 1024], BF16).ap()
    s_wg = nc.dram_tensor("s_wg", [128, 8 * 2048], BF16).ap()
    s_wco = nc.dram_tensor("s_wco", [128, 4 * D], BF16).ap()
    s_wmo = nc.dram_tensor("s_wmo", [64, NH * D], BF16).ap()
    s_wout = nc.dram_tensor("s_wout", [128, 8 * D], BF16).ap()
    s_wup = nc.dram_tensor("s_wup", [NPAIR, 128, 8 * 256], BF16).ap()
    s_wdn = nc.dram_tensor("s_wdn", [128, NPAIR * D], BF16).ap()
    dbg_outs = {}

    with contextlib.ExitStack() as st:
        ARENA_ELEMS = 106000
        arena_t = st.enter_context(nc.sbuf_tensor("arena", [128, ARENA_ELEMS], BF16))
        psum = st.enter_context(nc.psum_tensor("psum", [128, 8, 512], F32))
        A = Arena(arena_t, ARENA_ELEMS)
        S = Sched(nc)
        out_dmas = []

        class PS:
            avail = list(range(8))
            pos = 0

        def psb():
            b = PS.avail[PS.pos % len(PS.avail)]
            PS.pos += 1
            return b

        def ptok(b):
            return ("ps", b)

        def dma(eng, out, in_, reads=(), writes=()):
            return S.add(eng, lambda e: e.dma_start(out=out, in_=in_), reads=reads, writes=writes, dma=True)

        def mm(out, lhsT, rhs, start, stop, reads, writes, fuse_eng=None):
            S.add("pe", lambda e: e.matmul(out, lhsT=lhsT, rhs=rhs, start=start, stop=stop), reads=reads, writes=writes,
                  fuse_eng=fuse_eng)

        def act(out, in_, func, reads, writes, bias=None, scale=1.0):
            if bias is None:
                S.add("act", lambda e: e.activation(out=out, in_=in_, func=func, scale=scale), reads=reads, writes=writes)
            else:
                S.add("act", lambda e: e.activation(out=out, in_=in_, func=func, bias=bias, scale=scale), reads=reads, writes=writes)

        def tt(eng, out, in0, in1, op, reads, writes):
            S.add(eng, lambda e: e.tensor_tensor(out=out, in0=in0, in1=in1, op=op), reads=reads, writes=writes)

        def ts(eng, out, in0, s1, s2, op0, op1, reads, writes):
            if op1 is None:
                S.add(eng, lambda e: e.tensor_scalar(out=out, in0=in0, scalar1=s1, scalar2=None, op0=op0), reads=reads, writes=writes)
            else:
                S.add(eng, lambda e: e.tensor_scalar(out=out, in0=in0, scalar1=s1, scalar2=s2, op0=op0, op1=op1), reads=reads, writes=writes)

        def stt(eng, out, in0, scalar, in1, op0, op1, reads, writes):
            S.add(eng, lambda e: e.scalar_tensor_tensor(out=out, in0=in0, scalar=scalar, in1=in1, op0=op0, op1=op1),
                  reads=reads, writes=writes)

        def cp(eng, out, in_, reads, writes):
            S.add(eng, lambda e: e.tensor_copy(out=out, in_=in_), reads=reads, writes=writes)

        def recip(out, in_, reads, writes):
            S.add("dve", lambda e: e.reciprocal(out=out, in_=in_), reads=reads, writes=writes)

        def memset(eng, ap, val, writes):
            S.add(eng, lambda e: e.memset(ap, val), writes=writes)

        def dump(name, ap, reads):
            if debug is None or name not in debug:
                return
            shape = list(ap.shape)
            dten = nc.dram_tensor("dbg_" + name, shape, ap.dtype, kind="ExternalOutput").ap()
            dbg_outs[name] = dten
            out_dmas.append(dma("sp", dten, ap, reads=reads))

        vec = A.alloc((NV,), F32)
        ident = A.alloc((128,), F32)
        onesb = A.alloc((128,), BF16)
        onesf = A.alloc((64,), F32)
        epst = A.alloc((1,), F32)
        dma("sp", vec, d_vec, writes=["vec0"])
        dma("sp", ident, d_ident, writes=["ident"])
        memset("dve", onesb, 1.0, ["onesb"])
        memset("dve", onesf, 1.0, ["onesf"])
        memset("dve", epst, EPS, ["eps"])
        ts("dve", vec[:, VOFF["lngh"]:VOFF["lngh"] + 8], vec[:, VOFF["lng"]:VOFF["lng"] + 8], 0.5, None, ALU.mult, None,
           ["vec0"], ["vec"])
        m_c0 = A.mark()

        def vcol(name, i, lo=0, hi=128):
            o = VOFF[name] + i
            return vec[lo:hi, o:o + 1]

        def rms_norm(src, src_reads, nch, n, gname, dst, dst_writes, tmp, dfeat, mask_ap=None, mask_reads=(), stage="all"):
            sq, sd, rs = tmp["sq"], tmp["sd"], tmp["rs"]
            tk = tmp["tok"]
            if stage in ("all", "a"):
                _rms_a(src, src_reads, nch, n, sq, sd, rs, tk, dfeat, mask_ap, mask_reads)
            if stage in ("all", "b"):
                _rms_b(src, src_reads, nch, n, gname, dst, dst_writes, rs, tk)

        def _rms_a(src, src_reads, nch, n, sq, sd, rs, tk, dfeat, mask_ap, mask_reads):
            if isinstance(src, list):
                for c in range(nch):
                    act(sq[:, c, 0:n], src[c], AF.Square, reads=[src_reads[c]], writes=[(tk, "sq")])
            else:
                act(sq[:, 0:nch, 0:n], src[:, 0:nch, 0:n], AF.Square, reads=src_reads, writes=[(tk, "sq")])
            b = psb()
            for c in range(nch):
                mm(psum[:, b, 0:n], onesb[:, 0:128], sq[:, c, 0:n], c == 0, c == nch - 1,
                   reads=[(tk, "sq"), "onesb"], writes=[ptok(b)])
            act(sd[:, 0:n], psum[:, b, 0:n], AF.Ln, reads=[ptok(b), "eps"], writes=[(tk, "sd")], bias=epst[:, 0:1],
                scale=1.0 / dfeat)
            act(rs[:, 0:n], sd[:, 0:n], AF.Exp, reads=[(tk, "sd")], writes=[(tk, "rs")], scale=-0.5)
            if mask_ap is not None:
                tt("dve", rs[:, 0:n], rs[:, 0:n], mask_ap, ALU.mult, reads=[(tk, "rs")] + list(mask_reads), writes=[(tk, "rs")])

        def _rms_b(src, src_reads, nch, n, gname, dst, dst_writes, rs, tk):
            if gname is None and not isinstance(src, list):
                tt("dve", dst[:, 0:nch, 0:n], src[:, 0:nch, 0:n], rs[:, 0:n].unsqueeze(1).broadcast_to([128, nch, n]), ALU.mult,
                   reads=list(src_reads) + [(tk, "rs")], writes=dst_writes)
                return
            for c in range(nch):
                if isinstance(src, list):
                    s_c, r_c = src[c], [src_reads[c]]
                else:
                    s_c, r_c = src[:, c, 0:n], list(src_reads)
                if gname is None:
                    tt("dve", dst[:, c, 0:n], s_c, rs[:, 0:n], ALU.mult, reads=r_c + [(tk, "rs")], writes=dst_writes)
                else:
                    stt("dve", dst[:, c, 0:n], s_c, vcol(gname, c), rs[:, 0:n], ALU.mult, ALU.mult,
                        reads=r_c + [(tk, "rs"), "vec"], writes=dst_writes)

        def rope_tables(pos_ap, n, c2, s2, rt, writes):
            P = slice(64, 97)
            posi, f = rt["posi"], rt["f"]
            tk = rt["tok"]
            dma("sp", posi[P, 0:n], pos_ap.partition_broadcast(33), writes=[(tk, "posi")])
            cp("dve", f[0][P, 0:n], posi[P, 0:n], reads=[(tk, "posi")], writes=[(tk, 0)])
            ts("dve", f[1][P, 0:n], f[0][P, 0:n], vcol("invf", 0, 64, 97), None, ALU.mult, None, reads=[(tk, 0), "vec"], writes=[(tk, 1)])
            ts("dve", f[0][P, 0:n], f[1][P, 0:n], 1.0 / TWO_PI, None, ALU.mult, None, reads=[(tk, 1)], writes=[(tk, 0)])
            cp("dve", posi[P, 0:n], f[0][P, 0:n], reads=[(tk, 0)], writes=[(tk, "posi")])
            cp("dve", f[0][P, 0:n], posi[P, 0:n], reads=[(tk, "posi")], writes=[(tk, 0)])
            stt("dve", f[2][P, 0:n], f[0][P, 0:n], -CW1, f[1][P, 0:n], ALU.mult, ALU.add, reads=[(tk, 0), (tk, 1)], writes=[(tk, 2)])
            stt("dve", f[1][P, 0:n], f[0][P, 0:n], -CW2, f[2][P, 0:n], ALU.mult, ALU.add, reads=[(tk, 0), (tk, 2)], writes=[(tk, 1)])
            act(s2[P, 0:n], f[1][P, 0:n], AF.Sin, reads=[(tk, 1)], writes=writes)
            ts("dve", f[2][P, 0:n], f[1][P, 0:n], math.pi / 2, None, ALU.add, None, reads=[(tk, 1)], writes=[(tk, 2)])
            ts("dve", f[0][P, 0:n], f[2][P, 0:n], math.pi, -TWO_PI, ALU.is_gt, ALU.mult, reads=[(tk, 2)], writes=[(tk, 0)])
            tt("dve", f[2][P, 0:n], f[2][P, 0:n], f[0][P, 0:n], ALU.add, reads=[(tk, 2), (tk, 0)], writes=[(tk, 2)])
            act(c2[P, 0:n], f[2][P, 0:n], AF.Sin, reads=[(tk, 2)], writes=writes)

        OT = A.alloc((NH, E), BF16)
        m_ot = A.mark()
        kvn = A.alloc((2, SEQ), BF16)
        krope = A.alloc((SEQ,), BF16)
        P97 = slice(64, 97)

        m1 = A.mark()
        wkvr = A.alloc((8, 322), BF16)
        dma("pool", wkvr, d_wkvr.rearrange("(c p) n -> p c n", p=128), writes=["wkvr"])
        xts = [A.alloc((8, 512), F32) for _ in range(2)]
        hs = [A.alloc((8, 512), BF16) for _ in range(2)]
        tmpA = [dict(sq=A.alloc((8, 512), BF16), sd=A.alloc((512,), F32), rs=A.alloc((512,), F32), tok=("tA", i)) for i in range(2)]
        tmpB = [dict(sq=A.alloc((2, 512), BF16), sd=A.alloc((512,), F32), rs=A.alloc((512,), F32), tok=("tB", i)) for i in range(2)]
        rts = [dict(posi=A.alloc((512,), I32), f=[A.alloc((512,), F32) for _ in range(3)], tok=("rt", 0))] * 2
        c2s = [A.alloc((512,), F32) for _ in range(2)]
        s2s = [A.alloc((512,), F32) for _ in range(2)]
        rtmp = [[A.alloc((512,), F32) for _ in range(2)] for _ in range(2)]
        for c in range(8):
            ts("dve", wkvr[:, c, :], wkvr[:, c, :], vcol("g1", c), None, ALU.mult, None, reads=["wkvr", "vec"], writes=["wkvr"])

        def p1_front(i, stage):
            sl = i % 2
            tsl = slice(i * 512, (i + 1) * 512)
            if stage == "a":
                dma("sp", xts[sl], d_xf[i].rearrange("p (c n) -> p c n", c=8), writes=[("xt", sl)])
            rms_norm(xts[sl], [("xt", sl)], 8, 512, None, hs[sl], [("h", sl)], tmpA[sl], D, stage=stage)

        def p1_front2(i):
            sl = i % 2
            tsl = slice(i * 512, (i + 1) * 512)
            rope_tables(d_posf[:, tsl], 512, c2s[sl], s2s[sl], rts[sl], [("cs", sl)])

        def p1_back(i):
            sl = i % 2
            tsl = slice(i * 512, (i + 1) * 512)
            banks = []
            for mc in range(2):
                b = psb()
                banks.append(b)
                for c in range(8):
                    mm(psum[:, b, :], wkvr[:, c, mc * 128:(mc + 1) * 128], hs[sl][:, c, :], c == 0, c == 7,
                       reads=["wkvr", ("h", sl)], writes=[ptok(b)])
            bA, bB = psb(), psb()
            for (bb, off) in ((bA, 256), (bB, 289)):
                for c in range(8):
                    mm(psum[P97, bb, :], wkvr[:, c, off:off + 33], hs[sl][:, c, :], c == 0, c == 7,
                       reads=["wkvr", ("h", sl)], writes=[ptok(bb)])
            rms_norm([psum[:, b, :] for b in banks], [ptok(b) for b in banks], 2, 512, None, kvn[:, :, tsl], [("kvn", i)],
                     tmpB[sl], KVLR)
            tt("dve", rtmp[sl][0][P97, :], psum[P97, bA, :], c2s[sl][P97, :], ALU.mult, reads=[ptok(bA), ("cs", sl)], writes=[("rtmp0", sl)])
            tt("dve", rtmp[sl][1][P97, :], psum[P97, bB, :], s2s[sl][P97, :], ALU.mult, reads=[ptok(bB), ("cs", sl)], writes=[("rtmp1", sl)])
            tt("dve", krope[P97, tsl], rtmp[sl][0][P97, :], rtmp[sl][1][P97, :], ALU.add, reads=[("rtmp0", sl), ("rtmp1", sl)],
               writes=[("krope", i)])

        NT1 = SEQ // 512
        for i in (0, 1):
            p1_front(i, "a")
            p1_front(i, "b")
        p1_front2(0)
        for i in range(NT1):
            p1_back(i)
            if i + 2 < NT1:
                p1_front(i + 2, "a")
            if i + 1 < NT1:
                p1_front2(i + 1)
            if i + 2 < NT1:
                p1_front(i + 2, "b")
        KROPE_ALL = [("krope", i) for i in range(16)]
        KVN_ALL = [("kvn", i) for i in range(16)]
        memset("dve", krope[64:65, :], 1.0, KROPE_ALL)
        dump("kvn", kvn, KVN_ALL)
        dump("krope", krope[P97, :], KROPE_ALL)
        S.barrier()
        A.release(m1)

        qn = A.alloc((3, E), BF16)
        c2o = A.alloc((E,), F32)
        s2o = A.alloc((E,), F32)
        m2 = A.mark()
        wq = A.alloc((8, QLR), BF16)
        for b_ in range(3):
            dma("pool", wq[:, :, b_ * 128:(b_ + 1) * 128], d_wq[b_].rearrange("p (c n) -> p c n", c=8), writes=[("wq", b_)])
        xts = [A.alloc((8, TH), F32) for _ in range(2)]
        hs = [A.alloc((8, TH), BF16) for _ in range(2)]
        tmpA = [dict(sq=A.alloc((8, TH), BF16), sd=A.alloc((TH,), F32), rs=A.alloc((TH,), F32), tok=("tA", i)) for i in range(2)]
        tmpB = [dict(sq=A.alloc((3, T), BF16), sd=A.alloc((T,), F32), rs=A.alloc((T,), F32), tok=("tB", i)) for i in range(2)]
        rts = [dict(posi=A.alloc((512,), I32), f=[A.alloc((512,), F32) for _ in range(3)], tok=("rt", 0))] * 2
        def p2_front(t, stage):
            sl = t % 2
            e0 = t * T
            esl = slice(e0, e0 + T)
            if stage == "a":
                dma("sp", xts[sl][:, :, 0:T], d_xo[t].rearrange("p (c n) -> p c n", c=8), writes=[("xt", sl)])
            rms_norm(xts[sl], [("xt", sl)], 8, T, "g1", hs[sl], [("h", sl)], tmpA[sl], D, stage=stage)

        def p2_front2(t):
            sl = t % 2
            esl = slice(t * T, (t + 1) * T)
            rope_tables(d_poso[:, esl], T, c2o[:, esl], s2o[:, esl], rts[sl], [("cso", t)])

        def p2_back(t):
            sl = t % 2
            e0 = t * T
            esl = slice(e0, e0 + T)
            banks = []
            for mc in range(3):
                b = psb()
                banks.append(b)
                for c in range(8):
                    mm(psum[:, b, 0:T], wq[:, c, mc * 128:(mc + 1) * 128], hs[sl][:, c, 0:T], c == 0, c == 7,
                       reads=[("wq", mc), ("h", sl)], writes=[ptok(b)])
            rms_norm([psum[:, b, 0:T] for b in banks], [ptok(b) for b in banks], 3, T, "gq", qn[:, :, esl], [("qn", t)],
                     tmpB[sl], QLR)

        for t in (0, 1):
            p2_front(t, "a")
            p2_front(t, "b")
        p2_front2(0)
        for t in range(NT):
            p2_back(t)
            if t + 2 < NT:
                p2_front(t + 2, "a")
            if t + 1 < NT:
                p2_front2(t + 1)
            if t + 2 < NT:
                p2_front(t + 2, "b")
        QN_ALL = [("qn", t) for t in range(NT)]
        dump("qn", qn, QN_ALL)
        dump("c2o", c2o[P97, :], [("cso", t) for t in range(NT)])
        dump("s2o", s2o[P97, :], [("cso", t) for t in range(NT)])
        S.barrier()
        A.release(m2)

        m3 = A.mark()
        wqh = A.alloc((3, NH * 194), BF16)
        wk = A.alloc((2, NH * 128), BF16)
        wv = A.alloc((2, NH * 64), BF16)
        dma("pool", wqh, d_wqh.rearrange("(c p) n -> p c n", p=128), writes=["wqh"])
        dma("pool", wk, d_wk.rearrange("(c p) n -> p c n", p=128), writes=["wk"])
        dma("pool", wv, d_wv.rearrange("(c p) n -> p c n", p=128), writes=["wv"])
        for c in range(2):
            ts("dve", wk[:, c, :], wk[:, c, :], vcol("gkv", c), None, ALU.mult, None, reads=["wk", "vec"], writes=["wk"])
            ts("dve", wv[:, c, :], wv[:, c, :], vcol("gkv", c), None, ALU.mult, None, reads=["wv", "vec"], writes=["wv"])
        kaug = [A.alloc((SEQ,), BF16) for _ in range(2)]
        vbuf = [A.alloc((64, 65), BF16) for _ in range(2)]
        qaug = [A.alloc((T,), BF16) for _ in range(2)]
        pT = [A.alloc((2, T), BF16) for _ in range(3)]
        sqk = [A.alloc((512,), BF16) for _ in range(2)]
        kmx = [A.alloc((17,), F32) for _ in range(2)]
        qtmp = [[A.alloc((T,), F32) for _ in range(2)] for _ in range(2)]
        sqq = [A.alloc((T,), BF16) for _ in range(2)]
        rinv = [A.alloc((T,), F32) for _ in range(2)]
        rhi = [A.alloc((T,), BF16) for _ in range(2)]
        rlo = [A.alloc((T,), BF16) for _ in range(2)]
        sel = A.alloc((128,), BF16)
        memset("pool", sel, 0.0, ["sel0"])
        S.add("pool", lambda e: e.memset(sel[64:65, :], 1.0), reads=["sel0"], writes=["sel"])
        for i in range(2):
            memset("pool", rhi[i], 0.0, [("rhi", i)])
            memset("pool", rlo[i], 0.0, [("rlo", i)])
        osb = [A.alloc((T,), F32) for _ in range(2)]
        for i in range(2):
            memset("pool", vbuf[i][:, :, 64:65], 1.0, [("vones", i)])
        PS.avail = [6, 7]
        PS.pos = 0
        def stage_weights():
            dma("pool", s_wa.rearrange("p (c n) -> p c n", c=8), d_wa.rearrange("(c p) n -> p c n", p=128), reads=[("qc", 0)], writes=["s_wa"])
            for c4 in range(0, 8, 4):
                dma("pool", s_wg.rearrange("p (c n) -> p c n", c=8)[:, c4:c4 + 4, :],
                    d_wg.rearrange("(c p) n -> p c n", p=128)[:, c4:c4 + 4, :], writes=[("s_wg", c4)])
            dma("pool", s_wmo.rearrange("p (h n) -> p h n", h=NH), d_wmo.rearrange("(h p) n -> p h n", p=64), writes=["s_wmo"])
            dma("pool", s_wco.rearrange("p (c n) -> p c n", c=4), d_wco.rearrange("(c p) n -> p c n", p=128), writes=["s_wco"])
            dma("pool", s_wout.rearrange("p (c n) -> p c n", c=8), d_wout.rearrange("(c p) n -> p c n", p=128), writes=["s_wout"])
            for p in range(NPAIR):
                dma("pool", s_wup[p].rearrange("p (c n) -> p c n", c=8), d_wup[p].rearrange("(c p) n -> p c n", p=128),
                    writes=[("s_wup", p)])
            for k4 in range(0, NPAIR, 2):
                k5 = min(k4 + 2, NPAIR)
                dma("pool", s_wdn[:, k4 * D:k5 * D].rearrange("p (k n) -> p k n", n=D),
                    d_wdn[k4 * 128:k5 * 128, :].rearrange("(k p) n -> p k n", p=128), writes=[("s_wdn", k4)])


        def gen_kv_pieces(h):
            kb = h % 2
            pieces = []

            def k_tile(i):
                tsl = slice(i * 512, (i + 1) * 512)
                b = psb()
                for c in range(2):
                    mm(psum[:, b, :], wk[:, c, h * 128:(h + 1) * 128], kvn[:, c, tsl], c == 0, c == 1,
                       reads=["wk", ("kvn", i)], writes=[ptok(b)])
                cp("dve", kaug[kb][0:64, tsl], psum[0:64, b, :], reads=[ptok(b)], writes=[("kaug", kb, i)])

            def k_rope_rows():
                dma("sp", kaug[kb][P97, :], krope[P97, :], reads=KROPE_ALL, writes=[("kaugr", kb)])

            def k_sq(i):
                tsl = slice(i * 512, (i + 1) * 512)
                s_ = i % 2
                tt("dve", sqk[s_][0:97, :], kaug[kb][0:97, tsl], kaug[kb][0:97, tsl], ALU.mult,
                   reads=[("kaug", kb, i), ("kaugr", kb)], writes=[("sqk", s_)])

            def k_max(i):
                s_ = i % 2
                b = psb()
                mm(psum[0:97, b, :], onesb[0:97, 0:97], sqk[s_][0:97, :], True, True, reads=[("sqk", s_), "onesb"], writes=[ptok(b)])
                S.add("dve", lambda e, o=kmx[kb][64:65, i:i + 1], a=psum[64:65, b, :]: e.reduce_max(out=o, in_=a, axis=AX.X),
                      reads=[ptok(b)], writes=[("kmxp", kb)])
                if i == 15:
                    S.add("dve", lambda e, o=kmx[kb][64:65, 16:17], a=kmx[kb][64:65, 0:16]: e.reduce_max(out=o, in_=a, axis=AX.X),
                          reads=[("kmxp", kb)], writes=[("kmx", kb)])

            def v_group(g):
                b = psb()
                for jj in range(8):
                    j = g * 8 + jj
                    for c in range(2):
                        mm(psum[:, b, jj * 64:(jj + 1) * 64], kvn[:, c, j * 128:(j + 1) * 128], wv[:, c, h * 64:(h + 1) * 64],
                           c == 0, c == 1, reads=["wv", ("kvn", j // 4)], writes=[ptok(b)])
                cp("dve", vbuf[kb][:, g * 8:(g + 1) * 8, 0:64], psum[:, b, :].rearrange("p (a b) -> p a b", a=8),
                   reads=[ptok(b)], writes=[("v", kb, g)])

            if h < 2:
                pieces.append(k_rope_rows)
            for i in range(16):
                pieces.append(lambda i=i: k_tile(i))
            pieces.append(lambda: k_sq(0))
            for i in range(16):
                if i + 1 < 16:
                    pieces.append(lambda i=i: (k_sq(i + 1), k_max(i)))
                else:
                    pieces.append(lambda i=i: k_max(i))
            for g in range(8):
                pieces.append(lambda g=g: v_group(g))
            return pieces

        def gen_q_a(h, t):
            u = h * NT + t
            qs = u % 2
            esl = slice(t * T, (t + 1) * T)
            b1, b2 = psb(), psb()
            for c in range(3):
                mm(psum[0:97, b1, 0:T], wqh[:, c, h * 194:h * 194 + 97], qn[:, c, esl], c == 0, c == 2,
                   reads=["wqh", ("qn", t)], writes=[ptok(b1)])
            for c in range(3):
                mm(psum[0:97, b2, 0:T], wqh[:, c, h * 194 + 97:h * 194 + 194], qn[:, c, esl], c == 0, c == 2,
                   reads=["wqh", ("qn", t)], writes=[ptok(b2)])
            cp("dve", qaug[qs][0:64, :], psum[0:64, b1, 0:T], reads=[ptok(b1)], writes=[("qa", qs)])
            tt("dve", qtmp[qs][0][P97, :], psum[P97, b1, 0:T], c2o[P97, esl], ALU.mult, reads=[ptok(b1), ("cso", t)], writes=[("qt0", qs)])
            tt("dve", qtmp[qs][1][P97, :], psum[P97, b2, 0:T], s2o[P97, esl], ALU.mult, reads=[ptok(b2), ("cso", t)], writes=[("qt1", qs)])
            tt("dve", qaug[qs][P97, :], qtmp[qs][0][P97, :], qtmp[qs][1][P97, :], ALU.add, reads=[("qt0", qs), ("qt1", qs)],
               writes=[("qb", qs)])
            tt("dve", sqq[qs][0:97, :], qaug[qs][0:97, :], qaug[qs][0:97, :], ALU.mult, reads=[("qa", qs), ("qb", qs)], writes=[("sqq", qs)])

        def gen_q_b(h, t):
            u = h * NT + t
            qs = u % 2
            kb = h % 2
            b3 = psb()
            mm(psum[0:97, b3, 0:T], onesb[0:97, 0:97], sqq[qs][0:97, :], True, True, reads=[("sqq", qs), "onesb"], writes=[ptok(b3)])
            ts("dve", qaug[qs][64:65, :], psum[64:65, b3, 0:T], kmx[kb][64:65, 16:17], -0.5, ALU.add, ALU.mult,
               reads=[ptok(b3), ("kmx", kb), ("sqq", qs)], writes=[("qc", qs)])

        n_heads_run = NH if (debug is None or "heads" not in debug) else debug["heads"]
        units = [(h, t) for h in range(n_heads_run) for t in range(NT)]
        NG = 32
        groups = [(ui, g) for ui in range(len(units)) for g in range(NG)]

        def s_mm(gi):
            ui, g = groups[gi]
            h, t = units[ui]
            qs, kb = ui % 2, h % 2
            sb = (gi % 2) * 2
            for jj in range(2):
                j = g * 2 + jj
                mm(psum[:, sb + jj, 0:T], kaug[kb][0:97, j * 128:(j + 1) * 128], qaug[qs][0:97, :], True, True,
                   reads=[("kaug", kb, j // 4), ("kaugr", kb), ("qa", qs), ("qb", qs), ("qc", qs)], writes=[ptok(sb + jj)],
                   fuse_eng="act")

        def exp_g(gi):
            sb = (gi % 2) * 2
            ps_ = gi % 3
            act(pT[ps_][:, :, :], psum[:, sb:sb + 2, 0:T], AF.Exp, reads=[ptok(sb), ptok(sb + 1)], writes=[("pT", ps_)],
                scale=SCALE)

        def pv_mm(gi):
            ui, g = groups[gi]
            h, t = units[ui]
            kb = h % 2
            ob = 4 + (ui % 2)
            ps_ = gi % 3
            for jj in range(2):
                j = g * 2 + jj
                mm(psum[0:65, ob, 0:T], vbuf[kb][:, j, 0:65], pT[ps_][:, jj, :], j == 0, j == 63,
                   reads=[("v", kb, j // 8), ("vones", kb), ("pT", ps_)], writes=[ptok(ob)], fuse_eng="act")

        def epilogue_a(ui):
            qs = ui % 2
            ob = 4 + (ui % 2)
            recip(rinv[qs][64:65, :], psum[64:65, ob, 0:T], reads=[ptok(ob)], writes=[("rinv", qs)])
            cp("dve", rhi[qs][64:65, :], rinv[qs][64:65, :], reads=[("rinv", qs)], writes=[("rhi", qs)])
            tt("dve", rlo[qs][64:65, :], rinv[qs][64:65, :], rhi[qs][64:65, :], ALU.subtract, reads=[("rinv", qs), ("rhi", qs)],
               writes=[("rlo", qs)])
            cp("dve", osb[qs][0:64, :], psum[0:64, ob, 0:T], reads=[ptok(ob)], writes=[("osb", qs)])

        def epilogue_b(ui):
            h, t = units[ui]
            qs = ui % 2
            esl = slice(t * T, (t + 1) * T)
            bb = psb()
            mm(psum[:, bb, 0:T], sel[:, :], rhi[qs][:, :], True, False, reads=[("rhi", qs), "sel"], writes=[ptok(bb)])
            mm(psum[:, bb, 0:T], sel[:, :], rlo[qs][:, :], False, True, reads=[("rlo", qs), "sel"], writes=[ptok(bb)])
            tt("dve", OT[0:64, h, esl], osb[qs][0:64, :], psum[0:64, bb, 0:T], ALU.mult, reads=[("osb", qs), ptok(bb)],
               writes=[("OT", h, t)])

        side = {}

        def at(gi, f):
            side.setdefault(gi, []).append(f)

        PS.avail = list(range(8))
        PS.pos = 0
        gen_q_a(0, 0)
        for pc in gen_kv_pieces(0):
            pc()
        gen_q_b(0, 0)
        PS.avail = [6, 7]
        PS.pos = 0
        stage_weights()
        for h in range(n_heads_run):
            base = h * NT * NG
            if h + 1 < n_heads_run:
                pcs = gen_kv_pieces(h + 1)
                for k, pc in enumerate(pcs):
                    at(base + 2 + (k * 154) // (len(pcs) - 1), pc)
        for ui in range(len(units)):
            base = ui * NG
            if ui + 1 < len(units):
                nh, nt_ = units[ui + 1]
                at(base + 8, lambda nh=nh, nt_=nt_: gen_q_a(nh, nt_))
                at(base + 14, lambda nh=nh, nt_=nt_: gen_q_b(nh, nt_))
            at(base + NG - 1, lambda ui=ui: epilogue_a(ui))
            if ui + 1 < len(units):
                at(base + NG + 4, lambda ui=ui: epilogue_b(ui))
        s_mm(0)
        s_mm(1)
        for gi in range(len(groups)):
            exp_g(gi)
            if gi + 2 < len(groups):
                s_mm(gi + 2)
            pv_mm(gi)
            for f in side.get(gi, ()):
                f()
        epilogue_b(len(units) - 1)
        OT_ALL = [("OT", h, t) for h in range(NH) for t in range(NT)]
        dump("OT", OT[0:64, :, :], OT_ALL)
        dump("kaug0", kaug[0][0:97, :], [])
        S.barrier()
        A.release(m_ot)
        PS.avail = list(range(8))
        PS.pos = 0

        s_all = A.alloc((4, E), BF16)
        m4 = A.mark()
        wa = A.alloc((8, 1024), BF16)
        diag = A.alloc((4, CK, 128), BF16)
        dma("pool", wa, s_wa.rearrange("p (c n) -> p c n", c=8), writes=["wa"])
        for m in range(4):
            o = VOFF["cw"] + m * CK
            tt("dve", diag[:, m, :, :], ident[:, :].unsqueeze(1).broadcast_to([128, CK, 128]),
               vec[:, o:o + CK].unsqueeze(2).broadcast_to([128, CK, 128]), ALU.mult,
               reads=["vec", "ident"], writes=[("diag", m)])
        xts = [A.alloc((8, TH), F32) for _ in range(2)]
        hs = [A.alloc((8, TH), BF16) for _ in range(2)]
        tmpA = [dict(sq=A.alloc((8, TH), BF16), sd=A.alloc((TH,), F32), rs=A.alloc((TH,), F32), tok=("tA", i)) for i in range(2)]
        ubuf = [A.alloc((4, TH), BF16) for _ in range(2)]
        tgs = [A.alloc((TH,), F32) for _ in range(2)]
        vhs = [A.alloc((TH,), F32) for _ in range(2)]
        vb = A.alloc((4, T), F32)
        vbb = A.alloc((4, T), BF16)
        sqv = A.alloc((4, T), BF16)
        mean = A.alloc((T,), F32)
        m2t = A.alloc((T,), F32)
        var = A.alloc((T,), F32)
        sdv = A.alloc((T,), F32)
        rsv = A.alloc((T,), F32)
        tcs = [A.alloc((T,), F32)] * 2
        tns = [A.alloc((T,), F32) for _ in range(2)]
        ths = [A.alloc((T,), F32) for _ in range(2)]
        zhs = [A.alloc((T,), F32) for _ in range(2)]

        def p4a_A(t):
            sl = t % 2
            e0 = t * T
            dma("sp", xts[sl], d_xh[t].rearrange("p (c n) -> p c n", c=8), writes=[("xt", sl)])
            rms_norm(xts[sl], [("xt", sl)], 8, TH, "g1", hs[sl], [("h", sl)], tmpA[sl], D)

        def p4a_B(t):
            sl = t % 2
            for m in range(4):
                s2_ = m % 2
                bv, bg = psb(), psb()
                for (bb, off) in ((bv, 0), (bg, 512)):
                    for c in range(8):
                        mm(psum[:, bb, 0:TH], wa[:, c, off + m * 128:off + (m + 1) * 128], hs[sl][:, c, :], c == 0, c == 7,
                           reads=["wa", ("h", sl)], writes=[ptok(bb)], fuse_eng="act")
                act(tgs[s2_], psum[:, bg, 0:TH], AF.Tanh, reads=[ptok(bg)], writes=[("tg", s2_)], scale=0.5)
                act(vhs[s2_], psum[:, bv, 0:TH], AF.Copy, reads=[ptok(bv)], writes=[("vh", s2_)], scale=0.5)
                stt("dve", ubuf[sl][:, m, :], tgs[s2_], 1.0, vhs[s2_], ALU.add, ALU.mult, reads=[("tg", s2_), ("vh", s2_)],
                    writes=[("u", sl, m)])

        def p4a_back(t):
            sl = t % 2
            e0 = t * T
            esl = slice(e0, e0 + T)
            for m in range(4):
                if m == 0 and t + 2 < NT:
                    p4a_A(t + 2)
                bc = psb()
                for j in range(CK):
                    mm(psum[:, bc, 0:T], diag[:, m, j, :], ubuf[sl][:, m, j:j + T], j == 0, j == CK - 1,
                       reads=[("diag", m), ("u", sl, m)], writes=[ptok(bc)], fuse_eng="act")
                act(vb[:, m, :], psum[:, bc, 0:T], AF.Identity, reads=[ptok(bc), "vec"], writes=[("vb", m)], bias=vcol("dwb", m))
                cp("dve", vbb[:, m, :], vb[:, m, :], reads=[("vb", m)], writes=[("vbb", m)])
                act(sqv[:, m, :], vb[:, m, :], AF.Square, reads=[("vb", m)], writes=[("sqv", m)])
            bm, bq = psb(), psb()
            for m in range(4):
                mm(psum[:, bm, 0:T], onesb[:, :], vbb[:, m, :], m == 0, m == 3, reads=[("vbb", m), "onesb"], writes=[ptok(bm)])
            for m in range(4):
                mm(psum[:, bq, 0:T], onesb[:, :], sqv[:, m, :], m == 0, m == 3, reads=[("sqv", m), "onesb"], writes=[ptok(bq)])
            act(mean, psum[:, bm, 0:T], AF.Copy, reads=[ptok(bm)], writes=["mean"], scale=1.0 / CW)
            tt("dve", m2t, mean, mean, ALU.mult, reads=["mean"], writes=["m2t"])
            stt("dve", var, psum[:, bq, 0:T], 1.0 / CW, m2t, ALU.mult, ALU.subtract, reads=[ptok(bq), "m2t"], writes=["var"])
            ts("dve", var, var, 0.0, None, ALU.max, None, reads=["var"], writes=["var"])
            act(sdv, var, AF.Ln, reads=["var", "eps"], writes=["sdv"], bias=epst[:, 0:1])
            act(rsv, sdv, AF.Exp, reads=["sdv"], writes=["rsv"], scale=-0.5)
            for m in range(4):
                s2_ = m % 2
                tt("dve", tcs[s2_], vb[:, m, :], mean, ALU.subtract, reads=[("vb", m), "mean"], writes=[("tc", 0)])
                tt("dve", tns[s2_], tcs[s2_], rsv, ALU.mult, reads=[("tc", 0), "rsv"], writes=[("tn", s2_)])
                act(ths[s2_], tns[s2_], AF.Tanh, reads=[("tn", s2_), "vec"], writes=[("th", s2_)], bias=vcol("lnbh", m),
                    scale=vcol("lngh", m))
                ts("dve", zhs[s2_], tns[s2_], vcol("lngh", m), vcol("lnbh", m), ALU.mult, ALU.add, reads=[("tn", s2_), "vec"],
                   writes=[("zh", s2_)])
                stt("dve", s_all[:, m, esl], ths[s2_], 1.0, zhs[s2_], ALU.add, ALU.mult, reads=[("th", s2_), ("zh", s2_)],
                    writes=[("s", t, m)])

        p4a_A(0)
        p4a_A(1)
        p4a_B(0)
        for t in range(NT):
            p4a_back(t)
            if t + 1 < NT:
                p4a_B(t + 1)
        dump("s_all", s_all, [("s", t, m) for t in range(NT) for m in range(4)])
        S.barrier()
        A.release(m4)

        mrg = A.alloc_top((8, E), BF16)
        wg = A.alloc((8, 2048), BF16)
        wmo = A.alloc((NH, D), BF16)
        wco = A.alloc((4, D), BF16)
        dma("pool", wg, s_wg.rearrange("p (c n) -> p c n", c=8), writes=["wg"])
        dma("pool", wmo[0:64, :, :], s_wmo.rearrange("p (h n) -> p h n", h=NH), writes=["wmo"])
        dma("pool", wco, s_wco.rearrange("p (c n) -> p c n", c=4), writes=["wco"])
        xts = [A.alloc((8, T), F32) for _ in range(2)]
        hs = [A.alloc((8, T), BF16) for _ in range(2)]
        tmpA = [dict(sq=A.alloc((8, T), BF16), sd=A.alloc((T,), F32), rs=A.alloc((T,), F32), tok=("tA", 0))] * 2
        t1s = [A.alloc((T,), F32) for _ in range(2)]
        t2s = [A.alloc((T,), F32) for _ in range(2)]
        ymh = [A.alloc((T,), F32) for _ in range(2)]
        ycs = [A.alloc((T,), F32) for _ in range(2)]
        aas = [A.alloc((T,), F32) for _ in range(2)]
        bbs = [A.alloc((T,), F32) for _ in range(2)]

        def p4b_norm(t):
            sl = t % 2
            e0 = t * T
            dma("sp", xts[sl], d_xo[t].rearrange("p (c n) -> p c n", c=8), writes=[("xt", sl)])
            rms_norm(xts[sl], [("xt", sl)], 8, T, "g1", hs[sl], [("h", sl)], tmpA[sl], D)

        p4b_norm(0)
        for t in range(NT):
            sl = t % 2
            e0 = t * T
            esl = slice(e0, e0 + T)
            for mc in range(8):
                if mc == 0 and t + 1 < NT:
                    p4b_norm(t + 1)
                s2_ = mc % 2
                b1, b2, b3, b4 = psb(), psb(), psb(), psb()
                for (bb, off) in ((b1, 0), (b2, 1024)):
                    for c in range(8):
                        mm(psum[:, bb, 0:T], wg[:, c, off + mc * 128:off + (mc + 1) * 128], hs[sl][:, c, :], c == 0, c == 7,
                           reads=["wg", ("h", sl)], writes=[ptok(bb)], fuse_eng="act")
                for hh in range(NH):
                    mm(psum[:, b3, 0:T], wmo[0:64, hh, mc * 128:(mc + 1) * 128], OT[0:64, hh, esl], hh == 0, hh == NH - 1,
                       reads=["wmo"], writes=[ptok(b3)], fuse_eng="act")
                for m in range(4):
                    mm(psum[:, b4, 0:T], wco[:, m, mc * 128:(mc + 1) * 128], s_all[:, m, esl], m == 0, m == 3,
                       reads=["wco"], writes=[ptok(b4)], fuse_eng="act")
                act(t1s[s2_], psum[:, b1, 0:T], AF.Tanh, reads=[ptok(b1)], writes=[("t1", s2_)], scale=0.5)
                act(t2s[s2_], psum[:, b2, 0:T], AF.Tanh, reads=[ptok(b2)], writes=[("t2", s2_)], scale=0.5)
                act(ymh[s2_], psum[:, b3, 0:T], AF.Copy, reads=[ptok(b3)], writes=[("ymh", s2_)], scale=0.5)
                act(ycs[s2_], psum[:, b4, 0:T], AF.Identity, reads=[ptok(b4), "vec"], writes=[("ycs", s2_)], bias=vcol("bco", mc))
                stt("dve", aas[s2_], t1s[s2_], 1.0, ycs[s2_], ALU.add, ALU.mult, reads=[("t1", s2_), ("ycs", s2_)],
                    writes=[("aa", s2_)])
                stt("dve", bbs[s2_], t2s[s2_], 1.0, ymh[s2_], ALU.add, ALU.mult, reads=[("t2", s2_), ("ymh", s2_)],
                    writes=[("bb", s2_)])
                stt("dve", mrg[:, mc, esl], aas[s2_], 0.5, bbs[s2_], ALU.mult, ALU.add, reads=[("aa", s2_), ("bb", s2_)],
                    writes=[("mrg", t, mc)])
        dump("merged", mrg, [("mrg", t, mc) for t in range(NT) for mc in range(8)])
        S.barrier()
        A.release(m_c0)

        x1 = A.alloc((8, E), F32)
        h2 = A.alloc((8, E), BF16)
        m4c = A.mark()
        maskr = A.alloc((E,), F32)
        dma("sp", maskr, d_mask.partition_broadcast(128), writes=["mask"])
        wout = A.alloc((8, D), BF16)
        dma("pool", wout, s_wout.rearrange("p (c n) -> p c n", c=8), writes=["wout"])
        xts = [A.alloc((8, T), F32) for _ in range(2)]
        tmpA = [dict(sq=A.alloc((8, T), BF16), sd=A.alloc((T,), F32), rs=A.alloc((T,), F32), tok=("tA", i)) for i in range(2)]

        def p4c_mm(t):
            sl = t % 2
            e0 = t * T
            esl = slice(e0, e0 + T)
            dma("sp", xts[sl], d_xo[t].rearrange("p (c n) -> p c n", c=8), writes=[("xt", sl)])
            for mc2 in range(8):
                b = psb()
                for mc in range(8):
                    mm(psum[:, b, 0:T], wout[:, mc, mc2 * 128:(mc2 + 1) * 128], mrg[:, mc, esl], mc == 0, mc == 7,
                       reads=["wout"], writes=[ptok(b)], fuse_eng="dve")
                tt("dve", x1[:, mc2, esl], xts[sl][:, mc2, :], psum[:, b, 0:T], ALU.add, reads=[("xt", sl), ptok(b)],
                   writes=[("x1", t)])

        def p4c_norm(t):
            sl = t % 2
            esl = slice(t * T, (t + 1) * T)
            rms_norm(x1[:, :, esl], [("x1", t)], 8, T, "g2", h2[:, :, esl], [("h2", t)], tmpA[sl], D,
                     mask_ap=maskr[:, esl], mask_reads=["mask"])

        for t in range(NT):
            p4c_mm(t)
            if t > 0:
                p4c_norm(t - 1)
        p4c_norm(NT - 1)
        dump("x1", x1, [("x1", t) for t in range(NT)])
        dump("h2", h2, [("h2", t) for t in range(NT)])
        S.barrier()
        A.release(m4c)
        A.release_top()

        m5 = A.mark()
        GROUPS = [(0, 6), (6, 12), (12, 17), (17, 22)]
        grp_of = {}
        for gi, (p0, p1) in enumerate(GROUPS):
            for p in range(p0, p1):
                grp_of[p] = (gi, p0, p1)
        actb = A.alloc((6, OWN), BF16)
        wdn = A.alloc((6, D), BF16)
        wups = [A.alloc((8, 256), BF16) for _ in range(3)]
        gbufs = [A.alloc((E,), F32) for _ in range(2)]
        ubufs = [A.alloc((E,), F32) for _ in range(2)]
        gc = A.alloc((OWN,), F32)
        uc = A.alloc((OWN,), F32)
        thf = A.alloc((OWN,), F32)
        wdn_v = d_wdn.rearrange("(k p) n -> p k n", p=128)

        def load_wup(p):
            if p < NPAIR:
                dma("pool", wups[p % 3], s_wup[p].rearrange("p (c n) -> p c n", c=8), writes=[("wup", p % 3)])

        def load_wdn(gi):
            p0, p1 = GROUPS[gi]
            dma("pool", wdn[:, 0:p1 - p0, :], s_wdn[:, p0 * D:p1 * D].rearrange("p (k n) -> p k n", n=D), writes=["wdn"])

        def ffn_a(p, tiles):
            ws, bs = p % 3, p % 2
            for t in tiles:
                esl = slice(t * T, (t + 1) * T)
                bg, bu = psb(), psb()
                for (bb, off) in ((bg, 0), (bu, 128)):
                    for c in range(8):
                        mm(psum[:, bb, 0:T], wups[ws][:, c, off:off + 128], h2[:, c, esl], c == 0, c == 7,
                           reads=[("wup", ws)], writes=[ptok(bb)], fuse_eng="act")
                act(gbufs[bs][:, esl], psum[:, bg, 0:T], AF.Copy, reads=[ptok(bg)], writes=[("gbuf", bs, t)])
                act(ubufs[bs][:, esl], psum[:, bu, 0:T], AF.Copy, reads=[ptok(bu)], writes=[("ubuf", bs, t)])

        def ffn_b(p):
            bs = p % 2
            for (src, dst, q, rn, wn) in ((gbufs[bs], gc, p, "gbuf", "gc"), (ubufs[bs], uc, NPAIR + p, "ubuf", "uc")):
                rd = [(rn, bs, t) for t in range(NT)]
                act(dst, src[:, 0:OWN], AF.Identity, reads=rd + ["vec"], writes=[wn], bias=vcol("fb", q), scale=vcol("fw", q))
                stt("dve", dst, src[:, 1:OWN + 1], vcol("fw", 44 + q), dst, ALU.mult, ALU.add, reads=rd + [wn, "vec"], writes=[wn])
                stt("dve", dst, src[:, 2:OWN + 2], vcol("fw", 88 + q), dst, ALU.mult, ALU.add, reads=rd + [wn, "vec"], writes=[wn])

        def ffn_c(p):
            gi, p0, p1 = grp_of[p]
            kk = p - p0
            act(thf, gc, AF.Tanh, reads=["gc"], writes=["thf"], scale=0.5)
            stt("dve", thf, thf, 1.0, gc, ALU.add, ALU.mult, reads=["thf", "gc"], writes=["thf"])
            stt("dve", actb[:, kk, :], thf, 0.5, uc, ALU.mult, ALU.mult, reads=["thf", "uc"], writes=[("actb", kk)])
            if p == p1 - 1:
                npg = p1 - p0
                for i in range(4):
                    for mc2 in range(8):
                        b = psb()
                        for k2 in range(npg):
                            mm(psum[:, b, :], wdn[:, k2, mc2 * 128:(mc2 + 1) * 128], actb[:, k2, i * 512:(i + 1) * 512],
                               k2 == 0, k2 == npg - 1, reads=["wdn", ("actb", k2)], writes=[ptok(b)], fuse_eng="dve")
                        xs = x1[:, mc2, 1 + i * 512:1 + (i + 1) * 512]
                        tt("dve", xs, xs, psum[:, b, :], ALU.add, reads=[ptok(b), ("x2", i)], writes=[("x2", i)])
                if gi + 1 < len(GROUPS):
                    load_wdn(gi + 1)

        load_wup(0)
        load_wup(1)
        load_wdn(0)
        load_wup(2)
        ffn_a(0, range(0, 3))
        ffn_a(0, range(3, NT))
        ffn_b(0)
        for p in range(1, NPAIR):
            load_wup(p + 2)
            ffn_a(p, range(0, 4))
            ffn_c(p - 1)
            ffn_a(p, range(4, NT))
            ffn_b(p)
        ffn_c(NPAIR - 1)
        dump("x2", x1, [("x2", i) for i in range(4)])
        S.barrier()
        A.release(m5)
        otile = [A.alloc((8, 512), F32) for _ in range(2)]
        tmpF = [dict(sq=A.alloc((8, 512), BF16), sd=A.alloc((512,), F32), rs=A.alloc((512,), F32), tok=("tF", i)) for i in range(2)]
        out_v = d_out.rearrange("(c p) t -> p c t", p=128)
        for i in range(4):
            sl = i % 2
            rms_norm(x1[:, :, 1 + i * 512:1 + (i + 1) * 512], [("x2", i)], 8, 512, "gf", otile[sl], [("ot", sl)], tmpF[sl], D)
            out_dmas.append(dma("sp", out_v[:, :, i * 512:(i + 1) * 512], otile[sl], reads=[("ot", sl)]))
        S.emit(out_dma_ops=out_dmas)
        print("arena peak bytes/partition:", A.peak, "ops:", {e: len(S.ops[e]) for e in ENGS})
    return nc, dbg_outs


def _prep_shared(inp):
    f = np.float32
    w_in = np.asarray(inp["w_in"][0], f)
    sh = {}
    a_in = w_in[:, 0:1024]
    qc = w_in[:, 1024:1408]
    kvc = w_in[:, 1408:1664]
    rope = w_in[:, 1664:1696]
    gates = w_in[:, 1696:3744]
    z1 = np.zeros((D, 1), f)
    rope_sw = np.concatenate([rope[:, 16:32], rope[:, 0:16]], axis=1)
    sh["wkvr"] = np.ascontiguousarray(np.concatenate([kvc, z1, rope, z1, rope_sw], axis=1))
    def blocked(w, cols):
        kc = w.shape[0] // 128
        out = np.empty((len(cols), 128, kc * 128), f)
        for i, c0 in enumerate(cols):
            out[i] = w[:, c0:c0 + 128].reshape(kc, 128, 128).transpose(1, 0, 2).reshape(128, kc * 128)
        return out

    sh["wq"] = blocked(qc, [0, 128, 256])
    sh["wa"] = np.ascontiguousarray(a_in)
    sh["wg"] = np.ascontiguousarray(gates)
    w_uq = np.asarray(inp["w_uq"][0], f).reshape(QLR, NH, 96)
    zq = np.zeros((QLR, NH, 1), f)
    qrope = w_uq[:, :, 64:96]
    qrope_sw = np.concatenate([qrope[:, :, 16:32], qrope[:, :, 0:16]], axis=2)
    z64 = np.zeros((QLR, NH, 64), f)
    sh["wqh"] = np.ascontiguousarray(
        np.concatenate([w_uq[:, :, 0:64], zq, qrope, z64, zq, qrope_sw], axis=2).reshape(QLR, NH * 194))
    w_ukv = np.asarray(inp["w_ukv"][0], f).reshape(KVLR, NH, 128)
    sh["wk"] = np.ascontiguousarray(
        np.concatenate([w_ukv[:, :, 0:64], np.zeros((KVLR, NH, 64), f)], axis=2).reshape(KVLR, NH * 128))
    sh["wv"] = np.ascontiguousarray(w_ukv[:, :, 64:128].reshape(KVLR, NH * 64))
    sh["wco"] = np.ascontiguousarray(np.asarray(inp["w_conv_out"][0], f))
    sh["wmo"] = np.ascontiguousarray(np.asarray(inp["w_mla_out"][0], f))
    sh["wout"] = np.ascontiguousarray(np.asarray(inp["w_out"][0], f))
    w_up = np.asarray(inp["w_ffn_up"][0], f)
    gpart = w_up[:, 0:DFF].reshape(D, NPAIR, 128)
    upart = w_up[:, DFF:2 * DFF].reshape(D, NPAIR, 128)
    sh["wup"] = np.ascontiguousarray(np.concatenate([gpart, upart], axis=2).transpose(1, 0, 2))
    sh["wdn"] = np.ascontiguousarray(np.asarray(inp["w_ffn_down"][0], f))
    vec = np.zeros((128, NV), f)

    def put(name, arr, nch):
        vec[:, VOFF[name]:VOFF[name] + nch] = np.asarray(arr, f).reshape(nch, 128).T

    put("g1", inp["norm1_g"][0], 8)
    put("gq", inp["q_norm_g"][0], 3)
    put("gkv", inp["kv_norm_g"][0], 2)
    put("dwb", inp["conv_dw_b"][0], 4)
    put("lng", inp["conv_ln_g"][0], 4)
    put("lnb", inp["conv_ln_b"][0], 4)
    put("bco", inp["b_conv_out"][0], 8)
    put("g2", inp["norm2_g"][0], 8)
    put("gf", inp["norm_f_g"], 8)
    put("fw", inp["ffn_dw_w"][0], 132)
    put("fb", inp["ffn_dw_b"][0], 44)
    cwm = np.asarray(inp["conv_dw_w"][0], f).reshape(CK, 4, 128).transpose(1, 0, 2)
    vec[:, VOFF["cw"]:VOFF["cw"] + 4 * CK] = cwm.reshape(4 * CK, 128).T
    invf = (1.0 / (np.float32(10000.0) ** (np.arange(0, 32, 2, dtype=np.float32) / np.float32(32)))).astype(f)
    vec[65:81, VOFF["invf"]] = -invf
    vec[81:97, VOFF["invf"]] = invf
    vec[65:81, VOFF["sgn"]] = -1.0
    vec[81:97, VOFF["sgn"]] = 1.0
    sh["vec"] = vec
    sh["ident"] = np.eye(128, dtype=f)
    return sh


def _prep_core(inp, core, xT):
    b, c = divmod(core, 4)
    f = np.float32
    m = {}
    m["xf"] = xT[b]
    lo = c * OWN - 16
    xo = np.zeros((D, XO), f)
    s0, s1 = max(lo, 0), min(lo + XO, SEQ)
    xo[:, s0 - lo:s1 - lo] = xT[b + NB][:, s0:s1]
    xo3 = xo.reshape(8, 128, XO)
    m["xo"] = np.ascontiguousarray(
        np.stack([xo3[:, :, 15 + t * T:15 + (t + 1) * T].transpose(1, 0, 2).reshape(128, 8 * T) for t in range(NT)]))
    m["xh"] = np.ascontiguousarray(
        np.stack([xo3[:, :, t * T:t * T + TH].transpose(1, 0, 2).reshape(128, 8 * TH) for t in range(NT)]))
    pos = np.asarray(inp["positions"], np.int32)
    m["posf"] = np.ascontiguousarray(pos[b:b + 1, :])
    elo = c * OWN - 1
    poso = np.zeros((1, E), np.int32)
    mask = np.zeros((1, E), f)
    s0, s1 = max(elo, 0), min(elo + E, SEQ)
    poso[0, s0 - elo:s1 - elo] = pos[b, s0:s1]
    mask[0, s0 - elo:s1 - elo] = 1.0
    m["poso"] = poso
    m["mask"] = mask
    return m


def _prep_x(x):
    xTp = [np.ascontiguousarray(x[b].T) for b in range(NB)]
    return [np.ascontiguousarray(xt.reshape(8, 128, SEQ // 512, 512).transpose(2, 1, 0, 3).reshape(SEQ // 512, 128, 8 * 512))
            for xt in xTp] + xTp


_NC_CACHE = {}


def kernel(**inputs):
    x = np.asarray(inputs["x"], np.float32)
    xT = _prep_x(x)
    sh = _prep_shared(inputs)
    in_maps = []
    for core in range(8):
        m = dict(sh)
        m.update(_prep_core(inputs, core, xT))
        in_maps.append(m)
    if "nc" not in _NC_CACHE:
        _NC_CACHE["nc"] = build_nc()[0]
    nc = _NC_CACHE["nc"]
    res = run_bass_kernel_spmd(nc, in_maps, core_ids=list(range(8)))
    out = np.empty((NB, SEQ, D), np.float32)
    for core in range(8):
        b, c = divmod(core, 4)
        out[b, c * OWN:(c + 1) * OWN, :] = np.asarray(res.results[core]["out"]).T
    return out
```
